# Optimizing a Trainium2 kernel written in Bass

```python
import math
import jax
import jax.numpy as jnp
from jax import lax
import numpy as np

D_MODEL = 1024
BATCH = 2
SEQ = 8192
DEPTH = 4

HEAD_DIM = 64
GROUP_HEADS = 4
GROUP_WIDTH = GROUP_HEADS * HEAD_DIM
D_MIX = 4 * GROUP_WIDTH
GLA_RANK = 16
GLA_TAU = 16.0
GLA_CHUNK = 64
GRID_W = 64
NA_ROWS_MAX = 8
NA_COLS = 16
LRU_CONV = 4
LRU_CONV_LEFT = 2
LRU_C = 8.0
DIL_PAIRS = ((128, 1), (512, 4), (2048, 16))
ROPE_THETA = 10000.0
D_FF = -(-8 * D_MODEL // (3 * 256)) * 256
EPS = 1e-6
SPLITS = (GROUP_WIDTH,) * 4 + (2 * GLA_RANK,) + (GROUP_WIDTH,) * 3 + (GROUP_WIDTH,) * 2 + (GROUP_WIDTH,) * 3
D_IN = sum(SPLITS)

kernel_name = "hybrid_parallel_head_group_encoder"


def rmsnorm(x, g):
    xf = x.astype(jnp.float32)
    y = xf * lax.rsqrt(jnp.mean(xf * xf, axis=-1, keepdims=True) + EPS)
    return (y * g.astype(jnp.float32)).astype(x.dtype)


def to_heads(t):
    B, L, _ = t.shape
    return t.reshape(B, L, -1, HEAD_DIM).transpose(0, 2, 1, 3)


def from_heads(t):
    B, H, L, dh = t.shape
    return t.transpose(0, 2, 1, 3).reshape(B, L, H * dh)


def rope(t, cos, sin):
    t1, t2 = jnp.split(t, 2, axis=-1)
    c = cos.astype(t.dtype)
    s = sin.astype(t.dtype)
    return jnp.concatenate([t1 * c - t2 * s, t2 * c + t1 * s], axis=-1)


def gla_chunked(q, k, v, log_a):
    B, H, L, dk = q.shape
    dv = v.shape[-1]
    C = GLA_CHUNK
    n = L // C
    q, k, v, log_a = [t.reshape(B, H, n, C, -1) for t in (q, k, v, log_a)]
    b = jnp.cumsum(log_a, axis=3)
    b_last = b[:, :, :, C - 1:C, :]
    b_mid = b[:, :, :, C // 2 - 1:C // 2, :]
    att = jnp.einsum('bhnck,bhnsk->bhncs', q * jnp.exp(b - b_mid), k * jnp.exp(b_mid - b))
    tri = np.tril(np.ones((C, C), dtype=bool))
    att = jnp.where(tri, att, 0.0)
    o_intra = jnp.einsum('bhncs,bhnsv->bhncv', att, v)
    chunk_kv = jnp.einsum('bhnck,bhncv->bhnkv', k * jnp.exp(b_last - b), v)
    decay = jnp.exp(b_last[:, :, :, 0, :])

    def step(S, inp):
        kv_c, d_c = inp
        return S * d_c[..., None] + kv_c, S

    S0 = jnp.zeros((B, H, dk, dv), jnp.float32)
    _, S_prev = lax.scan(step, S0, (jnp.moveaxis(chunk_kv, 2, 0), jnp.moveaxis(decay, 2, 0)))
    S_prev = jnp.moveaxis(S_prev, 0, 2)
    o_inter = jnp.einsum('bhnck,bhnkv->bhncv', q * jnp.exp(b), S_prev)
    return (o_intra + o_inter).reshape(B, H, L, dv)


def gla_mixer(q, k, v, g, z, w_gate, b_gate, norm_g):
    B, L, _ = q.shape
    f32 = jnp.float32
    zl = z.astype(f32).reshape(B, L, 2, GLA_RANK)
    logit = jnp.einsum('bler,erc->eblc', zl, w_gate.astype(f32)) + b_gate.astype(f32)[:, None, None, :]
    log_a = jax.nn.log_sigmoid(logit) / GLA_TAU
    qh = to_heads(q).astype(f32) * (HEAD_DIM ** -0.5)
    kh = to_heads(k).astype(f32)
    vh = to_heads(v).astype(f32)
    flip = lambda t: jnp.flip(t, axis=2)
    o_f = gla_chunked(qh, kh, vh, to_heads(log_a[0]))
    o_b = flip(gla_chunked(flip(qh), flip(kh), flip(vh), flip(to_heads(log_a[1]))))
    o = o_f + o_b
    o = o * lax.rsqrt(jnp.mean(o * o, axis=-1, keepdims=True) + EPS)
    o = o * norm_g.astype(f32).reshape(GROUP_HEADS, 1, HEAD_DIM)
    return (from_heads(o) * jax.nn.silu(g.astype(f32))).astype(q.dtype)


def neighbourhood_attention(q, k, v, rpb):
    B, H, L, dh = q.shape
    rows = L // GRID_W
    kr = min(NA_ROWS_MAX, rows)
    grid = lambda t: t.reshape(B, H, rows, GRID_W, dh)
    qg, kg, vg = grid(q), grid(k), grid(v)
    r = np.arange(rows)
    row_idx = np.clip(r - kr // 2, 0, rows - kr)[:, None] + np.arange(kr)[None, :]
    k_rows = kg[:, :, row_idx]
    v_rows = vg[:, :, row_idx]
    c = np.arange(GRID_W)
    col_start = np.clip(c - NA_COLS // 2, 0, GRID_W - NA_COLS)
    col_ok = (c[None, :] >= col_start[:, None]) & (c[None, :] < col_start[:, None] + NA_COLS)
    dr = row_idx - r[:, None]
    dc = np.clip(c[None, :] - c[:, None], -(NA_COLS - 1), NA_COLS - 1)
    bias = rpb[:, (dr + NA_ROWS_MAX - 1)[:, None, :, None], (dc + NA_COLS - 1)[None, :, None, :]]
    s = jnp.einsum('bhrqd,bhrikd->bhrqik', qg, k_rows).astype(jnp.float32) * (dh ** -0.5)
    s = s + bias.astype(jnp.float32)
    s = jnp.where(col_ok[:, None, :], s, -jnp.inf)
    p = jax.nn.softmax(s.reshape(B, H, rows, GRID_W, kr * GRID_W), axis=-1).reshape(s.shape)
    o = jnp.einsum('bhrqik,bhrikd->bhrqd', p.astype(v.dtype), v_rows)
    return o.reshape(B, H, L, dh)


def linear_scan(a, u):
    def combine(left, right):
        a_l, u_l = left
        a_r, u_r = right
        return a_l * a_r, a_r * u_l + u_r
    return lax.associative_scan(combine, (a, u), axis=1)[1]


def rglru_mixer(xb, gate, conv_w, conv_b, w_a, b_a, w_x, b_x, lam):
    B, L, C = xb.shape
    f32 = jnp.float32
    xp = jnp.pad(xb.astype(f32), ((0, 0), (LRU_CONV_LEFT, LRU_CONV - 1 - LRU_CONV_LEFT), (0, 0)))
    xc = conv_b.astype(f32)
    for j in range(LRU_CONV):
        xc = xc + xp[:, j:j + L, :] * conv_w[j].astype(f32)
    xh = xc.reshape(B, L, GROUP_HEADS, HEAD_DIM)
    r = jax.nn.sigmoid(jnp.einsum('blhi,ehij->eblhj', xh, w_a.astype(f32)).reshape(2, B, L, C)
                       + b_a.astype(f32)[:, None, None, :])
    i = jax.nn.sigmoid(jnp.einsum('blhi,ehij->eblhj', xh, w_x.astype(f32)).reshape(2, B, L, C)
                       + b_x.astype(f32)[:, None, None, :])
    log_a = -LRU_C * r * jax.nn.softplus(-lam.astype(f32))[:, None, None, :]
    a = jnp.exp(log_a)
    u = jnp.sqrt(-jnp.expm1(2.0 * log_a)) * (i * xc[None])
    flip = lambda t: jnp.flip(t, axis=1)
    h = linear_scan(a[0], u[0]) + flip(linear_scan(flip(a[1]), flip(u[1])))
    return (h * jax.nn.gelu(gate.astype(f32))).astype(xb.dtype)


def band_attention(q, k, v, radius):
    lead = q.shape[:-2]
    n, dh = q.shape[-2], q.shape[-1]
    Q = radius
    nb = -(-n // Q)
    n_pad = nb * Q
    nl = len(lead)
    qb = jnp.pad(q, ((0, 0),) * nl + ((0, n_pad - n), (0, 0))).reshape(lead + (nb, Q, dh))
    padkv = lambda t: jnp.pad(t, ((0, 0),) * nl + ((Q, n_pad - n + Q), (0, 0))).reshape(lead + (nb + 2, Q, dh))
    kp, vp = padkv(k), padkv(v)
    band = lambda t: jnp.concatenate([t[..., 0:nb, :, :], t[..., 1:nb + 1, :, :], t[..., 2:nb + 2, :, :]], axis=-2)
    kb, vb = band(kp), band(vp)
    blk = np.arange(nb)[:, None, None]
    qpos = blk * Q + np.arange(Q)[None, :, None]
    kpos = blk * Q + np.arange(3 * Q)[None, None, :] - Q
    valid = (np.abs(kpos - qpos) <= radius) & (kpos >= 0) & (kpos < n)
    s = jnp.einsum('...bqd,...bkd->...bqk', qb, kb).astype(jnp.float32) * (dh ** -0.5)
    s = jnp.where(valid, s, -jnp.inf)
    m = jnp.max(s, axis=-1, keepdims=True)
    e = jnp.exp(s - m)
    den = jnp.sum(e, axis=-1, keepdims=True)
    o = jnp.einsum('...bqk,...bkd->...bqd', (e / den).astype(v.dtype), vb)
    lse = (m + jnp.log(den))[..., 0]
    o = o.reshape(lead + (n_pad, dh))[..., :n, :]
    lse = lse.reshape(lead + (n_pad,))[..., :n]
    return o, lse


def dilated_attention(q, k, v):
    B, H, L, dh = q.shape
    outs, lses = [], []
    for window, dil in DIL_PAIRS:
        radius = window // (2 * dil)
        n = L // dil
        sub = lambda t: t.reshape(B, H, n, dil, dh).transpose(0, 1, 3, 2, 4)
        o, lse = band_attention(sub(q), sub(k), sub(v), radius)
        outs.append(o.transpose(0, 1, 3, 2, 4).reshape(B, H, L, dh).astype(jnp.float32))
        lses.append(lse.transpose(0, 1, 3, 2).reshape(B, H, L))
    wts = jax.nn.softmax(jnp.stack(lses, axis=0), axis=0)
    return jnp.einsum('gbhl,gbhld->bhld', wts, jnp.stack(outs, axis=0)).astype(q.dtype)


def setup_inputs(seed: int = 0) -> dict:
    key = jax.random.key(seed)
    ks = jax.random.split(key, 24)
    f32 = jnp.float32
    nrm = lambda k, shape, scale: scale * jax.random.normal(k, shape, f32)
    gain = lambda k, d: 1.0 + 0.02 * jax.random.normal(k, (DEPTH, d), f32)
    u = jax.random.uniform(ks[14], (DEPTH, 2, GROUP_WIDTH), f32, 0.9, 0.999)
    return {
        "x": jax.random.normal(ks[0], (BATCH, SEQ, D_MODEL), f32),
        "mix_norm_pre": gain(ks[1], D_MODEL),
        "mix_norm_post": gain(ks[2], D_MODEL),
        "w_in": nrm(ks[3], (DEPTH, D_MODEL, D_IN), D_MODEL ** -0.5),
        "gla_w_gate": nrm(ks[4], (DEPTH, 2, GLA_RANK, GROUP_WIDTH), GLA_RANK ** -0.5),
        "gla_b_gate": nrm(ks[5], (DEPTH, 2, GROUP_WIDTH), 0.1),
        "gla_norm": gain(ks[6], GROUP_WIDTH),
        "na_rpb": nrm(ks[7], (DEPTH, GROUP_HEADS, 2 * NA_ROWS_MAX - 1, 2 * NA_COLS - 1), 0.1),
        "lru_conv_w": nrm(ks[8], (DEPTH, LRU_CONV, GROUP_WIDTH), LRU_CONV ** -0.5),
        "lru_conv_b": nrm(ks[9], (DEPTH, GROUP_WIDTH), 0.02),
        "lru_w_a": nrm(ks[10], (DEPTH, 2, GROUP_HEADS, HEAD_DIM, HEAD_DIM), HEAD_DIM ** -0.5),
        "lru_b_a": nrm(ks[11], (DEPTH, 2, GROUP_WIDTH), 0.1),
        "lru_w_x": nrm(ks[12], (DEPTH, 2, GROUP_HEADS, HEAD_DIM, HEAD_DIM), HEAD_DIM ** -0.5),
        "lru_b_x": nrm(ks[13], (DEPTH, 2, GROUP_WIDTH), 0.1),
        "lru_lambda": jnp.log(u) - jnp.log1p(-u),
        "w_out": nrm(ks[15], (DEPTH, D_MIX, D_MODEL), D_MIX ** -0.5),
        "ffn_norm_pre": gain(ks[16], D_MODEL),
        "ffn_norm_post": gain(ks[17], D_MODEL),
        "ffn_w_in": nrm(ks[18], (DEPTH, D_MODEL, 2 * D_FF), D_MODEL ** -0.5),
        "ffn_w_out": nrm(ks[19], (DEPTH, D_FF, D_MODEL), D_FF ** -0.5),
    }


def reference(x, mix_norm_pre, mix_norm_post, w_in, gla_w_gate, gla_b_gate, gla_norm, na_rpb,
              lru_conv_w, lru_conv_b, lru_w_a, lru_b_a, lru_w_x, lru_b_x, lru_lambda, w_out,
              ffn_norm_pre, ffn_norm_post, ffn_w_in, ffn_w_out):
    B, L, _ = x.shape
    pos = jnp.arange(L, dtype=jnp.float32)
    inv_freq = ROPE_THETA ** (-jnp.arange(0, HEAD_DIM, 2, dtype=jnp.float32) / HEAD_DIM)
    ang = pos[:, None] * inv_freq[None, :]
    cos, sin = jnp.cos(ang), jnp.sin(ang)
    split_at = [int(s) for s in np.cumsum(SPLITS)[:-1]]
    for l in range(DEPTH):
        h = rmsnorm(x, mix_norm_pre[l])
        p = h @ w_in[l]
        qa, ka, va, ga, za, qb, kb, vb, xc, gc, qd, kd, vd = jnp.split(p, split_at, axis=-1)
        ya = gla_mixer(qa, ka, va, ga, za, gla_w_gate[l], gla_b_gate[l], gla_norm[l])
        yb = from_heads(neighbourhood_attention(to_heads(qb), to_heads(kb), to_heads(vb), na_rpb[l]))
        yc = rglru_mixer(xc, gc, lru_conv_w[l], lru_conv_b[l], lru_w_a[l], lru_b_a[l],
                         lru_w_x[l], lru_b_x[l], lru_lambda[l])
        yd = from_heads(dilated_attention(rope(to_heads(qd), cos, sin), rope(to_heads(kd), cos, sin),
                                          to_heads(vd)))
        y = jnp.concatenate([ya, yb.astype(x.dtype), yc, yd.astype(x.dtype)], axis=-1) @ w_out[l]
        x = x + rmsnorm(y, mix_norm_post[l])
        h = rmsnorm(x, ffn_norm_pre[l])
        gate, up = jnp.split(h @ ffn_w_in[l], 2, axis=-1)
        f = (jax.nn.silu(gate) * up) @ ffn_w_out[l]
        x = x + rmsnorm(f, ffn_norm_post[l])
    return x
```

```python
import contextlib
import numpy as np
import concourse.bass as bass
import concourse.mybir as mybir
from concourse.bass_utils import run_bass_kernel_spmd

F32 = mybir.dt.float32
BF16 = mybir.dt.bfloat16
AF = mybir.ActivationFunctionType
ALU = mybir.AluOpType

D_MODEL = 1024
SEQ = 8192
BATCH = 2
DEPTH = 4
D_IN = 3104
D_FF = 2816
EPS = 1e-6
NCORES = 8
TPC = 2048
SBUF_BASE = 16512
SBUF_LIMIT = 226000

ENGS = ("sp", "act", "pe", "dve", "pool")


class Buf:
    __slots__ = ("name", "last_w", "w_dma", "readers")

    def __init__(self, name):
        self.name = name
        self.last_w = None
        self.w_dma = False
        self.readers = {}


class Ten:
    def __init__(self, h, b):
        self.h = h
        self.b = b

    def __getitem__(self, idx):
        return self.h[idx]


class Prog:
    def __init__(self, nc):
        self.nc = nc
        self.stack = contextlib.ExitStack()
        self.ops = {e: [] for e in ENGS}
        self.waited = {e: {} for e in ENGS}
        self.sems = {}
        self.count = {}
        self.dma_sems = set()
        self.sb_off = SBUF_BASE
        self.ps_bank = 0
        self.psall = None
        self.phase = 0

    def sem(self, name):
        if name not in self.sems:
            self.sems[name] = self.stack.enter_context(self.nc.semaphore(name))
            self.count[name] = 0
        return self.sems[name]

    def sbuf(self, name, shape, dtype):
        nbytes = int(np.prod(shape[1:])) * (4 if dtype == F32 else 2)
        off = (self.sb_off + 63) // 64 * 64
        uname = f"p{self.phase}_{name}"
        h = self.nc.alloc_sbuf_tensor_at(uname, list(shape), dtype, offset=off)
        self.sb_off = off + nbytes
        assert self.sb_off <= SBUF_LIMIT, (uname, self.sb_off)
        return Ten(h, Buf(uname))

    def psum(self, name, shape, dtype=F32):
        if self.psall is None:
            self.psall = self.nc.alloc_psum_tensor("psall", [128, 4096], F32)
        e32 = int(np.prod(shape[1:])) * (4 if dtype == F32 else 2) // 4
        nb = (e32 + 511) // 512
        st = self.ps_bank * 512
        self.ps_bank += nb
        assert self.ps_bank <= 8, name
        v = self.psall[0:shape[0], st:st + e32]
        if dtype != F32:
            v = v.bitcast(dtype)
        if len(shape) == 3:
            v = v.rearrange("p (a b) -> p a b", b=shape[2])
        return Ten(v, Buf(f"p{self.phase}_{name}"))

    def barrier(self):
        for eng in ENGS:
            waits = []
            for sname, cnt in sorted(self.count.items()):
                if cnt == 0 or sname == "c_" + eng:
                    continue
                if self.waited[eng].get(sname, 0) >= cnt:
                    continue
                waits.append((sname, cnt))
                self.waited[eng][sname] = cnt
            if waits:
                self.ops[eng].append((waits, None, None, 0, True))

    def phase_reset(self, sb_mark):
        self.barrier()
        self.sb_off = sb_mark
        self.ps_bank = 0
        self.phase += 1

    def dram(self, name, shape, dtype, kind):
        h = self.nc.dram_tensor(name, list(shape), dtype, kind=kind).ap()
        return Ten(h, Buf(name))

    def op(self, eng, fn, reads=(), writes=(), dma_sem=None):
        waits = {}
        own = "c_" + eng

        def need(tok, is_dma):
            if tok is None:
                return
            s, v = tok
            if eng == "pe" and s == own:
                return
            if s in self.dma_sems:
                v = max(v, self.count[s])
            if self.waited[eng].get(s, 0) >= v:
                return
            if waits.get(s, 0) < v:
                waits[s] = v

        for b in reads:
            need(b.last_w, b.w_dma)
        for b in writes:
            grouped = (dma_sem is not None and b.w_dma and not b.readers
                       and b.last_w is not None and b.last_w[0] == dma_sem)
            if not grouped:
                need(b.last_w, b.w_dma)
            for s, v in b.readers.items():
                need((s, v), False)
        if dma_sem is not None:
            self.sem(dma_sem)
            self.dma_sems.add(dma_sem)
            sname, inc = dma_sem, 16
        else:
            self.sem(own)
            sname, inc = own, 1
        self.count[sname] += inc
        tok = (sname, self.count[sname])
        for s, v in waits.items():
            self.sem(s)
            self.waited[eng][s] = v
        for b in reads:
            if b.readers.get(tok[0], 0) < tok[1]:
                b.readers[tok[0]] = tok[1]
        for b in writes:
            b.last_w = tok
            b.w_dma = dma_sem is not None
            b.readers = {}
        self.ops[eng].append((sorted(waits.items()), fn, sname, inc, dma_sem is not None))
        return tok

    def finish(self):
        waits = [(s, self.count[s]) for s in sorted(self.dma_sems)]
        self.ops["sp"].append((waits, None, None, 0, True))

    def emit(self):
        nc = self.nc
        sems = self.sems

        def replay(e, name):
            for waits, fn, sname, inc, is_dma in self.ops[name]:
                if fn is None:
                    for s, v in waits:
                        e.wait_ge(sems[s], v)
                    continue
                if is_dma or not waits:
                    for s, v in waits:
                        e.wait_ge(sems[s], v)
                    ins = fn(e)
                else:
                    for s, v in waits[:-1]:
                        e.wait_ge(sems[s], v)
                    ins = fn(e)
                    ins._wait_ge(sems[waits[-1][0]], waits[-1][1])
                ins.then_inc(sems[sname], inc)

        with nc.Block() as block:
            @block.sync
            def _(e):
                replay(e, "sp")

            @block.scalar
            def _(e):
                replay(e, "act")

            @block.tensor
            def _(e):
                replay(e, "pe")

            @block.vector
            def _(e):
                replay(e, "dve")

            @block.gpsimd
            def _(e):
                replay(e, "pool")
        self.stack.close()


def build_dense(has_front, has_back):
    nc = bass.Bass("TRN2", target_bir_lowering=False)
    P = Prog(nc)
    T = TPC
    HT = 1024
    NBLK = HT // 512

    xT_d = P.dram("xT", [D_MODEL, T], F32, "ExternalInput")
    xv = xT_d.h.rearrange("(c p) t -> p c t", p=128)
    if has_front:
        yT_d = P.dram("yT", [D_MODEL, T], F32, "ExternalInput")
        yv = yT_d.h.rearrange("(c p) t -> p c t", p=128)
        wo_d = P.dram("w_out", [D_MODEL, D_MODEL], F32, "ExternalInput")
        wov = wo_d.h.rearrange("(c p) n -> p c n", p=128)
        wfi_d = P.dram("ffn_w_in", [D_MODEL, 2 * D_FF], F32, "ExternalInput")
        wfiv = wfi_d.h.rearrange("(c p) n -> p c n", p=128)
        wfo_d = P.dram("ffn_w_out", [D_FF, D_MODEL], F32, "ExternalInput")
        wfov = wfo_d.h.rearrange("(c p) n -> p c n", p=128)
        g_d = P.dram("g_front", [128, 3, 8], F32, "ExternalInput")
    if has_back:
        win_d = P.dram("w_in", [D_MODEL, D_IN], F32, "ExternalInput")
        winv = win_d.h.rearrange("(c p) n -> p c n", p=128)
        gb_d = P.dram("g_back", [128, 8], F32, "ExternalInput")
        pT_d = P.dram("pT", [D_IN, T], F32, "ExternalOutput")
    if has_front:
        xo_d = P.dram("xoT", [D_MODEL, T], F32, "ExternalOutput")
        xov = xo_d.h.rearrange("(c p) t -> p c t", p=128)

    x = P.sbuf("x", [128, 8, HT], F32)
    hb = P.sbuf("hb", [128, 8, HT], BF16)
    sq = P.sbuf("sq", [128, 8, 512], BF16)
    rstd = P.sbuf("rstd", [128, 512], F32)
    ones = P.sbuf("ones", [128, 128], BF16)
    epst = P.sbuf("epst", [128, 1], F32)
    slabs = [P.sbuf(f"slab{i}", [128, 8, 256], BF16) for i in range(4)]
    if has_front:
        z = P.sbuf("z", [128, 8, HT], F32)
        act = P.sbuf("act", [128, 22, HT], BF16)
        wfo_s = [P.sbuf(f"wfo{i}", [128, 22, 128], BF16) for i in range(2)]
        sil = [P.sbuf(f"sil{i}", [128, 512], F32) for i in range(2)]
        gf = P.sbuf("gf", [128, 3, 8], F32)
    if has_back:
        gbk = P.sbuf("gbk", [128, 8], F32)
        stage = [P.sbuf(f"stage{i}", [128, 512], F32) for i in range(4)]
    pss = [P.psum(f"ps{i}", [128, 512], F32) for i in range(6)]
    psn = [P.psum(f"psn{i}", [128, 512], F32) for i in range(2)]
    st = {"ps": 0, "psn": 0, "slab": 0, "wfo": 0, "sil": 0, "stage": 0, "ev": 0}

    def rr(key, lst):
        i = st[key]
        st[key] = (i + 1) % len(lst)
        return lst[i]

    P.op("dve", lambda e: e.memset(ones[:], 1.0 / D_MODEL), writes=[ones.b])
    P.op("dve", lambda e: e.memset(epst[:], EPS), writes=[epst.b])
    if has_front:
        P.op("sp", lambda e: e.dma_start(out=gf[:], in_=g_d[:, :, :]), writes=[gf.b], reads=[g_d.b], dma_sem="d_gf")
    if has_back:
        P.op("sp", lambda e: e.dma_start(out=gbk[:], in_=gb_d[:, :]), writes=[gbk.b], reads=[gb_d.b], dma_sem="d_gbk")

    def load_slab(view, c0, ncols, src_b):
        s = rr("slab", slabs)
        P.op("pool", lambda e: e.dma_start(out=s[:, :, 0:ncols], in_=view[:, :, c0:c0 + ncols]),
             reads=[src_b], writes=[s.b], dma_sem="d_" + s.b.name)
        return s

    def evac(dst_ap, dst_b, ps):
        st["ev"] ^= 1
        if st["ev"]:
            P.op("act", lambda e: e.activation(dst_ap, ps[:], AF.Copy), reads=[ps.b], writes=[dst_b])
        else:
            P.op("dve", lambda e: e.tensor_copy(dst_ap, ps[:]), reads=[ps.b], writes=[dst_b])

    def rms_rstd(src, blk):
        tsl = slice(blk * 512, (blk + 1) * 512)
        P.op("act", lambda e: e.activation(sq[:], src[:, :, tsl], AF.Square), reads=[src.b], writes=[sq.b])
        ps = rr("psn", psn)
        for ci in range(8):
            P.op("pe", lambda e, ci=ci: e.matmul(ps[:], ones[:], sq[:, ci, :], start=(ci == 0), stop=(ci == 7)),
                 reads=[ones.b, sq.b], writes=[ps.b])
        P.op("act", lambda e: e.activation(rstd[:], ps[:], AF.Sqrt, bias=epst[:], scale=1.0),
             reads=[ps.b, epst.b], writes=[rstd.b])
        P.op("dve", lambda e: e.reciprocal(rstd[:], rstd[:]), reads=[rstd.b], writes=[rstd.b])

    def pre_norm(g_ap_fn, g_b, blk):
        tsl = slice(blk * 512, (blk + 1) * 512)
        rms_rstd(x, blk)
        for ci in range(8):
            P.op("dve", lambda e, ci=ci: e.scalar_tensor_tensor(hb[:, ci, tsl], x[:, ci, tsl], g_ap_fn(ci), rstd[:],
                                                                ALU.mult, ALU.mult),
                 reads=[x.b, rstd.b, g_b], writes=[hb.b])

    def post_norm_add(g_ap_fn, g_b, blk):
        tsl = slice(blk * 512, (blk + 1) * 512)
        rms_rstd(z, blk)
        for ci in range(8):
            P.op("dve", lambda e, ci=ci: e.scalar_tensor_tensor(z[:, ci, tsl], z[:, ci, tsl], g_ap_fn(ci), rstd[:],
                                                                ALU.mult, ALU.mult),
                 reads=[z.b, rstd.b, g_b], writes=[z.b])
        P.op("dve", lambda e: e.tensor_tensor(x[:, :, tsl], x[:, :, tsl], z[:, :, tsl], ALU.add),
             reads=[x.b, z.b], writes=[x.b])

    for half in range(2):
        t0 = half * HT
        P.op("sp", lambda e, t0=t0: e.dma_start(out=x[:], in_=xv[:, :, t0:t0 + HT]),
             reads=[xT_d.b], writes=[x.b], dma_sem="d_x")
        if has_front:
            P.op("pool", lambda e, t0=t0: e.dma_start(out=act[:, 0:8, :], in_=yv[:, :, t0:t0 + HT]),
                 reads=[yT_d.b], writes=[act.b], dma_sem="d_act")
            for sl in range(4):
                s = load_slab(wov, sl * 256, 256, wo_d.b)
                for ccl in range(2):
                    cc = sl * 2 + ccl
                    for blk in range(NBLK):
                        tsl = slice(blk * 512, (blk + 1) * 512)
                        ps = rr("ps", pss)
                        for ci in range(8):
                            P.op("pe", lambda e, ci=ci, s=s, ccl=ccl, tsl=tsl, ps=ps: e.matmul(
                                ps[:], s[:, ci, ccl * 128:(ccl + 1) * 128], act[:, ci, tsl],
                                start=(ci == 0), stop=(ci == 7)), reads=[s.b, act.b], writes=[ps.b])
                        evac(z[:, cc, tsl], z.b, ps)
            for blk in range(NBLK):
                post_norm_add(lambda ci: gf[:, 0, ci:ci + 1], gf.b, blk)
            for blk in range(NBLK):
                pre_norm(lambda ci: gf[:, 1, ci:ci + 1], gf.b, blk)
            for sl in range(11):
                sg = load_slab(wfiv, sl * 256, 256, wfi_d.b)
                su = load_slab(wfiv, D_FF + sl * 256, 256, wfi_d.b)
                for fcl in range(2):
                    fc = sl * 2 + fcl
                    for blk in range(NBLK):
                        tsl = slice(blk * 512, (blk + 1) * 512)
                        pg = rr("ps", pss)
                        for ci in range(8):
                            P.op("pe", lambda e, ci=ci, sg=sg, fcl=fcl, tsl=tsl, pg=pg: e.matmul(
                                pg[:], sg[:, ci, fcl * 128:(fcl + 1) * 128], hb[:, ci, tsl],
                                start=(ci == 0), stop=(ci == 7)), reads=[sg.b, hb.b], writes=[pg.b])
                        pu = rr("ps", pss)
                        for ci in range(8):
                            P.op("pe", lambda e, ci=ci, su=su, fcl=fcl, tsl=tsl, pu=pu: e.matmul(
                                pu[:], su[:, ci, fcl * 128:(fcl + 1) * 128], hb[:, ci, tsl],
                                start=(ci == 0), stop=(ci == 7)), reads=[su.b, hb.b], writes=[pu.b])
                        sb = rr("sil", sil)
                        P.op("act", lambda e, sb=sb, pg=pg: e.activation(sb[:], pg[:], AF.Silu),
                             reads=[pg.b], writes=[sb.b])
                        P.op("dve", lambda e, sb=sb, pu=pu, fc=fc, tsl=tsl: e.tensor_tensor(
                            act[:, fc, tsl], sb[:], pu[:], ALU.mult), reads=[sb.b, pu.b], writes=[act.b])
            for cc in range(8):
                w = rr("wfo", wfo_s)
                P.op("pool", lambda e, w=w, cc=cc: e.dma_start(out=w[:], in_=wfov[:, :, cc * 128:(cc + 1) * 128]),
                     reads=[wfo_d.b], writes=[w.b], dma_sem="d_" + w.b.name)
                for blk in range(NBLK):
                    tsl = slice(blk * 512, (blk + 1) * 512)
                    ps = rr("ps", pss)
                    for fc in range(22):
                        P.op("pe", lambda e, fc=fc, w=w, tsl=tsl, ps=ps: e.matmul(
                            ps[:], w[:, fc, :], act[:, fc, tsl], start=(fc == 0), stop=(fc == 21)),
                            reads=[w.b, act.b], writes=[ps.b])
                    evac(z[:, cc, tsl], z.b, ps)
            for blk in range(NBLK):
                post_norm_add(lambda ci: gf[:, 2, ci:ci + 1], gf.b, blk)
        if has_back:
            for blk in range(NBLK):
                pre_norm(lambda ci: gbk[:, ci:ci + 1], gbk.b, blk)
            for sl in range(13):
                ncols = 256 if sl < 12 else 32
                s = load_slab(winv, sl * 256, ncols, win_d.b)
                for ccl in range((ncols + 127) // 128):
                    m = min(128, ncols - ccl * 128)
                    c0 = sl * 256 + ccl * 128
                    for blk in range(NBLK):
                        tsl = slice(blk * 512, (blk + 1) * 512)
                        ps = rr("ps", pss)
                        for ci in range(8):
                            P.op("pe", lambda e, ci=ci, s=s, ccl=ccl, m=m, tsl=tsl, ps=ps: e.matmul(
                                ps[0:m, :], s[:, ci, ccl * 128:ccl * 128 + m], hb[:, ci, tsl],
                                start=(ci == 0), stop=(ci == 7)), reads=[s.b, hb.b], writes=[ps.b])
                        sg_ = rr("stage", stage)
                        st["ev"] ^= 1
                        if st["ev"]:
                            P.op("act", lambda e, sg_=sg_, ps=ps, m=m: e.activation(sg_[0:m, :], ps[0:m, :], AF.Copy),
                                 reads=[ps.b], writes=[sg_.b])
                        else:
                            P.op("dve", lambda e, sg_=sg_, ps=ps, m=m: e.tensor_copy(sg_[0:m, :], ps[0:m, :]),
                                 reads=[ps.b], writes=[sg_.b])
                        P.op("sp", lambda e, sg_=sg_, m=m, c0=c0, t0=t0, blk=blk: e.dma_start(
                            out=pT_d[c0:c0 + m, t0 + blk * 512:t0 + (blk + 1) * 512], in_=sg_[0:m, :]),
                            reads=[sg_.b], writes=[pT_d.b], dma_sem="o_" + sg_.b.name)
        if has_front:
            P.op("sp", lambda e, t0=t0: e.dma_start(out=xov[:, :, t0:t0 + HT], in_=x[:]),
                 reads=[x.b], writes=[xo_d.b], dma_sem="o_x")
    P.finish()
    P.emit()
    return nc


_DENSE_CACHE = {}


def run_dense(has_front, has_back, in_maps):
    key = (has_front, has_back)
    nc = build_dense(has_front, has_back)
    res = run_bass_kernel_spmd(nc, in_maps, core_ids=list(range(NCORES)))
    return res.results


def gvec(g):
    return np.ascontiguousarray(g.reshape(8, 128).T)


def _consts(P, npart=64):
    one_t = P.sbuf("one_t", [128, 1], F32)
    eps_t = P.sbuf("eps_t", [128, 1], F32)
    P.op("dve", lambda e: e.memset(one_t[:], 1.0), writes=[one_t.b])
    P.op("dve", lambda e: e.memset(eps_t[:], EPS), writes=[eps_t.b])
    return one_t, eps_t


def emit_lru(P, L, d):
    TB = 1024
    NB = L // TB
    xp_d, gate_d, cw_d, wax_d, prm_d, y_d = d["xpad"], d["gate"], d["cw"], d["wax"], d["prm"], d["y"]

    one_t, eps_t = _consts(P)
    xp = P.sbuf("xp", [64, L + 3], F32)
    xc = P.sbuf("xc", [64, L], F32)
    xcb = P.sbuf("xcb", [64, L], BF16)
    hf = P.sbuf("hf", [64, L], F32)
    cw = P.sbuf("cw_s", [64, 4], F32)
    wax = P.sbuf("wax_s", [64, 256], F32)
    waxb = P.sbuf("waxb", [64, 256], BF16)
    prm = P.sbuf("prm_s", [64, 7], F32)
    sp_ = P.sbuf("sp_s", [64, 2], F32)
    s8 = P.sbuf("s8", [64, 2], F32)
    s16 = P.sbuf("s16", [64, 2], F32)
    carry = P.sbuf("carry", [64, 1], F32)
    r_ = P.sbuf("r", [64, TB], F32)
    i_ = P.sbuf("i", [64, TB], F32)
    a_ = P.sbuf("a", [64, TB], F32)
    a2 = P.sbuf("a2", [64, TB], F32)
    u_ = P.sbuf("u", [64, TB], F32)
    hbk = P.sbuf("hbk", [64, TB], F32)
    gts = [P.sbuf(f"gt{i}", [64, TB], F32) for i in range(2)]
    gl = P.sbuf("gl", [64, TB], F32)
    yb = [P.sbuf(f"yb{i}", [64, TB], F32) for i in range(2)]
    pss = [P.psum(f"ps{i}", [64, 512], F32) for i in range(4)]
    st = {"ps": 0}

    def nps():
        i = st["ps"]
        st["ps"] = (i + 1) % 4
        return pss[i]

    for t, d in ((xp, xp_d), (cw, cw_d), (wax, wax_d), (prm, prm_d)):
        P.op("sp", lambda e, t=t, d=d: e.dma_start(out=t[:], in_=d[:, :]), reads=[d.b], writes=[t.b],
             dma_sem="d_" + t.b.name)
    P.op("dve", lambda e: e.tensor_copy(waxb[:], wax[:]), reads=[wax.b], writes=[waxb.b])
    P.op("act", lambda e: e.activation(sp_[:], prm[:, 5:7], AF.Exp, scale=-1.0), reads=[prm.b], writes=[sp_.b])
    P.op("act", lambda e: e.activation(sp_[:], sp_[:], AF.Ln, bias=one_t[0:64, :]), reads=[sp_.b, one_t.b], writes=[sp_.b])
    P.op("dve", lambda e: e.tensor_scalar(s8[:], sp_[:], -8.0, None, ALU.mult), reads=[sp_.b], writes=[s8.b])
    P.op("dve", lambda e: e.tensor_scalar(s16[:], sp_[:], -16.0, None, ALU.mult), reads=[sp_.b], writes=[s16.b])
    for b in range(NB):
        sl = slice(b * TB, (b + 1) * TB)
        P.op("act", lambda e, sl=sl: e.activation(xc[:, sl], xp[:, sl], AF.Identity, bias=prm[:, 0:1], scale=cw[:, 0:1]),
             reads=[xp.b, prm.b, cw.b], writes=[xc.b])
        for j in range(1, 4):
            P.op("dve", lambda e, sl=sl, j=j, b=b: e.scalar_tensor_tensor(
                xc[:, sl], xp[:, b * TB + j:(b + 1) * TB + j], cw[:, j:j + 1], xc[:, sl], ALU.mult, ALU.add),
                reads=[xp.b, cw.b, xc.b], writes=[xc.b])
        P.op("pool", lambda e, sl=sl: e.tensor_copy(xcb[:, sl], xc[:, sl]), reads=[xc.b], writes=[xcb.b])

    for e_ in range(2):
        order = range(NB) if e_ == 0 else range(NB - 1, -1, -1)
        first = True
        for b in order:
            sl = slice(b * TB, (b + 1) * TB)
            if e_ == 1:
                gt = gts[b % 2]
                P.op("sp", lambda e, gt=gt, sl=sl: e.dma_start(out=gt[:], in_=gate_d[:, sl]),
                     reads=[gate_d.b], writes=[gt.b], dma_sem="d_" + gt.b.name)
            for sb in range(TB // 512):
                c0 = b * TB + sb * 512
                pr = nps()
                P.op("pe", lambda e, pr=pr, c0=c0, e_=e_: e.matmul(pr[:], waxb[:, e_ * 64:(e_ + 1) * 64], xcb[:, c0:c0 + 512],
                                                            start=True, stop=True),
                     reads=[waxb.b, xcb.b], writes=[pr.b])
                pi = nps()
                P.op("pe", lambda e, pi=pi, c0=c0, e_=e_: e.matmul(pi[:], waxb[:, 128 + e_ * 64:128 + (e_ + 1) * 64],
                                                            xcb[:, c0:c0 + 512], start=True, stop=True),
                     reads=[waxb.b, xcb.b], writes=[pi.b])
                P.op("act", lambda e, pr=pr, sb=sb, e_=e_: e.activation(r_[:, sb * 512:(sb + 1) * 512], pr[:], AF.Sigmoid,
                                                                 bias=prm[:, 1 + e_:2 + e_]),
                     reads=[pr.b, prm.b], writes=[r_.b])
                P.op("act", lambda e, pi=pi, sb=sb, e_=e_: e.activation(i_[:, sb * 512:(sb + 1) * 512], pi[:], AF.Sigmoid,
                                                                 bias=prm[:, 3 + e_:4 + e_]),
                     reads=[pi.b, prm.b], writes=[i_.b])
            P.op("act", lambda e, e_=e_: e.activation(a_[:], r_[:], AF.Exp, scale=s8[:, e_:e_ + 1]),
                 reads=[r_.b, s8.b], writes=[a_.b])
            P.op("act", lambda e, e_=e_: e.activation(a2[:], r_[:], AF.Exp, scale=s16[:, e_:e_ + 1]),
                 reads=[r_.b, s16.b], writes=[a2.b])
            P.op("act", lambda e: e.activation(a2[:], a2[:], AF.Sqrt, bias=one_t[0:64, :], scale=-1.0),
                 reads=[a2.b, one_t.b], writes=[a2.b])
            P.op("dve", lambda e, sl=sl: e.tensor_tensor(u_[:], i_[:], xc[:, sl], ALU.mult),
                 reads=[i_.b, xc.b], writes=[u_.b])
            P.op("dve", lambda e: e.tensor_tensor(u_[:], u_[:], a2[:], ALU.mult), reads=[u_.b, a2.b], writes=[u_.b])
            if e_ == 0:
                init = 0.0 if first else hf[:, b * TB - 1:b * TB]
                P.op("dve", lambda e, sl=sl, init=init: e.tensor_tensor_scan(hf[:, sl], a_[:], u_[:], init, ALU.mult, ALU.add),
                     reads=[a_.b, u_.b, hf.b], writes=[hf.b])
            else:
                init = 0.0 if first else carry[:]
                P.op("dve", lambda e, init=init: e.tensor_tensor_scan(hbk[:, ::-1], a_[:, ::-1], u_[:, ::-1], init,
                                                                      ALU.mult, ALU.add),
                     reads=[a_.b, u_.b, carry.b], writes=[hbk.b])
                P.op("dve", lambda e: e.tensor_copy(carry[:], hbk[:, 0:1]), reads=[hbk.b], writes=[carry.b])
                P.op("act", lambda e, gt=gt: e.activation(gl[:], gt[:], AF.Gelu_apprx_tanh), reads=[gt.b], writes=[gl.b])
                yo = yb[b % 2]
                P.op("dve", lambda e, sl=sl, yo=yo: e.tensor_tensor(yo[:], hf[:, sl], hbk[:], ALU.add),
                     reads=[hf.b, hbk.b], writes=[yo.b])
                P.op("dve", lambda e, yo=yo: e.tensor_tensor(yo[:], yo[:], gl[:], ALU.mult), reads=[yo.b, gl.b], writes=[yo.b])
                P.op("sp", lambda e, sl=sl, yo=yo: e.dma_start(out=y_d[:, sl], in_=yo[:]), reads=[yo.b], writes=[y_d.b],
                     dma_sem="o_" + yo.b.name)
            first = False


def prep_lru(pT, b_, j, lp, L):
    c0 = 1824 + 64 * j
    xpad = np.zeros((64, L + 3), np.float32)
    xpad[:, 2:2 + L] = pT[c0:c0 + 64]
    g0 = 2080 + 64 * j
    hs = slice(64 * j, 64 * j + 64)
    wax = np.concatenate([lp["lru_w_a"][0, j], lp["lru_w_a"][1, j], lp["lru_w_x"][0, j], lp["lru_w_x"][1, j]], axis=1)
    prm = np.stack([lp["lru_conv_b"][hs], lp["lru_b_a"][0, hs], lp["lru_b_a"][1, hs], lp["lru_b_x"][0, hs],
                    lp["lru_b_x"][1, hs], lp["lru_lambda"][0, hs], lp["lru_lambda"][1, hs]], axis=1)
    return {"xpad": xpad, "gate": np.ascontiguousarray(pT[g0:g0 + 64]),
            "cw": np.ascontiguousarray(lp["lru_conv_w"][:, hs].T), "wax": np.ascontiguousarray(wax, dtype=np.float32),
            "prm": np.ascontiguousarray(prm, dtype=np.float32)}


def _attn_finalize(P, acc, sel, y_d, L, pds, tag):
    rds = [P.sbuf(f"rd{tag}{i}", [64, 512], F32) for i in range(2)]
    yos = [P.sbuf(f"yo{tag}{i}", [64, 512], F32) for i in range(2)]
    for blk in range(L // 512):
        cols = slice(blk * 512, (blk + 1) * 512)
        pd = pds[blk % 2]
        rd = rds[blk % 2]
        yo = yos[blk % 2]
        P.op("pe", lambda e, pd=pd, cols=cols: e.matmul(pd[:], sel[:], acc[:, cols], start=True, stop=True),
             reads=[sel.b, acc.b], writes=[pd.b])
        P.op("dve", lambda e, pd=pd, rd=rd: e.reciprocal(rd[:], pd[:]), reads=[pd.b], writes=[rd.b])
        P.op("dve", lambda e, rd=rd, yo=yo, cols=cols: e.tensor_tensor(yo[:], acc[0:64, cols], rd[:], ALU.mult),
             reads=[acc.b, rd.b], writes=[yo.b])
        P.op("sp", lambda e, yo=yo, cols=cols: e.dma_start(out=y_d[:, cols], in_=yo[:]), reads=[yo.b], writes=[y_d.b],
             dma_sem="o_" + yo.b.name)


def _make_sel(P):
    sel = P.sbuf("sel", [65, 64], F32)
    P.op("dve", lambda e: e.memset(sel[:], 0.0), writes=[sel.b])
    P.op("dve", lambda e: e.memset(sel[64:65, :], 1.0), writes=[sel.b])
    return sel


def _load_cast(P, dst, src_d, L, stg, nrows=64, blkw=2048, eng="pool"):
    for b in range(L // blkw):
        s = stg[b % len(stg)]
        sl = slice(b * blkw, (b + 1) * blkw)
        P.op("sp", lambda e, s=s, sl=sl: e.dma_start(out=s[0:nrows, 0:blkw], in_=src_d[:, sl]), reads=[src_d.b], writes=[s.b],
             dma_sem="d_" + s.b.name)
        P.op(eng, lambda e, s=s, sl=sl: e.tensor_copy(dst[:, sl], s[0:nrows, 0:blkw]), reads=[s.b], writes=[dst.b])


def emit_na(P, L, d):
    ntile = L // 128
    qT_d, kT_d, va_d, bias_d, mask_d, y_d = d["qT"], d["kT"], d["vaug"], d["biasg"], d["maskc"], d["y"]

    qb = P.sbuf("qb", [64, L], BF16)
    kb = P.sbuf("kb", [64, L], BF16)
    stg = [P.sbuf(f"stg{i}", [64, 2048], F32) for i in range(2)]
    vb = P.sbuf("vb", [128, ntile, 65], BF16)
    vst = [P.sbuf(f"vst{i}", [128, 16, 65], F32) for i in range(2)]
    Fm = P.sbuf("Fm", [128, 5, 640], BF16)
    bst = P.sbuf("bst", [128, 640], F32)
    mst = P.sbuf("mst", [128, 640], F32)
    acc = P.sbuf("acc", [65, L], F32)
    sel = _make_sel(P)
    pexp = [P.sbuf(f"pexp{i}", [128, 640], BF16) for i in range(2)]
    pms = [P.sbuf(f"pm{i}", [128, 640], BF16) for i in range(2)]
    pSs = [P.psum(f"pS{i}", [128, 1024], F32) for i in range(2)]
    pos = [P.psum(f"po{i}", [65, 512], F32) for i in range(2)]
    pds = [P.psum(f"pd{i}", [64, 512], F32) for i in range(2)]

    _load_cast(P, qb, qT_d, L, stg)
    _load_cast(P, kb, kT_d, L, stg)
    for g in range((ntile + 15) // 16):
        s = vst[g % 2]
        n = min(16, ntile - g * 16)
        P.op("sp", lambda e, s=s, g=g, n=n: e.dma_start(out=s[:, 0:n, :], in_=va_d[:, g * 16:g * 16 + n, :]),
             reads=[va_d.b], writes=[s.b], dma_sem="d_" + s.b.name)
        P.op("pool", lambda e, s=s, g=g, n=n: e.tensor_copy(vb[:, g * 16:g * 16 + n, :], s[:, 0:n, :]),
             reads=[s.b], writes=[vb.b])
    for fi in range(5):
        P.op("sp", lambda e, fi=fi: e.dma_start(out=bst[:], in_=bias_d[:, fi, :]), reads=[bias_d.b], writes=[bst.b], dma_sem="d_bst")
        P.op("sp", lambda e, fi=fi: e.dma_start(out=mst[:], in_=mask_d[:, fi, :]), reads=[mask_d.b], writes=[mst.b], dma_sem="d_mst")
        P.op("act", lambda e: e.activation(bst[:], bst[:], AF.Exp), reads=[bst.b], writes=[bst.b])
        P.op("dve", lambda e, fi=fi: e.tensor_tensor(Fm[:, fi, :], bst[:], mst[:], ALU.mult), reads=[bst.b, mst.b], writes=[Fm.b])

    for m in range(ntile):
        tb = min(max(m - 2, 0), ntile - 5)
        fi = 0 if m == 0 else 1 if m == 1 else 3 if m == ntile - 2 else 4 if m == ntile - 1 else 2
        pS = pSs[m % 2]
        for jt in range(5):
            P.op("pe", lambda e, pS=pS, jt=jt, tb=tb, m=m: e.matmul(
                pS[:, jt * 128:(jt + 1) * 128], kb[:, (tb + jt) * 128:(tb + jt + 1) * 128], qb[:, m * 128:(m + 1) * 128],
                start=True, stop=True), reads=[kb.b, qb.b], writes=[pS.b])
        pe_ = pexp[m % 2]
        P.op("act", lambda e, pe_=pe_, pS=pS: e.activation(pe_[:], pS[:, 0:640], AF.Exp, scale=0.125),
             reads=[pS.b], writes=[pe_.b])
        pm_ = pms[m % 2]
        P.op("dve", lambda e, pe_=pe_, pm_=pm_, fi=fi: e.tensor_tensor(pm_[:], pe_[:], Fm[:, fi, :], ALU.mult),
             reads=[pe_.b, Fm.b], writes=[pm_.b])
        po = pos[(m // 4) % 2]
        for jt in range(5):
            P.op("pe", lambda e, po=po, jt=jt, tb=tb, m=m, pm_=pm_: e.matmul(
                po[:, (m % 4) * 128:(m % 4 + 1) * 128], vb[:, tb + jt, :], pm_[:, jt * 128:(jt + 1) * 128],
                start=(jt == 0), stop=(jt == 4)), reads=[vb.b, pm_.b], writes=[po.b])
        if m % 4 == 3:
            P.op("act", lambda e, po=po, m=m: e.activation(acc[:, (m - 3) * 128:(m + 1) * 128], po[:], AF.Copy),
                 reads=[po.b], writes=[acc.b])
    _attn_finalize(P, acc, sel, y_d, L, pds, "n")


def na_tables(L):
    rows = L // 64
    ntile = L // 128
    ms = [0, 1, 2, ntile - 2, ntile - 1]
    dr_i = np.zeros((5, 128, 5, 128), np.int64)
    dc_i = np.zeros((5, 128, 5, 128), np.int64)
    mask = np.zeros((5, 128, 5, 128), np.float32)
    pk = np.arange(128)
    fq = np.arange(128)
    for vi, m in enumerate(ms):
        tb = min(max(m - 2, 0), ntile - 5)
        for jt in range(5):
            krow = 2 * (tb + jt) + pk // 64
            kc = pk % 64
            qrow = 2 * m + fq // 64
            qc = fq % 64
            rs = np.clip(qrow - 4, 0, rows - 8)
            row_ok = (krow[:, None] >= rs[None, :]) & (krow[:, None] < rs[None, :] + 8)
            cs = np.clip(qc - 8, 0, 48)
            col_ok = (kc[:, None] >= cs[None, :]) & (kc[:, None] < cs[None, :] + 16)
            dr = np.clip(krow[:, None] - qrow[None, :], -7, 7)
            dc = np.clip(kc[:, None] - qc[None, :], -15, 15)
            dr_i[vi, :, jt, :] = dr + 7
            dc_i[vi, :, jt, :] = dc + 15
            mask[vi, :, jt, :] = (row_ok & col_ok).astype(np.float32)
    return dr_i, dc_i, mask


_NA_TAB = {}


def prep_na(pT, b_, j, lp, L):
    if L not in _NA_TAB:
        _NA_TAB[L] = na_tables(L)
    dr_i, dc_i, mask = _NA_TAB[L]
    ntile = L // 128
    q0, k0, v0 = 1056 + 64 * j, 1312 + 64 * j, 1568 + 64 * j
    v = pT[v0:v0 + 64]
    vaug = np.ones((128, ntile, 65), np.float32)
    vaug[:, :, 0:64] = v.T.reshape(ntile, 128, 64).transpose(1, 0, 2)
    rpb = lp["na_rpb"][j]
    biasg = rpb[dr_i, dc_i]
    biasg = np.ascontiguousarray(biasg.transpose(1, 0, 2, 3).reshape(128, 5, 640), dtype=np.float32)
    maskc = np.ascontiguousarray(mask.transpose(1, 0, 2, 3).reshape(128, 5, 640))
    return {"qT": np.ascontiguousarray(pT[q0:q0 + 64]), "kT": np.ascontiguousarray(pT[k0:k0 + 64]),
            "vaug": vaug, "biasg": biasg, "maskc": maskc}


DIL_D = (1, 4, 16)
DIL_PAD = 1024


def _dil_tiles(L):
    idx = {}
    t = 0
    for d in DIL_D:
        n = L // d
        for r in range(d):
            for m in range(n // 128 + 1):
                idx[(d, r, m)] = t
                t += 1
    return idx, t


def emit_dil(P, L, d):
    tidx, NT = _dil_tiles(L)
    names = ("qT", "qsT", "kT", "ksT", "cos2", "sin2s")
    dd = {nm: d[nm] for nm in names}
    va_d, mk_d, y_d = d["vaug"], d["maskab"], d["y"]

    qh = P.sbuf("qh", [64, L], BF16)
    kh = P.sbuf("kh", [64, L + 2 * DIL_PAD], BF16)
    BW = 1024
    stg = {nm: P.sbuf("s_" + nm, [64, BW], F32) for nm in names}
    t1 = P.sbuf("t1", [64, BW], F32)
    t2 = P.sbuf("t2", [64, BW], F32)
    vb = P.sbuf("vb", [128, NT, 65], BF16)
    vst = [P.sbuf(f"vst{i}", [128, 16, 65], F32) for i in range(2)]
    mk = P.sbuf("mk", [128, 256], F32)
    acc = P.sbuf("acc", [65, L], F32)
    sel = _make_sel(P)
    pexp = [P.sbuf(f"pexp{i}", [128, 256], BF16) for i in range(2)]
    pms = [P.sbuf(f"pm{i}", [128, 256], BF16) for i in range(3)]
    pSs = [P.psum(f"pS{i}", [128, 512], F32) for i in range(3)]
    pos = [P.psum(f"po{i}", [65, 512], F32) for i in range(2)]
    pds = [P.psum(f"pd{i}", [64, 512], F32) for i in range(2)]

    P.op("sp", lambda e: e.dma_start(out=mk[:], in_=mk_d[:, :]), reads=[mk_d.b], writes=[mk.b], dma_sem="d_mk")
    P.op("dve", lambda e: e.memset(kh[:, 0:DIL_PAD], 0.0), writes=[kh.b])
    P.op("dve", lambda e: e.memset(kh[:, DIL_PAD + L:DIL_PAD + L + DIL_PAD], 0.0), writes=[kh.b])
    for g in range((NT + 15) // 16):
        s = vst[g % 2]
        n = min(16, NT - g * 16)
        P.op("sp", lambda e, s=s, g=g, n=n: e.dma_start(out=s[:, 0:n, :], in_=va_d[:, g * 16:g * 16 + n, :]),
             reads=[va_d.b], writes=[s.b], dma_sem="d_" + s.b.name)
        P.op("pool", lambda e, s=s, g=g, n=n: e.tensor_copy(vb[:, g * 16:g * 16 + n, :], s[:, 0:n, :]),
             reads=[s.b], writes=[vb.b])
    for b in range(L // BW):
        sl = slice(b * BW, (b + 1) * BW)
        for nm in names:
            P.op("sp", lambda e, nm=nm, sl=sl: e.dma_start(out=stg[nm][:], in_=dd[nm][:, sl]), reads=[dd[nm].b],
                 writes=[stg[nm].b], dma_sem="d_s_" + nm)
        for (a, s_, dst, off) in (("qT", "qsT", qh, 0), ("kT", "ksT", kh, DIL_PAD)):
            P.op("dve", lambda e, a=a: e.tensor_tensor(t1[:], stg[a][:], stg["cos2"][:], ALU.mult),
                 reads=[stg[a].b, stg["cos2"].b], writes=[t1.b])
            P.op("pool", lambda e, s_=s_: e.tensor_tensor(t2[:], stg[s_][:], stg["sin2s"][:], ALU.mult),
                 reads=[stg[s_].b, stg["sin2s"].b], writes=[t2.b])
            P.op("dve", lambda e, dst=dst, off=off, b=b: e.tensor_tensor(
                dst[:, off + b * BW:off + (b + 1) * BW], t1[:], t2[:], ALU.add), reads=[t1.b, t2.b], writes=[dst.b])

    cnt = {"k": 0, "po": 0}
    for d in DIL_D:
        n = L // d
        nq = n // 128
        for r in range(d):
            prev = None
            po = None
            for m in range(nq + 1):
                c_lo = 128 if m == 0 else 0
                c_hi = 128 if m == nq else 256
                i0 = 128 * (m - 1) + c_lo
                cntq = c_hi - c_lo
                ks = DIL_PAD + r + d * (128 * m - 64)
                qs = r + d * i0
                pS = pSs[cnt["k"] % 3]
                pe_ = pexp[cnt["k"] % 2]
                pm_ = pms[cnt["k"] % 3]
                cnt["k"] += 1
                P.op("pe", lambda e, pS=pS, ks=ks, qs=qs, d=d, cntq=cntq, c_lo=c_lo, c_hi=c_hi: e.matmul(
                    pS[:, c_lo:c_hi], kh[:, ks:ks + 127 * d + 1:d], qh[:, qs:qs + (cntq - 1) * d + 1:d], start=True, stop=True),
                    reads=[kh.b, qh.b], writes=[pS.b])
                P.op("act", lambda e, pe_=pe_, pS=pS, c_lo=c_lo, c_hi=c_hi: e.activation(
                    pe_[:, c_lo:c_hi], pS[:, c_lo:c_hi], AF.Exp, scale=0.125), reads=[pS.b], writes=[pe_.b])
                P.op("dve", lambda e, pe_=pe_, pm_=pm_, c_lo=c_lo, c_hi=c_hi: e.tensor_tensor(
                    pm_[:, c_lo:c_hi], pe_[:, c_lo:c_hi], mk[:, c_lo:c_hi], ALU.mult), reads=[pe_.b, mk.b], writes=[pm_.b])
                if m >= 1:
                    mq = m - 1
                    if mq % 4 == 0:
                        po = pos[cnt["po"] % 2]
                        cnt["po"] += 1
                    osl = slice((mq % 4) * 128, (mq % 4 + 1) * 128)
                    P.op("pe", lambda e, po=po, osl=osl, prev=prev, tA=tidx[(d, r, mq)]: e.matmul(
                        po[:, osl], vb[:, tA, :], prev[:, 128:256], start=True, stop=False),
                        reads=[vb.b, prev.b], writes=[po.b])
                    P.op("pe", lambda e, po=po, osl=osl, pm_=pm_, tB=tidx[(d, r, m)]: e.matmul(
                        po[:, osl], vb[:, tB, :], pm_[:, 0:128], start=False, stop=True),
                        reads=[vb.b, pm_.b], writes=[po.b])
                    if mq % 4 == 3 or mq == nq - 1:
                        mq0 = mq - (mq % 4)
                        w = (mq % 4 + 1) * 128
                        a0 = r + d * 128 * mq0
                        if d == 1:
                            P.op("act", lambda e, po=po, a0=a0, w=w: e.activation(acc[:, a0:a0 + w], po[:, 0:w], AF.Copy),
                                 reads=[po.b], writes=[acc.b])
                        else:
                            P.op("dve", lambda e, po=po, a0=a0, w=w, d=d: e.tensor_tensor(
                                acc[:, a0:a0 + (w - 1) * d + 1:d], acc[:, a0:a0 + (w - 1) * d + 1:d], po[:, 0:w], ALU.add),
                                reads=[po.b, acc.b], writes=[acc.b])
                prev = pm_
    _attn_finalize(P, acc, sel, y_d, L, pds, "d")


_ROPE = {}


def rope_tables(L):
    if L not in _ROPE:
        pos = np.arange(L, dtype=np.float32)
        inv_freq = (np.float32(10000.0) ** (-np.arange(0, 64, 2, dtype=np.float32) / np.float32(64))).astype(np.float32)
        ang = (pos[:, None] * inv_freq[None, :]).astype(np.float32)
        c = np.cos(ang).astype(np.float32).T
        s = np.sin(ang).astype(np.float32).T
        _ROPE[L] = (np.ascontiguousarray(np.concatenate([c, c], 0)), np.ascontiguousarray(np.concatenate([-s, s], 0)))
    return _ROPE[L]


_DIL_MASK = None


def prep_dil(pT, b_, j, lp, L):
    tidx, NT = _dil_tiles(L)
    q0, k0, v0 = 2336 + 64 * j, 2592 + 64 * j, 2848 + 64 * j
    q = pT[q0:q0 + 64]
    k = pT[k0:k0 + 64]
    v = pT[v0:v0 + 64]
    cos2, sin2s = rope_tables(L)
    vT = np.ascontiguousarray(v.T)
    vaug = np.zeros((128, NT, 65), np.float32)
    pk = np.arange(128)
    for d in DIL_D:
        n = L // d
        for r in range(d):
            for m in range(n // 128 + 1):
                i = 128 * m - 64 + pk
                ok = (i >= 0) & (i < n)
                tok = r + d * i[ok]
                t = tidx[(d, r, m)]
                vaug[ok, t, 0:64] = vT[tok]
                vaug[ok, t, 64] = 1.0
    fq = np.arange(128)
    mb = (pk[:, None] <= fq[None, :]).astype(np.float32)
    ma = (pk[:, None] >= fq[None, :]).astype(np.float32)
    return {"qT": np.ascontiguousarray(q), "qsT": np.ascontiguousarray(np.concatenate([q[32:], q[:32]], 0)),
            "kT": np.ascontiguousarray(k), "ksT": np.ascontiguousarray(np.concatenate([k[32:], k[:32]], 0)),
            "cos2": cos2, "sin2s": sin2s, "vaug": vaug, "maskab": np.ascontiguousarray(np.concatenate([mb, ma], 1))}


def emit_gla(P, L, d, debug=False):
    n = L // 128
    qT_d, kT_d, gT_d, vt_d = d["qT"], d["kT"], d["gT"], d["vtok"]
    z_d = [d["z0T"], d["z1T"]]
    wg_d, prm_d, cst_d, rst_d, y_d = d["wg"], d["prm"], d["cst"], d["resetm"], d["y"]

    one_t, eps_t = _consts(P)
    cs = P.sbuf("cs", [64, L], F32)
    csm = P.sbuf("csm", [64, n, 1], F32)
    qt = P.sbuf("qt", [64, L], BF16)
    kt = P.sbuf("kt", [64, L], BF16)
    acc = P.sbuf("acc", [64, L], F32)
    vb = P.sbuf("vb", [128, n, 64], BF16)
    ktok = P.sbuf("ktok", [128, n, 64], BF16)
    vst = [P.sbuf(f"vst{i}", [128, 16, 64], F32) for i in range(2)]
    BW = 1024
    stgq = [P.sbuf(f"stgq{i}", [64, BW], F32) for i in range(2)]
    stgk = [P.sbuf(f"stgk{i}", [64, BW], F32) for i in range(2)]
    ep = P.sbuf("ep", [64, BW], F32)
    em = P.sbuf("em", [64, BW], F32)
    zs = [P.sbuf(f"zs{i}", [16, 512], F32) for i in range(2)]
    wg = P.sbuf("wg_s", [16, 128], F32)
    prm = P.sbuf("prm_s", [64, 3], F32)
    nbg = P.sbuf("nbg", [64, 2], F32)
    cst = P.sbuf("cst_s", [128, 3, 128], F32)
    ident = P.sbuf("ident", [64, 64], BF16)
    rst = P.sbuf("rst", [64, 512], F32)
    tA = P.sbuf("tA", [64, 512], F32)
    tB = P.sbuf("tB", [64, 512], F32)
    dmid = P.sbuf("dmid", [64, n], F32)
    dlast = P.sbuf("dlast", [64, n], F32)
    dkv = P.sbuf("dkv", [64, n], F32)
    S = P.sbuf("S", [64, 64], F32)
    Sts = [P.sbuf(f"St{i}", [64, 64], BF16) for i in range(2)]
    tkv = P.sbuf("tkv", [64, 64], F32)
    Asb = [P.sbuf(f"Asb{i}", [128, 128], BF16) for i in range(2)]
    ones64 = P.sbuf("ones64", [64, 64], F32)
    gst = [P.sbuf(f"gst{i}", [64, 512], F32) for i in range(2)]
    yos = [P.sbuf(f"yo{i}", [64, 512], F32) for i in range(2)]
    rs = P.sbuf("rs", [64, 512], F32)
    pA = [P.psum(f"pA{i}", [128, 512], F32) for i in range(2)]
    pT_ = [P.psum(f"pT{i}", [128, 4, 64], BF16) for i in range(2)]
    pKV = [P.psum(f"pKV{i}", [64, 64], F32) for i in range(2)]
    pO = [P.psum(f"pO{i}", [64, 512], F32) for i in range(2)]

    for t, d in ((wg, wg_d), (prm, prm_d), (cst, cst_d), (rst, rst_d)):
        P.op("sp", lambda e, t=t, d=d: e.dma_start(out=t[:], in_=d.h), reads=[d.b], writes=[t.b], dma_sem="d_" + t.b.name)
    P.op("dve", lambda e: e.tensor_copy(ident[:], cst[0:64, 2, 0:64]), reads=[cst.b], writes=[ident.b])
    P.op("dve", lambda e: e.tensor_scalar(nbg[:], prm[:, 0:2], -1.0, None, ALU.mult), reads=[prm.b], writes=[nbg.b])
    P.op("dve", lambda e: e.memset(ones64[:], 1.0 / 64.0), writes=[ones64.b])
    for g in range((n + 15) // 16):
        s = vst[g % 2]
        m_ = min(16, n - g * 16)
        P.op("sp", lambda e, s=s, g=g, m_=m_: e.dma_start(out=s[:, 0:m_, :], in_=vt_d[:, g * 16:g * 16 + m_, :]),
             reads=[vt_d.b], writes=[s.b], dma_sem="d_" + s.b.name)
        P.op("pool", lambda e, s=s, g=g, m_=m_: e.tensor_copy(vb[:, g * 16:g * 16 + m_, :], s[:, 0:m_, :]),
             reads=[s.b], writes=[vb.b])

    csv = cs[:].rearrange("p (n c) -> p n c", c=128)
    for e_ in range(2):
        mid = 63 if e_ == 0 else 64
        last = 127 if e_ == 0 else 0
        for blk in range(L // 512):
            cols = slice(blk * 512, (blk + 1) * 512)
            z_ = zs[blk % 2]
            P.op("sp", lambda e, z_=z_, cols=cols, e_=e_: e.dma_start(out=z_[:], in_=z_d[e_][:, cols]),
                 reads=[z_d[e_].b], writes=[z_.b], dma_sem="d_" + z_.b.name)
            pl = pA[blk % 2]
            P.op("pe", lambda e, pl=pl, z_=z_, e_=e_: e.matmul(pl[0:64, :], wg[:, e_ * 64:(e_ + 1) * 64], z_[:],
                                                             start=True, stop=True),
                 reads=[wg.b, z_.b], writes=[pl.b])
            P.op("act", lambda e, pl=pl, e_=e_: e.activation(tA[:], pl[0:64, :], AF.Exp, bias=nbg[:, e_:e_ + 1], scale=-1.0),
                 reads=[pl.b, nbg.b], writes=[tA.b])
            P.op("act", lambda e: e.activation(tB[:], tA[:], AF.Ln, bias=one_t[0:64, :]),
                 reads=[tA.b, one_t.b], writes=[tB.b])
            if e_ == 0:
                P.op("dve", lambda e, cols=cols: e.tensor_tensor_scan(cs[:, cols], rst[:], tB[:], 0.0, ALU.mult, ALU.add),
                     reads=[rst.b, tB.b], writes=[cs.b])
            else:
                P.op("dve", lambda e, cols=cols: e.tensor_tensor_scan(cs[:, cols][:, ::-1], rst[:], tB[:, ::-1], 0.0,
                                                                      ALU.mult, ALU.add),
                     reads=[rst.b, tB.b], writes=[cs.b])
        P.op("act", lambda e, mid=mid: e.activation(dmid[:], cs[:, mid::128], AF.Exp, scale=-1.0 / 16),
             reads=[cs.b], writes=[dmid.b])
        P.op("act", lambda e, last=last: e.activation(dlast[:], cs[:, last::128], AF.Exp, scale=-1.0 / 16),
             reads=[cs.b], writes=[dlast.b])
        P.op("dve", lambda e, mid=mid: e.tensor_copy(csm[:], csv[:, :, mid:mid + 1]), reads=[cs.b], writes=[csm.b])
        P.op("dve", lambda e: e.tensor_tensor(csv, csv, csm[:].to_broadcast([64, n, 128]), ALU.subtract),
             reads=[cs.b, csm.b], writes=[cs.b])
        P.op("act", lambda e, last=last: e.activation(dkv[:], cs[:, last::128], AF.Exp, scale=-1.0 / 16),
             reads=[cs.b], writes=[dkv.b])
        for b in range(L // BW):
            sl = slice(b * BW, (b + 1) * BW)
            sq_, sk_ = stgq[b % 2], stgk[b % 2]
            P.op("sp", lambda e, sq_=sq_, sl=sl: e.dma_start(out=sq_[:], in_=qT_d[:, sl]), reads=[qT_d.b], writes=[sq_.b],
                 dma_sem="d_" + sq_.b.name)
            P.op("sp", lambda e, sk_=sk_, sl=sl: e.dma_start(out=sk_[:], in_=kT_d[:, sl]), reads=[kT_d.b], writes=[sk_.b],
                 dma_sem="d_" + sk_.b.name)
            P.op("act", lambda e, sl=sl: e.activation(ep[:], cs[:, sl], AF.Exp, scale=-1.0 / 16), reads=[cs.b], writes=[ep.b])
            P.op("act", lambda e, sl=sl: e.activation(em[:], cs[:, sl], AF.Exp, scale=1.0 / 16), reads=[cs.b], writes=[em.b])
            P.op("dve", lambda e, sl=sl, sq_=sq_: e.scalar_tensor_tensor(qt[:, sl], sq_[:], 0.125, ep[:], ALU.mult, ALU.mult),
                 reads=[sq_.b, ep.b], writes=[qt.b])
            P.op("pool", lambda e, sl=sl, sk_=sk_: e.tensor_tensor(kt[:, sl], sk_[:], em[:], ALU.mult),
                 reads=[sk_.b, em.b], writes=[kt.b])
        for g in range(n // 4):
            pt = pT_[g % 2]
            for i in range(4):
                c = g * 4 + i
                P.op("pe", lambda e, pt=pt, i=i, c=c: e.transpose(pt[:, i, :], kt[:, c * 128:(c + 1) * 128], ident[:]),
                     reads=[kt.b, ident.b], writes=[pt.b])
            P.op("act", lambda e, pt=pt, g=g: e.activation(ktok[:, g * 4:(g + 1) * 4, :], pt[:], AF.Copy),
                 reads=[pt.b], writes=[ktok.b])
        P.op("dve", lambda e: e.memset(S[:], 0.0), writes=[S.b])
        P.op("dve", lambda e: e.memset(Sts[0][:], 0.0), writes=[Sts[0].b])
        order = list(range(n)) if e_ == 0 else list(range(n - 1, -1, -1))
        mslot = 0 if e_ == 0 else 1

        def emit_A(k):
            c = order[k]
            pa = pA[k % 2]
            P.op("pe", lambda e, pa=pa, c=c: e.matmul(pa[:, 0:128], kt[:, c * 128:(c + 1) * 128], qt[:, c * 128:(c + 1) * 128],
                                                      start=True, stop=True), reads=[kt.b, qt.b], writes=[pa.b])
            P.op("dve", lambda e, pa=pa, k=k, mslot=mslot: e.tensor_tensor(Asb[k % 2][:], pa[:, 0:128], cst[:, mslot, :], ALU.mult),
                 reads=[pa.b, cst.b], writes=[Asb[k % 2].b])

        emit_A(0)
        for k, c in enumerate(order):
            if k + 1 < n:
                emit_A(k + 1)
            pk = pKV[k % 2]
            P.op("pe", lambda e, pk=pk, c=c: e.matmul(pk[:], ktok[:, c, :], vb[:, c, :], start=True, stop=True),
                 reads=[ktok.b, vb.b], writes=[pk.b])
            po = pO[(c // 4) % 2]
            osl = slice((c % 4) * 128, (c % 4 + 1) * 128)
            St = Sts[k % 2]
            P.op("pe", lambda e, po=po, osl=osl, c=c, k=k: e.matmul(po[:, osl], vb[:, c, :], Asb[k % 2][:], start=True, stop=False),
                 reads=[vb.b, Asb[k % 2].b], writes=[po.b])
            P.op("pe", lambda e, po=po, osl=osl, c=c, St=St: e.matmul(po[:, osl], St[:], qt[:, c * 128:(c + 1) * 128],
                                                                      start=False, stop=True),
                 reads=[St.b, qt.b], writes=[po.b])
            if k + 1 < n:
                cn = order[k + 1]
                Sn = Sts[(k + 1) % 2]
                P.op("dve", lambda e, pk=pk, c=c: e.tensor_scalar(tkv[:], pk[:], dkv[:, c:c + 1], None, ALU.mult),
                     reads=[pk.b, dkv.b], writes=[tkv.b])
                P.op("dve", lambda e, c=c: e.scalar_tensor_tensor(S[:], S[:], dlast[:, c:c + 1], tkv[:], ALU.mult, ALU.add),
                     reads=[S.b, dlast.b, tkv.b], writes=[S.b])
                P.op("dve", lambda e, Sn=Sn, cn=cn: e.tensor_scalar(Sn[:], S[:], dmid[:, cn:cn + 1], None, ALU.mult),
                     reads=[S.b, dmid.b], writes=[Sn.b])
            done = (c % 4 == 3) if e_ == 0 else (c % 4 == 0)
            if done:
                g0 = (c // 4) * 512
                if e_ == 0:
                    P.op("act", lambda e, po=po, g0=g0: e.activation(acc[:, g0:g0 + 512], po[:], AF.Copy),
                         reads=[po.b], writes=[acc.b])
                else:
                    P.op("dve", lambda e, po=po, g0=g0: e.tensor_tensor(acc[:, g0:g0 + 512], acc[:, g0:g0 + 512], po[:], ALU.add),
                         reads=[po.b, acc.b], writes=[acc.b])
    if debug:
        dbg_acc = P.dram("dbg_acc", [64, L], F32, "ExternalOutput")
        dbg_cs = P.dram("dbg_cs", [64, L], F32, "ExternalOutput")
        dbg_d = P.dram("dbg_d", [64, 3, n], F32, "ExternalOutput")
        dbg_kt = P.dram("dbg_kt", [128, n, 64], F32, "ExternalOutput")
        ktf = P.sbuf("ktf", [128, n, 64], F32)
        P.op("dve", lambda e: e.tensor_copy(ktf[:], ktok[:]), reads=[ktok.b], writes=[ktf.b])
        P.op("sp", lambda e: e.dma_start(out=dbg_kt.h, in_=ktf[:]), reads=[ktf.b], writes=[dbg_kt.b], dma_sem="o_dbg")
        P.op("sp", lambda e: e.dma_start(out=dbg_acc[:, :], in_=acc[:]), reads=[acc.b], writes=[dbg_acc.b], dma_sem="o_dbg")
        P.op("sp", lambda e: e.dma_start(out=dbg_cs[:, :], in_=cs[:]), reads=[cs.b], writes=[dbg_cs.b], dma_sem="o_dbg")
        for i_, t_ in enumerate((dmid, dlast, dkv)):
            P.op("sp", lambda e, i_=i_, t_=t_: e.dma_start(out=dbg_d[:, i_, :], in_=t_[:]), reads=[t_.b], writes=[dbg_d.b], dma_sem="o_dbg")
    for blk in range(L // 512):
        cols = slice(blk * 512, (blk + 1) * 512)
        g_ = gst[blk % 2]
        yo = yos[blk % 2]
        P.op("sp", lambda e, g_=g_, cols=cols: e.dma_start(out=g_[:], in_=gT_d[:, cols]), reads=[gT_d.b], writes=[g_.b],
             dma_sem="d_" + g_.b.name)
        P.op("act", lambda e, cols=cols: e.activation(tA[:], acc[:, cols], AF.Square), reads=[acc.b], writes=[tA.b])
        pm_ = pA[blk % 2]
        P.op("pe", lambda e, pm_=pm_: e.matmul(pm_[0:64, :], ones64[:], tA[:], start=True, stop=True),
             reads=[ones64.b, tA.b], writes=[pm_.b])
        P.op("act", lambda e, pm_=pm_: e.activation(rs[:], pm_[0:64, :], AF.Sqrt, bias=eps_t[0:64, :]),
             reads=[pm_.b, eps_t.b], writes=[rs.b])
        P.op("dve", lambda e: e.reciprocal(rs[:], rs[:]), reads=[rs.b], writes=[rs.b])
        P.op("act", lambda e, g_=g_: e.activation(tB[:], g_[:], AF.Silu), reads=[g_.b], writes=[tB.b])
        P.op("dve", lambda e, yo=yo, cols=cols: e.scalar_tensor_tensor(yo[:], acc[:, cols], prm[:, 2:3], rs[:], ALU.mult, ALU.mult),
             reads=[acc.b, prm.b, rs.b], writes=[yo.b])
        P.op("dve", lambda e, yo=yo: e.tensor_tensor(yo[:], yo[:], tB[:], ALU.mult), reads=[yo.b, tB.b], writes=[yo.b])
        P.op("sp", lambda e, yo=yo, cols=cols: e.dma_start(out=y_d[:, cols], in_=yo[:]), reads=[yo.b], writes=[y_d.b],
             dma_sem="o_" + yo.b.name)


_GLA_CST = None


def gla_consts():
    global _GLA_CST
    if _GLA_CST is None:
        i = np.arange(128)
        maskf = (i[:, None] <= i[None, :]).astype(np.float32)
        maskb = (i[:, None] >= i[None, :]).astype(np.float32)
        cst = np.ascontiguousarray(np.stack([maskf, maskb, np.eye(128, dtype=np.float32)], axis=1))
        rst = np.ones((64, 512), np.float32)
        rst[:, ::128] = 0.0
        _GLA_CST = (cst, rst)
    return _GLA_CST


def prep_gla(pT, b_, j, lp, L):
    n = L // 128
    cst, rst = gla_consts()
    hs = slice(64 * j, 64 * j + 64)
    v = pT[512 + 64 * j:512 + 64 * j + 64]
    vtok = np.ascontiguousarray(v.T.reshape(n, 128, 64).transpose(1, 0, 2))
    wg = np.concatenate([lp["gla_w_gate"][0][:, hs], lp["gla_w_gate"][1][:, hs]], axis=1)
    prm = np.stack([lp["gla_b_gate"][0, hs], lp["gla_b_gate"][1, hs], lp["gla_norm"][hs]], axis=1)
    return {"qT": np.ascontiguousarray(pT[64 * j:64 * j + 64]), "kT": np.ascontiguousarray(pT[256 + 64 * j:256 + 64 * j + 64]),
            "gT": np.ascontiguousarray(pT[768 + 64 * j:768 + 64 * j + 64]), "vtok": vtok,
            "z0T": np.ascontiguousarray(pT[1024:1040]), "z1T": np.ascontiguousarray(pT[1040:1056]),
            "wg": np.ascontiguousarray(wg, dtype=np.float32), "prm": np.ascontiguousarray(prm, dtype=np.float32),
            "cst": cst, "resetm": rst}


def mixer_specs(L):
    n = L // 128
    _, NT = _dil_tiles(L)
    v = [64, L]
    return {
        "gla": {"qT": v, "kT": v, "gT": v, "vtok": [128, n, 64], "z0T": [16, L], "z1T": [16, L], "wg": [16, 128],
                "prm": [64, 3], "cst": [128, 3, 128], "resetm": [64, 512]},
        "na": {"qT": v, "kT": v, "vaug": [128, n, 65], "biasg": [128, 5, 640], "maskc": [128, 5, 640]},
        "lru": {"xpad": [64, L + 3], "gate": v, "cw": [64, 4], "wax": [64, 256], "prm": [64, 7]},
        "dil": {"qT": v, "qsT": v, "kT": v, "ksT": v, "cos2": v, "sin2s": v, "vaug": [128, NT, 65], "maskab": [128, 256]},
    }


EMITTERS = {"gla": emit_gla, "na": emit_na, "lru": emit_lru, "dil": emit_dil}
PREPS = {"gla": prep_gla, "na": prep_na, "lru": prep_lru, "dil": prep_dil}


def build_mixers(L, which=("gla", "na", "lru", "dil")):
    nc = bass.Bass("TRN2", target_bir_lowering=False)
    P = Prog(nc)
    specs = mixer_specs(L)
    ds = {}
    for m in which:
        ds[m] = {k: P.dram(f"{m}_{k}", shp, F32, "ExternalInput") for k, shp in specs[m].items()}
        ds[m]["y"] = P.dram(f"{m}_y", [64, L], F32, "ExternalOutput")
    mark = P.sb_off
    for i, m in enumerate(which):
        if i:
            P.phase_reset(mark)
        EMITTERS[m](P, L, ds[m])
    P.finish()
    P.emit()
    return nc


def prep_mixers(pT, b_, j, lp, L, which=("gla", "na", "lru", "dil")):
    out = {}
    for m in which:
        for k, v in PREPS[m](pT, b_, j, lp, L).items():
            out[f"{m}_{k}"] = v
    return out


_NC_CACHE = {}


def _get_nc(key, builder):
    return builder()


def _launch(nc, in_maps):
    return run_bass_kernel_spmd(nc, in_maps, core_ids=list(range(NCORES))).results


def kernel(x, mix_norm_pre, mix_norm_post, w_in, gla_w_gate, gla_b_gate, gla_norm, na_rpb,
           lru_conv_w, lru_conv_b, lru_w_a, lru_b_a, lru_w_x, lru_b_x, lru_lambda, w_out,
           ffn_norm_pre, ffn_norm_post, ffn_w_in, ffn_w_out):
    f32 = lambda a: np.ascontiguousarray(np.asarray(a), dtype=np.float32)
    x = f32(x)
    prm = dict(mix_norm_pre=f32(mix_norm_pre), mix_norm_post=f32(mix_norm_post), w_in=f32(w_in),
               gla_w_gate=f32(gla_w_gate), gla_b_gate=f32(gla_b_gate), gla_norm=f32(gla_norm), na_rpb=f32(na_rpb),
               lru_conv_w=f32(lru_conv_w), lru_conv_b=f32(lru_conv_b), lru_w_a=f32(lru_w_a), lru_b_a=f32(lru_b_a),
               lru_w_x=f32(lru_w_x), lru_b_x=f32(lru_b_x), lru_lambda=f32(lru_lambda), w_out=f32(w_out),
               ffn_norm_pre=f32(ffn_norm_pre), ffn_norm_post=f32(ffn_norm_post), ffn_w_in=f32(ffn_w_in),
               ffn_w_out=f32(ffn_w_out))
    L = SEQ
    xf = x.reshape(BATCH * SEQ, D_MODEL)
    xT = [np.ascontiguousarray(xf[c * TPC:(c + 1) * TPC].T) for c in range(NCORES)]
    yT = None
    for l in range(DEPTH + 1):
        has_front = l > 0
        has_back = l < DEPTH
        in_maps = []
        for c in range(NCORES):
            m = {"xT": xT[c]}
            if has_front:
                lf = l - 1
                m.update({"yT": yT[c], "w_out": prm["w_out"][lf], "ffn_w_in": prm["ffn_w_in"][lf],
                          "ffn_w_out": prm["ffn_w_out"][lf],
                          "g_front": np.ascontiguousarray(np.stack([gvec(prm["mix_norm_post"][lf]), gvec(prm["ffn_norm_pre"][lf]),
                                                                    gvec(prm["ffn_norm_post"][lf])], axis=1))})
            if has_back:
                m.update({"w_in": prm["w_in"][l], "g_back": gvec(prm["mix_norm_pre"][l])})
            in_maps.append(m)
        nc = _get_nc(("dense", has_front, has_back), lambda: build_dense(has_front, has_back))
        res = _launch(nc, in_maps)
        if has_front:
            xT = [res[c]["xoT"] for c in range(NCORES)]
        if not has_back:
            break
        pT = [np.ascontiguousarray(np.concatenate([res[b_ * 4 + i]["pT"] for i in range(4)], axis=1)) for b_ in range(BATCH)]
        lp = {k: v[l] for k, v in prm.items()}
        nc = _get_nc(("mix",), lambda: build_mixers(L))
        mres = _launch(nc, [prep_mixers(pT[c // 4], c // 4, c % 4, lp, L) for c in range(NCORES)])
        yT = []
        for c in range(NCORES):
            b_, i = c // 4, c % 4
            rows = [mres[b_ * 4 + j][f"{m}_y"][:, i * TPC:(i + 1) * TPC] for m in ("gla", "na", "lru", "dil") for j in range(4)]
            yT.append(np.ascontiguousarray(np.concatenate(rows, axis=0)))
    out = np.concatenate([xT[c].T for c in range(NCORES)], axis=0).reshape(BATCH, SEQ, D_MODEL)
    return np.ascontiguousarray(out, dtype=np.float32)
```

```python
import contextlib
import numpy as np
import concourse.bass as bass
import concourse.mybir as mybir
from concourse.bass_utils import run_bass_kernel_spmd

F32 = mybir.dt.float32
BF16 = mybir.dt.bfloat16
AF = mybir.ActivationFunctionType
ALU = mybir.AluOpType

D_MODEL = 1024
SEQ = 8192
BATCH = 2
DEPTH = 4
D_IN = 3104
D_FF = 2816
EPS = 1e-6
NCORES = 8
TPC = 2048
SBUF_BASE = 16512
SBUF_LIMIT = 226000

ENGS = ("sp", "act", "pe", "dve", "pool")


class Buf:
    __slots__ = ("name", "last_w", "w_dma", "readers")

    def __init__(self, name):
        self.name = name
        self.last_w = None
        self.w_dma = False
        self.readers = {}


class Ten:
    def __init__(self, h, b):
        self.h = h
        self.b = b

    def __getitem__(self, idx):
        return self.h[idx]


class Prog:
    def __init__(self, nc):
        self.nc = nc
        self.stack = contextlib.ExitStack()
        self.ops = {e: [] for e in ENGS}
        self.waited = {e: {} for e in ENGS}
        self.sems = {}
        self.count = {}
        self.dma_sems = set()
        self.sb_off = SBUF_BASE
        self.ps_bank = 0
        self.psall = None
        self.phase = 0

    def sem(self, name):
        if name not in self.sems:
            self.sems[name] = self.stack.enter_context(self.nc.semaphore(name))
            self.count[name] = 0
        return self.sems[name]

    def sbuf(self, name, shape, dtype):
        nbytes = int(np.prod(shape[1:])) * (4 if dtype == F32 else 2)
        off = (self.sb_off + 63) // 64 * 64
        uname = f"p{self.phase}_{name}"
        h = self.nc.alloc_sbuf_tensor_at(uname, list(shape), dtype, offset=off)
        self.sb_off = off + nbytes
        assert self.sb_off <= SBUF_LIMIT, (uname, self.sb_off)
        return Ten(h, Buf(uname))

    def psum(self, name, shape, dtype=F32):
        if self.psall is None:
            self.psall = self.nc.alloc_psum_tensor("psall", [128, 4096], F32)
        e32 = int(np.prod(shape[1:])) * (4 if dtype == F32 else 2) // 4
        nb = (e32 + 511) // 512
        st = self.ps_bank * 512
        self.ps_bank += nb
        assert self.ps_bank <= 8, name
        v = self.psall[0:shape[0], st:st + e32]
        if dtype != F32:
            v = v.bitcast(dtype)
        if len(shape) == 3:
            v = v.rearrange("p (a b) -> p a b", b=shape[2])
        return Ten(v, Buf(f"p{self.phase}_{name}"))

    def barrier(self):
        for eng in ENGS:
            waits = []
            for sname, cnt in sorted(self.count.items()):
                if cnt == 0 or sname == "c_" + eng:
                    continue
                if self.waited[eng].get(sname, 0) >= cnt:
                    continue
                waits.append((sname, cnt))
                self.waited[eng][sname] = cnt
            if waits:
                self.ops[eng].append((waits, None, None, 0, True))

    def phase_reset(self, sb_mark):
        self.barrier()
        self.sb_off = sb_mark
        self.ps_bank = 0
        self.phase += 1

    def dram(self, name, shape, dtype, kind):
        h = self.nc.dram_tensor(name, list(shape), dtype, kind=kind).ap()
        return Ten(h, Buf(name))

    def op(self, eng, fn, reads=(), writes=(), dma_sem=None):
        waits = {}
        own = "c_" + eng

        def need(tok, is_dma):
            if tok is None:
                return
            s, v = tok
            if eng == "pe" and s == own:
                return
            if s in self.dma_sems:
                v = max(v, self.count[s])
            if self.waited[eng].get(s, 0) >= v:
                return
            if waits.get(s, 0) < v:
                waits[s] = v

        for b in reads:
            need(b.last_w, b.w_dma)
        for b in writes:
            grouped = (dma_sem is not None and b.w_dma and not b.readers
                       and b.last_w is not None and b.last_w[0] == dma_sem)
            if not grouped:
                need(b.last_w, b.w_dma)
            for s, v in b.readers.items():
                need((s, v), False)
        if dma_sem is not None:
            self.sem(dma_sem)
            self.dma_sems.add(dma_sem)
            sname, inc = dma_sem, 16
        else:
            self.sem(own)
            sname, inc = own, 1
        self.count[sname] += inc
        tok = (sname, self.count[sname])
        for s, v in waits.items():
            self.sem(s)
            self.waited[eng][s] = v
        for b in reads:
            if b.readers.get(tok[0], 0) < tok[1]:
                b.readers[tok[0]] = tok[1]
        for b in writes:
            b.last_w = tok
            b.w_dma = dma_sem is not None
            b.readers = {}
        self.ops[eng].append((sorted(waits.items()), fn, sname, inc, dma_sem is not None))
        return tok

    def finish(self):
        waits = [(s, self.count[s]) for s in sorted(self.dma_sems)]
        self.ops["sp"].append((waits, None, None, 0, True))

    def emit(self):
        nc = self.nc
        sems = self.sems

        def replay(e, name):
            for waits, fn, sname, inc, is_dma in self.ops[name]:
                if fn is None:
                    for s, v in waits:
                        e.wait_ge(sems[s], v)
                    continue
                if is_dma or not waits:
                    for s, v in waits:
                        e.wait_ge(sems[s], v)
                    ins = fn(e)
                else:
                    for s, v in waits[:-1]:
                        e.wait_ge(sems[s], v)
                    ins = fn(e)
                    ins._wait_ge(sems[waits[-1][0]], waits[-1][1])
                ins.then_inc(sems[sname], inc)

        with nc.Block() as block:
            @block.sync
            def _(e):
                replay(e, "sp")

            @block.scalar
            def _(e):
                replay(e, "act")

            @block.tensor
            def _(e):
                replay(e, "pe")

            @block.vector
            def _(e):
                replay(e, "dve")

            @block.gpsimd
            def _(e):
                replay(e, "pool")
        self.stack.close()


def build_dense(has_front, has_back):
    nc = bass.Bass("TRN2", target_bir_lowering=False)
    P = Prog(nc)
    T = TPC
    HT = 1024
    NBLK = HT // 512

    xT_d = P.dram("xT", [D_MODEL, T], F32, "ExternalInput")
    xv = xT_d.h.rearrange("(c p) t -> p c t", p=128)
    if has_front:
        yT_d = P.dram("yT", [D_MODEL, T], F32, "ExternalInput")
        yv = yT_d.h.rearrange("(c p) t -> p c t", p=128)
        wo_d = P.dram("w_out", [D_MODEL, D_MODEL], F32, "ExternalInput")
        wov = wo_d.h.rearrange("(c p) n -> p c n", p=128)
        wfi_d = P.dram("ffn_w_in", [D_MODEL, 2 * D_FF], F32, "ExternalInput")
        wfiv = wfi_d.h.rearrange("(c p) n -> p c n", p=128)
        wfo_d = P.dram("ffn_w_out", [D_FF, D_MODEL], F32, "ExternalInput")
        wfov = wfo_d.h.rearrange("(c p) n -> p c n", p=128)
        g_d = P.dram("g_front", [128, 3, 8], F32, "ExternalInput")
    if has_back:
        win_d = P.dram("w_in", [D_MODEL, D_IN], F32, "ExternalInput")
        winv = win_d.h.rearrange("(c p) n -> p c n", p=128)
        gb_d = P.dram("g_back", [128, 8], F32, "ExternalInput")
        pT_d = P.dram("pT", [D_IN, T], F32, "ExternalOutput")
    if has_front:
        xo_d = P.dram("xoT", [D_MODEL, T], F32, "ExternalOutput")
        xov = xo_d.h.rearrange("(c p) t -> p c t", p=128)

    x = P.sbuf("x", [128, 8, HT], F32)
    hb = P.sbuf("hb", [128, 8, HT], BF16)
    sq = P.sbuf("sq", [128, 8, 512], BF16)
    rstd = P.sbuf("rstd", [128, 512], F32)
    ones = P.sbuf("ones", [128, 128], BF16)
    epst = P.sbuf("epst", [128, 1], F32)
    slabs = [P.sbuf(f"slab{i}", [128, 8, 256], BF16) for i in range(4)]
    if has_front:
        z = P.sbuf("z", [128, 8, HT], F32)
        act = P.sbuf("act", [128, 22, HT], BF16)
        wfo_s = [P.sbuf(f"wfo{i}", [128, 22, 128], BF16) for i in range(2)]
        sil = [P.sbuf(f"sil{i}", [128, 512], F32) for i in range(2)]
        gf = P.sbuf("gf", [128, 3, 8], F32)
    if has_back:
        gbk = P.sbuf("gbk", [128, 8], F32)
        stage = [P.sbuf(f"stage{i}", [128, 512], F32) for i in range(4)]
    pss = [P.psum(f"ps{i}", [128, 512], F32) for i in range(6)]
    psn = [P.psum(f"psn{i}", [128, 512], F32) for i in range(2)]
    st = {"ps": 0, "psn": 0, "slab": 0, "wfo": 0, "sil": 0, "stage": 0, "ev": 0}

    def rr(key, lst):
        i = st[key]
        st[key] = (i + 1) % len(lst)
        return lst[i]

    P.op("dve", lambda e: e.memset(ones[:], 1.0 / D_MODEL), writes=[ones.b])
    P.op("dve", lambda e: e.memset(epst[:], EPS), writes=[epst.b])
    if has_front:
        P.op("sp", lambda e: e.dma_start(out=gf[:], in_=g_d[:, :, :]), writes=[gf.b], reads=[g_d.b], dma_sem="d_gf")
    if has_back:
        P.op("sp", lambda e: e.dma_start(out=gbk[:], in_=gb_d[:, :]), writes=[gbk.b], reads=[gb_d.b], dma_sem="d_gbk")

    def load_slab(view, c0, ncols, src_b):
        s = rr("slab", slabs)
        P.op("pool", lambda e: e.dma_start(out=s[:, :, 0:ncols], in_=view[:, :, c0:c0 + ncols]),
             reads=[src_b], writes=[s.b], dma_sem="d_" + s.b.name)
        return s

    def evac(dst_ap, dst_b, ps):
        st["ev"] ^= 1
        if st["ev"]:
            P.op("act", lambda e: e.activation(dst_ap, ps[:], AF.Copy), reads=[ps.b], writes=[dst_b])
        else:
            P.op("dve", lambda e: e.tensor_copy(dst_ap, ps[:]), reads=[ps.b], writes=[dst_b])

    def rms_rstd(src, blk):
        tsl = slice(blk * 512, (blk + 1) * 512)
        P.op("act", lambda e: e.activation(sq[:], src[:, :, tsl], AF.Square), reads=[src.b], writes=[sq.b])
        ps = rr("psn", psn)
        for ci in range(8):
            P.op("pe", lambda e, ci=ci: e.matmul(ps[:], ones[:], sq[:, ci, :], start=(ci == 0), stop=(ci == 7)),
                 reads=[ones.b, sq.b], writes=[ps.b])
        P.op("act", lambda e: e.activation(rstd[:], ps[:], AF.Sqrt, bias=epst[:], scale=1.0),
             reads=[ps.b, epst.b], writes=[rstd.b])
        P.op("dve", lambda e: e.reciprocal(rstd[:], rstd[:]), reads=[rstd.b], writes=[rstd.b])

    def pre_norm(g_ap_fn, g_b, blk):
        tsl = slice(blk * 512, (blk + 1) * 512)
        rms_rstd(x, blk)
        for ci in range(8):
            P.op("dve", lambda e, ci=ci: e.scalar_tensor_tensor(hb[:, ci, tsl], x[:, ci, tsl], g_ap_fn(ci), rstd[:],
                                                                ALU.mult, ALU.mult),
                 reads=[x.b, rstd.b, g_b], writes=[hb.b])

    def post_norm_add(g_ap_fn, g_b, blk):
        tsl = slice(blk * 512, (blk + 1) * 512)
        rms_rstd(z, blk)
        for ci in range(8):
            P.op("dve", lambda e, ci=ci: e.scalar_tensor_tensor(z[:, ci, tsl], z[:, ci, tsl], g_ap_fn(ci), rstd[:],
                                                                ALU.mult, ALU.mult),
                 reads=[z.b, rstd.b, g_b], writes=[z.b])
        P.op("dve", lambda e: e.tensor_tensor(x[:, :, tsl], x[:, :, tsl], z[:, :, tsl], ALU.add),
             reads=[x.b, z.b], writes=[x.b])

    for half in range(2):
        t0 = half * HT
        P.op("sp", lambda e, t0=t0: e.dma_start(out=x[:], in_=xv[:, :, t0:t0 + HT]),
             reads=[xT_d.b], writes=[x.b], dma_sem="d_x")
        if has_front:
            P.op("pool", lambda e, t0=t0: e.dma_start(out=act[:, 0:8, :], in_=yv[:, :, t0:t0 + HT]),
                 reads=[yT_d.b], writes=[act.b], dma_sem="d_act")
            for sl in range(4):
                s = load_slab(wov, sl * 256, 256, wo_d.b)
                for ccl in range(2):
                    cc = sl * 2 + ccl
                    for blk in range(NBLK):
                        tsl = slice(blk * 512, (blk + 1) * 512)
                        ps = rr("ps", pss)
                        for ci in range(8):
                            P.op("pe", lambda e, ci=ci, s=s, ccl=ccl, tsl=tsl, ps=ps: e.matmul(
                                ps[:], s[:, ci, ccl * 128:(ccl + 1) * 128], act[:, ci, tsl],
                                start=(ci == 0), stop=(ci == 7)), reads=[s.b, act.b], writes=[ps.b])
                        evac(z[:, cc, tsl], z.b, ps)
            for blk in range(NBLK):
                post_norm_add(lambda ci: gf[:, 0, ci:ci + 1], gf.b, blk)
            for blk in range(NBLK):
                pre_norm(lambda ci: gf[:, 1, ci:ci + 1], gf.b, blk)
            for sl in range(11):
                sg = load_slab(wfiv, sl * 256, 256, wfi_d.b)
                su = load_slab(wfiv, D_FF + sl * 256, 256, wfi_d.b)
                for fcl in range(2):
                    fc = sl * 2 + fcl
                    for blk in range(NBLK):
                        tsl = slice(blk * 512, (blk + 1) * 512)
                        pg = rr("ps", pss)
                        for ci in range(8):
                            P.op("pe", lambda e, ci=ci, sg=sg, fcl=fcl, tsl=tsl, pg=pg: e.matmul(
                                pg[:], sg[:, ci, fcl * 128:(fcl + 1) * 128], hb[:, ci, tsl],
                                start=(ci == 0), stop=(ci == 7)), reads=[sg.b, hb.b], writes=[pg.b])
                        pu = rr("ps", pss)
                        for ci in range(8):
                            P.op("pe", lambda e, ci=ci, su=su, fcl=fcl, tsl=tsl, pu=pu: e.matmul(
                                pu[:], su[:, ci, fcl * 128:(fcl + 1) * 128], hb[:, ci, tsl],
                                start=(ci == 0), stop=(ci == 7)), reads=[su.b, hb.b], writes=[pu.b])
                        sb = rr("sil", sil)
                        P.op("act", lambda e, sb=sb, pg=pg: e.activation(sb[:], pg[:], AF.Silu),
                             reads=[pg.b], writes=[sb.b])
                        P.op("dve", lambda e, sb=sb, pu=pu, fc=fc, tsl=tsl: e.tensor_tensor(
                            act[:, fc, tsl], sb[:], pu[:], ALU.mult), reads=[sb.b, pu.b], writes=[act.b])
            for cc in range(8):
                w = rr("wfo", wfo_s)
                P.op("pool", lambda e, w=w, cc=cc: e.dma_start(out=w[:], in_=wfov[:, :, cc * 128:(cc + 1) * 128]),
                     reads=[wfo_d.b], writes=[w.b], dma_sem="d_" + w.b.name)
                for blk in range(NBLK):
                    tsl = slice(blk * 512, (blk + 1) * 512)
                    ps = rr("ps", pss)
                    for fc in range(22):
                        P.op("pe", lambda e, fc=fc, w=w, tsl=tsl, ps=ps: e.matmul(
                            ps[:], w[:, fc, :], act[:, fc, tsl], start=(fc == 0), stop=(fc == 21)),
                            reads=[w.b, act.b], writes=[ps.b])
                    evac(z[:, cc, tsl], z.b, ps)
            for blk in range(NBLK):
                post_norm_add(lambda ci: gf[:, 2, ci:ci + 1], gf.b, blk)
        if has_back:
            for blk in range(NBLK):
                pre_norm(lambda ci: gbk[:, ci:ci + 1], gbk.b, blk)
            for sl in range(13):
                ncols = 256 if sl < 12 else 32
                s = load_slab(winv, sl * 256, ncols, win_d.b)
                for ccl in range((ncols + 127) // 128):
                    m = min(128, ncols - ccl * 128)
                    c0 = sl * 256 + ccl * 128
                    for blk in range(NBLK):
                        tsl = slice(blk * 512, (blk + 1) * 512)
                        ps = rr("ps", pss)
                        for ci in range(8):
                            P.op("pe", lambda e, ci=ci, s=s, ccl=ccl, m=m, tsl=tsl, ps=ps: e.matmul(
                                ps[0:m, :], s[:, ci, ccl * 128:ccl * 128 + m], hb[:, ci, tsl],
                                start=(ci == 0), stop=(ci == 7)), reads=[s.b, hb.b], writes=[ps.b])
                        sg_ = rr("stage", stage)
                        st["ev"] ^= 1
                        if st["ev"]:
                            P.op("act", lambda e, sg_=sg_, ps=ps, m=m: e.activation(sg_[0:m, :], ps[0:m, :], AF.Copy),
                                 reads=[ps.b], writes=[sg_.b])
                        else:
                            P.op("dve", lambda e, sg_=sg_, ps=ps, m=m: e.tensor_copy(sg_[0:m, :], ps[0:m, :]),
                                 reads=[ps.b], writes=[sg_.b])
                        P.op("sp", lambda e, sg_=sg_, m=m, c0=c0, t0=t0, blk=blk: e.dma_start(
                            out=pT_d[c0:c0 + m, t0 + blk * 512:t0 + (blk + 1) * 512], in_=sg_[0:m, :]),
                            reads=[sg_.b], writes=[pT_d.b], dma_sem="o_" + sg_.b.name)
        if has_front:
            P.op("sp", lambda e, t0=t0: e.dma_start(out=xov[:, :, t0:t0 + HT], in_=x[:]),
                 reads=[x.b], writes=[xo_d.b], dma_sem="o_x")
    P.finish()
    P.emit()
    return nc


_DENSE_CACHE = {}


def run_dense(has_front, has_back, in_maps):
    key = (has_front, has_back)
    nc = build_dense(has_front, has_back)
    res = run_bass_kernel_spmd(nc, in_maps, core_ids=list(range(NCORES)))
    return res.results


def gvec(g):
    return np.ascontiguousarray(g.reshape(8, 128).T)


def _consts(P, npart=64):
    one_t = P.sbuf("one_t", [128, 1], F32)
    eps_t = P.sbuf("eps_t", [128, 1], F32)
    P.op("dve", lambda e: e.memset(one_t[:], 1.0), writes=[one_t.b])
    P.op("dve", lambda e: e.memset(eps_t[:], EPS), writes=[eps_t.b])
    return one_t, eps_t


def emit_lru(P, L, d):
    TB = 1024
    NB = L // TB
    xp_d, gate_d, cw_d, wax_d, prm_d, y_d = d["xpad"], d["gate"], d["cw"], d["wax"], d["prm"], d["y"]

    one_t, eps_t = _consts(P)
    xp = P.sbuf("xp", [64, L + 3], F32)
    xc = P.sbuf("xc", [64, L], F32)
    xcb = P.sbuf("xcb", [64, L], BF16)
    hf = P.sbuf("hf", [64, L], F32)
    cw = P.sbuf("cw_s", [64, 4], F32)
    wax = P.sbuf("wax_s", [64, 256], F32)
    waxb = P.sbuf("waxb", [64, 256], BF16)
    prm = P.sbuf("prm_s", [64, 7], F32)
    sp_ = P.sbuf("sp_s", [64, 2], F32)
    s8 = P.sbuf("s8", [64, 2], F32)
    s16 = P.sbuf("s16", [64, 2], F32)
    carry = P.sbuf("carry", [64, 1], F32)
    r_s = [P.sbuf(f"r{i}", [64, TB], F32) for i in range(2)]
    i_s = [P.sbuf(f"i{i}", [64, TB], F32) for i in range(2)]
    a_s = [P.sbuf(f"a{i}", [64, TB], F32) for i in range(2)]
    a2s = [P.sbuf(f"a2{i}", [64, TB], F32) for i in range(2)]
    u_s = [P.sbuf(f"u{i}", [64, TB], F32) for i in range(2)]
    hbk = P.sbuf("hbk", [64, TB], F32)
    gts = [P.sbuf(f"gt{i}", [64, TB], F32) for i in range(2)]
    gls = [P.sbuf(f"gl{i}", [64, TB], F32) for i in range(2)]
    yb = [P.sbuf(f"yb{i}", [64, TB], F32) for i in range(2)]
    pss = [P.psum(f"ps{i}", [64, 512], F32) for i in range(4)]
    st = {"ps": 0}

    def nps():
        i = st["ps"]
        st["ps"] = (i + 1) % 4
        return pss[i]

    for t, d in ((xp, xp_d), (cw, cw_d), (wax, wax_d), (prm, prm_d)):
        P.op("sp", lambda e, t=t, d=d: e.dma_start(out=t[:], in_=d[:, :]), reads=[d.b], writes=[t.b],
             dma_sem="d_" + t.b.name)
    P.op("dve", lambda e: e.tensor_copy(waxb[:], wax[:]), reads=[wax.b], writes=[waxb.b])
    P.op("act", lambda e: e.activation(sp_[:], prm[:, 5:7], AF.Exp, scale=-1.0), reads=[prm.b], writes=[sp_.b])
    P.op("act", lambda e: e.activation(sp_[:], sp_[:], AF.Ln, bias=one_t[0:64, :]), reads=[sp_.b, one_t.b], writes=[sp_.b])
    P.op("dve", lambda e: e.tensor_scalar(s8[:], sp_[:], -8.0, None, ALU.mult), reads=[sp_.b], writes=[s8.b])
    P.op("dve", lambda e: e.tensor_scalar(s16[:], sp_[:], -16.0, None, ALU.mult), reads=[sp_.b], writes=[s16.b])
    for b in range(NB):
        sl = slice(b * TB, (b + 1) * TB)
        P.op("act", lambda e, sl=sl: e.activation(xc[:, sl], xp[:, sl], AF.Identity, bias=prm[:, 0:1], scale=cw[:, 0:1]),
             reads=[xp.b, prm.b, cw.b], writes=[xc.b])
        for j in range(1, 4):
            P.op("dve", lambda e, sl=sl, j=j, b=b: e.scalar_tensor_tensor(
                xc[:, sl], xp[:, b * TB + j:(b + 1) * TB + j], cw[:, j:j + 1], xc[:, sl], ALU.mult, ALU.add),
                reads=[xp.b, cw.b, xc.b], writes=[xc.b])
        P.op("pool", lambda e, sl=sl: e.tensor_copy(xcb[:, sl], xc[:, sl]), reads=[xc.b], writes=[xcb.b])

    for e_ in range(2):
        order = range(NB) if e_ == 0 else range(NB - 1, -1, -1)
        first = True
        for b in order:
            sl = slice(b * TB, (b + 1) * TB)
            r_, i_, a_, a2, u_, gl = r_s[b % 2], i_s[b % 2], a_s[b % 2], a2s[b % 2], u_s[b % 2], gls[b % 2]
            if e_ == 1:
                gt = gts[b % 2]
                P.op("sp", lambda e, gt=gt, sl=sl, r_=r_, i_=i_, a_=a_, a2=a2, u_=u_, gl=gl: e.dma_start(out=gt[:], in_=gate_d[:, sl]),
                     reads=[gate_d.b], writes=[gt.b], dma_sem="d_" + gt.b.name)
            for sb in range(TB // 512):
                c0 = b * TB + sb * 512
                pr = nps()
                P.op("pe", lambda e, pr=pr, c0=c0, e_=e_, r_=r_, i_=i_, a_=a_, a2=a2, u_=u_, gl=gl: e.matmul(pr[:], waxb[:, e_ * 64:(e_ + 1) * 64], xcb[:, c0:c0 + 512],
                                                            start=True, stop=True),
                     reads=[waxb.b, xcb.b], writes=[pr.b])
                pi = nps()
                P.op("pe", lambda e, pi=pi, c0=c0, e_=e_, r_=r_, i_=i_, a_=a_, a2=a2, u_=u_, gl=gl: e.matmul(pi[:], waxb[:, 128 + e_ * 64:128 + (e_ + 1) * 64],
                                                            xcb[:, c0:c0 + 512], start=True, stop=True),
                     reads=[waxb.b, xcb.b], writes=[pi.b])
                P.op("act", lambda e, pr=pr, sb=sb, e_=e_, r_=r_, i_=i_, a_=a_, a2=a2, u_=u_, gl=gl: e.activation(r_[:, sb * 512:(sb + 1) * 512], pr[:], AF.Sigmoid,
                                                                 bias=prm[:, 1 + e_:2 + e_]),
                     reads=[pr.b, prm.b], writes=[r_.b])
                P.op("act", lambda e, pi=pi, sb=sb, e_=e_, r_=r_, i_=i_, a_=a_, a2=a2, u_=u_, gl=gl: e.activation(i_[:, sb * 512:(sb + 1) * 512], pi[:], AF.Sigmoid,
                                                                 bias=prm[:, 3 + e_:4 + e_]),
                     reads=[pi.b, prm.b], writes=[i_.b])
            P.op("act", lambda e, e_=e_, r_=r_, i_=i_, a_=a_, a2=a2, u_=u_, gl=gl: e.activation(a_[:], r_[:], AF.Exp, scale=s8[:, e_:e_ + 1]),
                 reads=[r_.b, s8.b], writes=[a_.b])
            P.op("act", lambda e, e_=e_, r_=r_, i_=i_, a_=a_, a2=a2, u_=u_, gl=gl: e.activation(a2[:], r_[:], AF.Exp, scale=s16[:, e_:e_ + 1]),
                 reads=[r_.b, s16.b], writes=[a2.b])
            P.op("act", lambda e, r_=r_, i_=i_, a_=a_, a2=a2, u_=u_, gl=gl: e.activation(a2[:], a2[:], AF.Sqrt, bias=one_t[0:64, :], scale=-1.0),
                 reads=[a2.b, one_t.b], writes=[a2.b])
            P.op("dve", lambda e, sl=sl, r_=r_, i_=i_, a_=a_, a2=a2, u_=u_, gl=gl: e.tensor_tensor(u_[:], i_[:], xc[:, sl], ALU.mult),
                 reads=[i_.b, xc.b], writes=[u_.b])
            P.op("dve", lambda e, r_=r_, i_=i_, a_=a_, a2=a2, u_=u_, gl=gl: e.tensor_tensor(u_[:], u_[:], a2[:], ALU.mult), reads=[u_.b, a2.b], writes=[u_.b])
            if e_ == 0:
                init = 0.0 if first else hf[:, b * TB - 1:b * TB]
                P.op("dve", lambda e, sl=sl, init=init, r_=r_, i_=i_, a_=a_, a2=a2, u_=u_, gl=gl: e.tensor_tensor_scan(hf[:, sl], a_[:], u_[:], init, ALU.mult, ALU.add),
                     reads=[a_.b, u_.b, hf.b], writes=[hf.b])
            else:
                init = 0.0 if first else carry[:]
                P.op("dve", lambda e, init=init, r_=r_, i_=i_, a_=a_, a2=a2, u_=u_, gl=gl: e.tensor_tensor_scan(hbk[:, ::-1], a_[:, ::-1], u_[:, ::-1], init,
                                                                      ALU.mult, ALU.add),
                     reads=[a_.b, u_.b, carry.b], writes=[hbk.b])
                P.op("dve", lambda e, r_=r_, i_=i_, a_=a_, a2=a2, u_=u_, gl=gl: e.tensor_copy(carry[:], hbk[:, 0:1]), reads=[hbk.b], writes=[carry.b])
                P.op("act", lambda e, gt=gt, r_=r_, i_=i_, a_=a_, a2=a2, u_=u_, gl=gl: e.activation(gl[:], gt[:], AF.Gelu_apprx_tanh), reads=[gt.b], writes=[gl.b])
                yo = yb[b % 2]
                P.op("dve", lambda e, sl=sl, yo=yo, r_=r_, i_=i_, a_=a_, a2=a2, u_=u_, gl=gl: e.tensor_tensor(yo[:], hf[:, sl], hbk[:], ALU.add),
                     reads=[hf.b, hbk.b], writes=[yo.b])
                P.op("dve", lambda e, yo=yo, r_=r_, i_=i_, a_=a_, a2=a2, u_=u_, gl=gl: e.tensor_tensor(yo[:], yo[:], gl[:], ALU.mult), reads=[yo.b, gl.b], writes=[yo.b])
                P.op("sp", lambda e, sl=sl, yo=yo, r_=r_, i_=i_, a_=a_, a2=a2, u_=u_, gl=gl: e.dma_start(out=y_d[:, sl], in_=yo[:]), reads=[yo.b], writes=[y_d.b],
                     dma_sem="o_" + yo.b.name)
            first = False


def prep_lru(pT, b_, j, lp, L):
    c0 = 1824 + 64 * j
    xpad = np.zeros((64, L + 3), np.float32)
    xpad[:, 2:2 + L] = pT[c0:c0 + 64]
    g0 = 2080 + 64 * j
    hs = slice(64 * j, 64 * j + 64)
    wax = np.concatenate([lp["lru_w_a"][0, j], lp["lru_w_a"][1, j], lp["lru_w_x"][0, j], lp["lru_w_x"][1, j]], axis=1)
    prm = np.stack([lp["lru_conv_b"][hs], lp["lru_b_a"][0, hs], lp["lru_b_a"][1, hs], lp["lru_b_x"][0, hs],
                    lp["lru_b_x"][1, hs], lp["lru_lambda"][0, hs], lp["lru_lambda"][1, hs]], axis=1)
    return {"xpad": xpad, "gate": np.ascontiguousarray(pT[g0:g0 + 64]),
            "cw": np.ascontiguousarray(lp["lru_conv_w"][:, hs].T), "wax": np.ascontiguousarray(wax, dtype=np.float32),
            "prm": np.ascontiguousarray(prm, dtype=np.float32)}


def _attn_finalize(P, acc, sel, y_d, L, pds, tag):
    rds = [P.sbuf(f"rd{tag}{i}", [64, 512], F32) for i in range(2)]
    yos = [P.sbuf(f"yo{tag}{i}", [64, 512], F32) for i in range(2)]
    for blk in range(L // 512):
        cols = slice(blk * 512, (blk + 1) * 512)
        pd = pds[blk % 2]
        rd = rds[blk % 2]
        yo = yos[blk % 2]
        P.op("pe", lambda e, pd=pd, cols=cols: e.matmul(pd[:], sel[:], acc[:, cols], start=True, stop=True),
             reads=[sel.b, acc.b], writes=[pd.b])
        P.op("dve", lambda e, pd=pd, rd=rd: e.reciprocal(rd[:], pd[:]), reads=[pd.b], writes=[rd.b])
        P.op("dve", lambda e, rd=rd, yo=yo, cols=cols: e.tensor_tensor(yo[:], acc[0:64, cols], rd[:], ALU.mult),
             reads=[acc.b, rd.b], writes=[yo.b])
        P.op("sp", lambda e, yo=yo, cols=cols: e.dma_start(out=y_d[:, cols], in_=yo[:]), reads=[yo.b], writes=[y_d.b],
             dma_sem="o_" + yo.b.name)


def _make_sel(P):
    sel = P.sbuf("sel", [65, 64], F32)
    P.op("dve", lambda e: e.memset(sel[:], 0.0), writes=[sel.b])
    P.op("dve", lambda e: e.memset(sel[64:65, :], 1.0), writes=[sel.b])
    return sel


def _load_cast(P, dst, src_d, L, stg, nrows=64, blkw=2048, eng="pool"):
    for b in range(L // blkw):
        s = stg[b % len(stg)]
        sl = slice(b * blkw, (b + 1) * blkw)
        P.op("sp", lambda e, s=s, sl=sl: e.dma_start(out=s[0:nrows, 0:blkw], in_=src_d[:, sl]), reads=[src_d.b], writes=[s.b],
             dma_sem="d_" + s.b.name)
        P.op(eng, lambda e, s=s, sl=sl: e.tensor_copy(dst[:, sl], s[0:nrows, 0:blkw]), reads=[s.b], writes=[dst.b])


def emit_na(P, L, d):
    ntile = L // 128
    qT_d, kT_d, va_d, bias_d, mask_d, y_d = d["qT"], d["kT"], d["vaug"], d["biasg"], d["maskc"], d["y"]

    qb = P.sbuf("qb", [64, L], BF16)
    kb = P.sbuf("kb", [64, L], BF16)
    stg = [P.sbuf(f"stg{i}", [64, 2048], F32) for i in range(2)]
    vb = P.sbuf("vb", [128, ntile, 65], BF16)
    vst = [P.sbuf(f"vst{i}", [128, 16, 65], F32) for i in range(2)]
    Fm = P.sbuf("Fm", [128, 5, 640], BF16)
    bst = P.sbuf("bst", [128, 640], F32)
    mst = P.sbuf("mst", [128, 640], F32)
    acc = P.sbuf("acc", [65, L], F32)
    sel = _make_sel(P)
    pexp = [P.sbuf(f"pexp{i}", [128, 640], BF16) for i in range(2)]
    pms = [P.sbuf(f"pm{i}", [128, 640], BF16) for i in range(2)]
    pSs = [P.psum(f"pS{i}", [128, 1024], F32) for i in range(2)]
    pos = [P.psum(f"po{i}", [65, 512], F32) for i in range(2)]
    pds = [P.psum(f"pd{i}", [64, 512], F32) for i in range(2)]

    _load_cast(P, qb, qT_d, L, stg)
    _load_cast(P, kb, kT_d, L, stg)
    for g in range((ntile + 15) // 16):
        s = vst[g % 2]
        n = min(16, ntile - g * 16)
        P.op("sp", lambda e, s=s, g=g, n=n: e.dma_start(out=s[:, 0:n, :], in_=va_d[:, g * 16:g * 16 + n, :]),
             reads=[va_d.b], writes=[s.b], dma_sem="d_" + s.b.name)
        P.op("pool", lambda e, s=s, g=g, n=n: e.tensor_copy(vb[:, g * 16:g * 16 + n, :], s[:, 0:n, :]),
             reads=[s.b], writes=[vb.b])
    for fi in range(5):
        P.op("sp", lambda e, fi=fi: e.dma_start(out=bst[:], in_=bias_d[:, fi, :]), reads=[bias_d.b], writes=[bst.b], dma_sem="d_bst")
        P.op("sp", lambda e, fi=fi: e.dma_start(out=mst[:], in_=mask_d[:, fi, :]), reads=[mask_d.b], writes=[mst.b], dma_sem="d_mst")
        P.op("act", lambda e: e.activation(bst[:], bst[:], AF.Exp), reads=[bst.b], writes=[bst.b])
        P.op("dve", lambda e, fi=fi: e.tensor_tensor(Fm[:, fi, :], bst[:], mst[:], ALU.mult), reads=[bst.b, mst.b], writes=[Fm.b])

    def na_stage1(m):
        tb = min(max(m - 2, 0), ntile - 5)
        fi = 0 if m == 0 else 1 if m == 1 else 3 if m == ntile - 2 else 4 if m == ntile - 1 else 2
        pS = pSs[m % 2]
        for jt in range(5):
            P.op("pe", lambda e, pS=pS, jt=jt, tb=tb, m=m: e.matmul(
                pS[:, jt * 128:(jt + 1) * 128], kb[:, (tb + jt) * 128:(tb + jt + 1) * 128], qb[:, m * 128:(m + 1) * 128],
                start=True, stop=True), reads=[kb.b, qb.b], writes=[pS.b])
        pe_ = pexp[m % 2]
        P.op("act", lambda e, pe_=pe_, pS=pS: e.activation(pe_[:], pS[:, 0:640], AF.Exp, scale=0.125),
             reads=[pS.b], writes=[pe_.b])
        pm_ = pms[m % 2]
        P.op("dve", lambda e, pe_=pe_, pm_=pm_, fi=fi: e.tensor_tensor(pm_[:], pe_[:], Fm[:, fi, :], ALU.mult),
             reads=[pe_.b, Fm.b], writes=[pm_.b])
        return pm_, tb

    def na_stage2(m, pm_, tb):
        po = pos[(m // 4) % 2]
        for jt in range(5):
            P.op("pe", lambda e, po=po, jt=jt, tb=tb, m=m, pm_=pm_: e.matmul(
                po[:, (m % 4) * 128:(m % 4 + 1) * 128], vb[:, tb + jt, :], pm_[:, jt * 128:(jt + 1) * 128],
                start=(jt == 0), stop=(jt == 4)), reads=[vb.b, pm_.b], writes=[po.b])
        if m % 4 == 3:
            P.op("act", lambda e, po=po, m=m: e.activation(acc[:, (m - 3) * 128:(m + 1) * 128], po[:], AF.Copy),
                 reads=[po.b], writes=[acc.b])

    cur = na_stage1(0)
    for m in range(ntile):
        nxt = na_stage1(m + 1) if m + 1 < ntile else None
        na_stage2(m, *cur)
        cur = nxt
    _attn_finalize(P, acc, sel, y_d, L, pds, "n")


def na_tables(L):
    rows = L // 64
    ntile = L // 128
    ms = [0, 1, 2, ntile - 2, ntile - 1]
    dr_i = np.zeros((5, 128, 5, 128), np.int64)
    dc_i = np.zeros((5, 128, 5, 128), np.int64)
    mask = np.zeros((5, 128, 5, 128), np.float32)
    pk = np.arange(128)
    fq = np.arange(128)
    for vi, m in enumerate(ms):
        tb = min(max(m - 2, 0), ntile - 5)
        for jt in range(5):
            krow = 2 * (tb + jt) + pk // 64
            kc = pk % 64
            qrow = 2 * m + fq // 64
            qc = fq % 64
            rs = np.clip(qrow - 4, 0, rows - 8)
            row_ok = (krow[:, None] >= rs[None, :]) & (krow[:, None] < rs[None, :] + 8)
            cs = np.clip(qc - 8, 0, 48)
            col_ok = (kc[:, None] >= cs[None, :]) & (kc[:, None] < cs[None, :] + 16)
            dr = np.clip(krow[:, None] - qrow[None, :], -7, 7)
            dc = np.clip(kc[:, None] - qc[None, :], -15, 15)
            dr_i[vi, :, jt, :] = dr + 7
            dc_i[vi, :, jt, :] = dc + 15
            mask[vi, :, jt, :] = (row_ok & col_ok).astype(np.float32)
    return dr_i, dc_i, mask


_NA_TAB = {}


def prep_na(pT, b_, j, lp, L):
    if L not in _NA_TAB:
        _NA_TAB[L] = na_tables(L)
    dr_i, dc_i, mask = _NA_TAB[L]
    ntile = L // 128
    q0, k0, v0 = 1056 + 64 * j, 1312 + 64 * j, 1568 + 64 * j
    v = pT[v0:v0 + 64]
    vaug = np.ones((128, ntile, 65), np.float32)
    vaug[:, :, 0:64] = v.T.reshape(ntile, 128, 64).transpose(1, 0, 2)
    rpb = lp["na_rpb"][j]
    biasg = rpb[dr_i, dc_i]
    biasg = np.ascontiguousarray(biasg.transpose(1, 0, 2, 3).reshape(128, 5, 640), dtype=np.float32)
    maskc = np.ascontiguousarray(mask.transpose(1, 0, 2, 3).reshape(128, 5, 640))
    return {"qT": np.ascontiguousarray(pT[q0:q0 + 64]), "kT": np.ascontiguousarray(pT[k0:k0 + 64]),
            "vaug": vaug, "biasg": biasg, "maskc": maskc}


DIL_D = (1, 4, 16)
DIL_PAD = 1024


def _dil_tiles(L):
    idx = {}
    t = 0
    for d in DIL_D:
        n = L // d
        for r in range(d):
            for m in range(n // 128 + 1):
                idx[(d, r, m)] = t
                t += 1
    return idx, t


def emit_dil(P, L, d):
    tidx, NT = _dil_tiles(L)
    names = ("qT", "qsT", "kT", "ksT", "cos2", "sin2s")
    dd = {nm: d[nm] for nm in names}
    va_d, mk_d, y_d = d["vaug"], d["maskab"], d["y"]

    qh = P.sbuf("qh", [64, L], BF16)
    kh = P.sbuf("kh", [64, L + 2 * DIL_PAD], BF16)
    BW = 1024
    stg = {nm: P.sbuf("s_" + nm, [64, BW], F32) for nm in names}
    t1 = P.sbuf("t1", [64, BW], F32)
    t2 = P.sbuf("t2", [64, BW], F32)
    vb = P.sbuf("vb", [128, NT, 65], BF16)
    vst = [P.sbuf(f"vst{i}", [128, 16, 65], F32) for i in range(2)]
    mk = P.sbuf("mk", [128, 256], F32)
    acc = P.sbuf("acc", [65, L], F32)
    sel = _make_sel(P)
    pexp = [P.sbuf(f"pexp{i}", [128, 256], BF16) for i in range(2)]
    pms = [P.sbuf(f"pm{i}", [128, 256], BF16) for i in range(3)]
    pSs = [P.psum(f"pS{i}", [128, 512], F32) for i in range(3)]
    pos = [P.psum(f"po{i}", [65, 512], F32) for i in range(2)]
    pds = [P.psum(f"pd{i}", [64, 512], F32) for i in range(2)]

    P.op("sp", lambda e: e.dma_start(out=mk[:], in_=mk_d[:, :]), reads=[mk_d.b], writes=[mk.b], dma_sem="d_mk")
    P.op("dve", lambda e: e.memset(kh[:, 0:DIL_PAD], 0.0), writes=[kh.b])
    P.op("dve", lambda e: e.memset(kh[:, DIL_PAD + L:DIL_PAD + L + DIL_PAD], 0.0), writes=[kh.b])
    for g in range((NT + 15) // 16):
        s = vst[g % 2]
        n = min(16, NT - g * 16)
        P.op("sp", lambda e, s=s, g=g, n=n: e.dma_start(out=s[:, 0:n, :], in_=va_d[:, g * 16:g * 16 + n, :]),
             reads=[va_d.b], writes=[s.b], dma_sem="d_" + s.b.name)
        P.op("pool", lambda e, s=s, g=g, n=n: e.tensor_copy(vb[:, g * 16:g * 16 + n, :], s[:, 0:n, :]),
             reads=[s.b], writes=[vb.b])
    for b in range(L // BW):
        sl = slice(b * BW, (b + 1) * BW)
        for nm in names:
            P.op("sp", lambda e, nm=nm, sl=sl: e.dma_start(out=stg[nm][:], in_=dd[nm][:, sl]), reads=[dd[nm].b],
                 writes=[stg[nm].b], dma_sem="d_s_" + nm)
        for (a, s_, dst, off) in (("qT", "qsT", qh, 0), ("kT", "ksT", kh, DIL_PAD)):
            P.op("dve", lambda e, a=a: e.tensor_tensor(t1[:], stg[a][:], stg["cos2"][:], ALU.mult),
                 reads=[stg[a].b, stg["cos2"].b], writes=[t1.b])
            P.op("pool", lambda e, s_=s_: e.tensor_tensor(t2[:], stg[s_][:], stg["sin2s"][:], ALU.mult),
                 reads=[stg[s_].b, stg["sin2s"].b], writes=[t2.b])
            P.op("dve", lambda e, dst=dst, off=off, b=b: e.tensor_tensor(
                dst[:, off + b * BW:off + (b + 1) * BW], t1[:], t2[:], ALU.add), reads=[t1.b, t2.b], writes=[dst.b])

    tiles = []
    for d in DIL_D:
        nq = (L // d) // 128
        for r in range(d):
            for m in range(nq + 1):
                tiles.append((d, r, m, nq))
    cnt = {"po": 0}
    state = {"po": None}

    def dil_stage1(i):
        d, r, m, nq = tiles[i]
        c_lo = 128 if m == 0 else 0
        c_hi = 128 if m == nq else 256
        i0_ = 128 * (m - 1) + c_lo
        cntq = c_hi - c_lo
        ks = DIL_PAD + r + d * (128 * m - 64)
        qs = r + d * i0_
        pS = pSs[i % 3]
        pe_ = pexp[i % 2]
        pm_ = pms[i % 3]
        P.op("pe", lambda e, pS=pS, ks=ks, qs=qs, d=d, cntq=cntq, c_lo=c_lo, c_hi=c_hi: e.matmul(
            pS[:, c_lo:c_hi], kh[:, ks:ks + 127 * d + 1:d], qh[:, qs:qs + (cntq - 1) * d + 1:d], start=True, stop=True),
            reads=[kh.b, qh.b], writes=[pS.b])
        P.op("act", lambda e, pe_=pe_, pS=pS, c_lo=c_lo, c_hi=c_hi: e.activation(
            pe_[:, c_lo:c_hi], pS[:, c_lo:c_hi], AF.Exp, scale=0.125), reads=[pS.b], writes=[pe_.b])
        P.op("dve", lambda e, pe_=pe_, pm_=pm_, c_lo=c_lo, c_hi=c_hi: e.tensor_tensor(
            pm_[:, c_lo:c_hi], pe_[:, c_lo:c_hi], mk[:, c_lo:c_hi], ALU.mult), reads=[pe_.b, mk.b], writes=[pm_.b])

    def dil_stage2(i):
        d, r, m, nq = tiles[i]
        if m == 0:
            return
        prev = pms[(i - 1) % 3]
        pm_ = pms[i % 3]
        mq = m - 1
        if mq % 4 == 0:
            state["po"] = pos[cnt["po"] % 2]
            cnt["po"] += 1
        po = state["po"]
        osl = slice((mq % 4) * 128, (mq % 4 + 1) * 128)
        P.op("pe", lambda e, po=po, osl=osl, prev=prev, tA=tidx[(d, r, mq)]: e.matmul(
            po[:, osl], vb[:, tA, :], prev[:, 128:256], start=True, stop=False),
            reads=[vb.b, prev.b], writes=[po.b])
        P.op("pe", lambda e, po=po, osl=osl, pm_=pm_, tB=tidx[(d, r, m)]: e.matmul(
            po[:, osl], vb[:, tB, :], pm_[:, 0:128], start=False, stop=True),
            reads=[vb.b, pm_.b], writes=[po.b])
        if mq % 4 == 3 or mq == nq - 1:
            mq0 = mq - (mq % 4)
            w = (mq % 4 + 1) * 128
            a0 = r + d * 128 * mq0
            if d == 1:
                P.op("act", lambda e, po=po, a0=a0, w=w: e.activation(acc[:, a0:a0 + w], po[:, 0:w], AF.Copy),
                     reads=[po.b], writes=[acc.b])
            else:
                P.op("dve", lambda e, po=po, a0=a0, w=w, d=d: e.tensor_tensor(
                    acc[:, a0:a0 + (w - 1) * d + 1:d], acc[:, a0:a0 + (w - 1) * d + 1:d], po[:, 0:w], ALU.add),
                    reads=[po.b, acc.b], writes=[acc.b])

    dil_stage1(0)
    for i in range(len(tiles)):
        if i + 1 < len(tiles):
            dil_stage1(i + 1)
        dil_stage2(i)
    _attn_finalize(P, acc, sel, y_d, L, pds, "d")


_ROPE = {}


def rope_tables(L):
    if L not in _ROPE:
        pos = np.arange(L, dtype=np.float32)
        inv_freq = (np.float32(10000.0) ** (-np.arange(0, 64, 2, dtype=np.float32) / np.float32(64))).astype(np.float32)
        ang = (pos[:, None] * inv_freq[None, :]).astype(np.float32)
        c = np.cos(ang).astype(np.float32).T
        s = np.sin(ang).astype(np.float32).T
        _ROPE[L] = (np.ascontiguousarray(np.concatenate([c, c], 0)), np.ascontiguousarray(np.concatenate([-s, s], 0)))
    return _ROPE[L]


_DIL_MASK = None


def prep_dil(pT, b_, j, lp, L):
    tidx, NT = _dil_tiles(L)
    q0, k0, v0 = 2336 + 64 * j, 2592 + 64 * j, 2848 + 64 * j
    q = pT[q0:q0 + 64]
    k = pT[k0:k0 + 64]
    v = pT[v0:v0 + 64]
    cos2, sin2s = rope_tables(L)
    vT = np.ascontiguousarray(v.T)
    vaug = np.zeros((128, NT, 65), np.float32)
    pk = np.arange(128)
    for d in DIL_D:
        n = L // d
        for r in range(d):
            for m in range(n // 128 + 1):
                i = 128 * m - 64 + pk
                ok = (i >= 0) & (i < n)
                tok = r + d * i[ok]
                t = tidx[(d, r, m)]
                vaug[ok, t, 0:64] = vT[tok]
                vaug[ok, t, 64] = 1.0
    fq = np.arange(128)
    mb = (pk[:, None] <= fq[None, :]).astype(np.float32)
    ma = (pk[:, None] >= fq[None, :]).astype(np.float32)
    return {"qT": np.ascontiguousarray(q), "qsT": np.ascontiguousarray(np.concatenate([q[32:], q[:32]], 0)),
            "kT": np.ascontiguousarray(k), "ksT": np.ascontiguousarray(np.concatenate([k[32:], k[:32]], 0)),
            "cos2": cos2, "sin2s": sin2s, "vaug": vaug, "maskab": np.ascontiguousarray(np.concatenate([mb, ma], 1))}


def emit_gla(P, L, d, debug=False):
    n = L // 128
    qT_d, kT_d, gT_d, vt_d = d["qT"], d["kT"], d["gT"], d["vtok"]
    z_d = [d["z0T"], d["z1T"]]
    wg_d, prm_d, cst_d, rst_d, y_d = d["wg"], d["prm"], d["cst"], d["resetm"], d["y"]

    one_t, eps_t = _consts(P)
    cs = P.sbuf("cs", [64, L], F32)
    csm = P.sbuf("csm", [64, n, 1], F32)
    qt = P.sbuf("qt", [64, L], BF16)
    kt = P.sbuf("kt", [64, L], BF16)
    acc = P.sbuf("acc", [64, L], F32)
    vb = P.sbuf("vb", [128, n, 64], BF16)
    ktok = P.sbuf("ktok", [128, n, 64], BF16)
    vst = [P.sbuf(f"vst{i}", [128, 16, 64], F32) for i in range(2)]
    BW = 1024
    stgq = [P.sbuf(f"stgq{i}", [64, BW], F32) for i in range(2)]
    stgk = [P.sbuf(f"stgk{i}", [64, BW], F32) for i in range(2)]
    ep = P.sbuf("ep", [64, BW], F32)
    em = P.sbuf("em", [64, BW], F32)
    zs = [P.sbuf(f"zs{i}", [16, 512], F32) for i in range(2)]
    wg = P.sbuf("wg_s", [16, 128], F32)
    prm = P.sbuf("prm_s", [64, 3], F32)
    nbg = P.sbuf("nbg", [64, 2], F32)
    cst = P.sbuf("cst_s", [128, 3, 128], F32)
    ident = P.sbuf("ident", [64, 64], BF16)
    rst = P.sbuf("rst", [64, 512], F32)
    tA = P.sbuf("tA", [64, 512], F32)
    tB = P.sbuf("tB", [64, 512], F32)
    dmid = P.sbuf("dmid", [64, n], F32)
    dlast = P.sbuf("dlast", [64, n], F32)
    dkv = P.sbuf("dkv", [64, n], F32)
    S = P.sbuf("S", [64, 64], F32)
    Sts = [P.sbuf(f"St{i}", [64, 64], BF16) for i in range(2)]
    tkv = P.sbuf("tkv", [64, 64], F32)
    Asb = [P.sbuf(f"Asb{i}", [128, 128], BF16) for i in range(2)]
    ones64 = P.sbuf("ones64", [64, 64], F32)
    gst = [P.sbuf(f"gst{i}", [64, 512], F32) for i in range(2)]
    yos = [P.sbuf(f"yo{i}", [64, 512], F32) for i in range(2)]
    rs = P.sbuf("rs", [64, 512], F32)
    pA = [P.psum(f"pA{i}", [128, 512], F32) for i in range(2)]
    pT_ = [P.psum(f"pT{i}", [128, 4, 64], BF16) for i in range(2)]
    pKV = [P.psum(f"pKV{i}", [64, 64], F32) for i in range(2)]
    pO = [P.psum(f"pO{i}", [64, 512], F32) for i in range(2)]

    for t, d in ((wg, wg_d), (prm, prm_d), (cst, cst_d), (rst, rst_d)):
        P.op("sp", lambda e, t=t, d=d: e.dma_start(out=t[:], in_=d.h), reads=[d.b], writes=[t.b], dma_sem="d_" + t.b.name)
    P.op("dve", lambda e: e.tensor_copy(ident[:], cst[0:64, 2, 0:64]), reads=[cst.b], writes=[ident.b])
    P.op("dve", lambda e: e.tensor_scalar(nbg[:], prm[:, 0:2], -1.0, None, ALU.mult), reads=[prm.b], writes=[nbg.b])
    P.op("dve", lambda e: e.memset(ones64[:], 1.0 / 64.0), writes=[ones64.b])
    for g in range((n + 15) // 16):
        s = vst[g % 2]
        m_ = min(16, n - g * 16)
        P.op("sp", lambda e, s=s, g=g, m_=m_: e.dma_start(out=s[:, 0:m_, :], in_=vt_d[:, g * 16:g * 16 + m_, :]),
             reads=[vt_d.b], writes=[s.b], dma_sem="d_" + s.b.name)
        P.op("pool", lambda e, s=s, g=g, m_=m_: e.tensor_copy(vb[:, g * 16:g * 16 + m_, :], s[:, 0:m_, :]),
             reads=[s.b], writes=[vb.b])

    csv = cs[:].rearrange("p (n c) -> p n c", c=128)
    for e_ in range(2):
        mid = 63 if e_ == 0 else 64
        last = 127 if e_ == 0 else 0
        for blk in range(L // 512):
            cols = slice(blk * 512, (blk + 1) * 512)
            z_ = zs[blk % 2]
            P.op("sp", lambda e, z_=z_, cols=cols, e_=e_: e.dma_start(out=z_[:], in_=z_d[e_][:, cols]),
                 reads=[z_d[e_].b], writes=[z_.b], dma_sem="d_" + z_.b.name)
            pl = pA[blk % 2]
            P.op("pe", lambda e, pl=pl, z_=z_, e_=e_: e.matmul(pl[0:64, :], wg[:, e_ * 64:(e_ + 1) * 64], z_[:],
                                                             start=True, stop=True),
                 reads=[wg.b, z_.b], writes=[pl.b])
            P.op("act", lambda e, pl=pl, e_=e_: e.activation(tA[:], pl[0:64, :], AF.Exp, bias=nbg[:, e_:e_ + 1], scale=-1.0),
                 reads=[pl.b, nbg.b], writes=[tA.b])
            P.op("act", lambda e: e.activation(tB[:], tA[:], AF.Ln, bias=one_t[0:64, :]),
                 reads=[tA.b, one_t.b], writes=[tB.b])
            if e_ == 0:
                P.op("dve", lambda e, cols=cols: e.tensor_tensor_scan(cs[:, cols], rst[:], tB[:], 0.0, ALU.mult, ALU.add),
                     reads=[rst.b, tB.b], writes=[cs.b])
            else:
                P.op("dve", lambda e, cols=cols: e.tensor_tensor_scan(cs[:, cols][:, ::-1], rst[:], tB[:, ::-1], 0.0,
                                                                      ALU.mult, ALU.add),
                     reads=[rst.b, tB.b], writes=[cs.b])
        P.op("act", lambda e, mid=mid: e.activation(dmid[:], cs[:, mid::128], AF.Exp, scale=-1.0 / 16),
             reads=[cs.b], writes=[dmid.b])
        P.op("act", lambda e, last=last: e.activation(dlast[:], cs[:, last::128], AF.Exp, scale=-1.0 / 16),
             reads=[cs.b], writes=[dlast.b])
        P.op("dve", lambda e, mid=mid: e.tensor_copy(csm[:], csv[:, :, mid:mid + 1]), reads=[cs.b], writes=[csm.b])
        P.op("dve", lambda e: e.tensor_tensor(csv, csv, csm[:].to_broadcast([64, n, 128]), ALU.subtract),
             reads=[cs.b, csm.b], writes=[cs.b])
        P.op("act", lambda e, last=last: e.activation(dkv[:], cs[:, last::128], AF.Exp, scale=-1.0 / 16),
             reads=[cs.b], writes=[dkv.b])
        for b in range(L // BW):
            sl = slice(b * BW, (b + 1) * BW)
            sq_, sk_ = stgq[b % 2], stgk[b % 2]
            P.op("sp", lambda e, sq_=sq_, sl=sl: e.dma_start(out=sq_[:], in_=qT_d[:, sl]), reads=[qT_d.b], writes=[sq_.b],
                 dma_sem="d_" + sq_.b.name)
            P.op("sp", lambda e, sk_=sk_, sl=sl: e.dma_start(out=sk_[:], in_=kT_d[:, sl]), reads=[kT_d.b], writes=[sk_.b],
                 dma_sem="d_" + sk_.b.name)
            P.op("act", lambda e, sl=sl: e.activation(ep[:], cs[:, sl], AF.Exp, scale=-1.0 / 16), reads=[cs.b], writes=[ep.b])
            P.op("act", lambda e, sl=sl: e.activation(em[:], cs[:, sl], AF.Exp, scale=1.0 / 16), reads=[cs.b], writes=[em.b])
            P.op("dve", lambda e, sl=sl, sq_=sq_: e.scalar_tensor_tensor(qt[:, sl], sq_[:], 0.125, ep[:], ALU.mult, ALU.mult),
                 reads=[sq_.b, ep.b], writes=[qt.b])
            P.op("pool", lambda e, sl=sl, sk_=sk_: e.tensor_tensor(kt[:, sl], sk_[:], em[:], ALU.mult),
                 reads=[sk_.b, em.b], writes=[kt.b])
        for g in range(n // 4):
            pt = pT_[g % 2]
            for i in range(4):
                c = g * 4 + i
                P.op("pe", lambda e, pt=pt, i=i, c=c: e.transpose(pt[:, i, :], kt[:, c * 128:(c + 1) * 128], ident[:]),
                     reads=[kt.b, ident.b], writes=[pt.b])
            P.op("act", lambda e, pt=pt, g=g: e.activation(ktok[:, g * 4:(g + 1) * 4, :], pt[:], AF.Copy),
                 reads=[pt.b], writes=[ktok.b])
        P.op("dve", lambda e: e.memset(S[:], 0.0), writes=[S.b])
        P.op("dve", lambda e: e.memset(Sts[0][:], 0.0), writes=[Sts[0].b])
        order = list(range(n)) if e_ == 0 else list(range(n - 1, -1, -1))
        mslot = 0 if e_ == 0 else 1

        def emit_A(k):
            c = order[k]
            pa = pA[k % 2]
            P.op("pe", lambda e, pa=pa, c=c: e.matmul(pa[:, 0:128], kt[:, c * 128:(c + 1) * 128], qt[:, c * 128:(c + 1) * 128],
                                                      start=True, stop=True), reads=[kt.b, qt.b], writes=[pa.b])
            P.op("dve", lambda e, pa=pa, k=k, mslot=mslot: e.tensor_tensor(Asb[k % 2][:], pa[:, 0:128], cst[:, mslot, :], ALU.mult),
                 reads=[pa.b, cst.b], writes=[Asb[k % 2].b])

        emit_A(0)
        for k, c in enumerate(order):
            if k + 1 < n:
                emit_A(k + 1)
            pk = pKV[k % 2]
            P.op("pe", lambda e, pk=pk, c=c: e.matmul(pk[:], ktok[:, c, :], vb[:, c, :], start=True, stop=True),
                 reads=[ktok.b, vb.b], writes=[pk.b])
            po = pO[(c // 4) % 2]
            osl = slice((c % 4) * 128, (c % 4 + 1) * 128)
            St = Sts[k % 2]
            P.op("pe", lambda e, po=po, osl=osl, c=c, k=k: e.matmul(po[:, osl], vb[:, c, :], Asb[k % 2][:], start=True, stop=False),
                 reads=[vb.b, Asb[k % 2].b], writes=[po.b])
            P.op("pe", lambda e, po=po, osl=osl, c=c, St=St: e.matmul(po[:, osl], St[:], qt[:, c * 128:(c + 1) * 128],
                                                                      start=False, stop=True),
                 reads=[St.b, qt.b], writes=[po.b])
            if k + 1 < n:
                cn = order[k + 1]
                Sn = Sts[(k + 1) % 2]
                P.op("dve", lambda e, pk=pk, c=c: e.tensor_scalar(tkv[:], pk[:], dkv[:, c:c + 1], None, ALU.mult),
                     reads=[pk.b, dkv.b], writes=[tkv.b])
                P.op("dve", lambda e, c=c: e.scalar_tensor_tensor(S[:], S[:], dlast[:, c:c + 1], tkv[:], ALU.mult, ALU.add),
                     reads=[S.b, dlast.b, tkv.b], writes=[S.b])
                P.op("dve", lambda e, Sn=Sn, cn=cn: e.tensor_scalar(Sn[:], S[:], dmid[:, cn:cn + 1], None, ALU.mult),
                     reads=[S.b, dmid.b], writes=[Sn.b])
            done = (c % 4 == 3) if e_ == 0 else (c % 4 == 0)
            if done:
                g0 = (c // 4) * 512
                if e_ == 0:
                    P.op("act", lambda e, po=po, g0=g0: e.activation(acc[:, g0:g0 + 512], po[:], AF.Copy),
                         reads=[po.b], writes=[acc.b])
                else:
                    P.op("dve", lambda e, po=po, g0=g0: e.tensor_tensor(acc[:, g0:g0 + 512], acc[:, g0:g0 + 512], po[:], ALU.add),
                         reads=[po.b, acc.b], writes=[acc.b])
    if debug:
        dbg_acc = P.dram("dbg_acc", [64, L], F32, "ExternalOutput")
        dbg_cs = P.dram("dbg_cs", [64, L], F32, "ExternalOutput")
        dbg_d = P.dram("dbg_d", [64, 3, n], F32, "ExternalOutput")
        dbg_kt = P.dram("dbg_kt", [128, n, 64], F32, "ExternalOutput")
        ktf = P.sbuf("ktf", [128, n, 64], F32)
        P.op("dve", lambda e: e.tensor_copy(ktf[:], ktok[:]), reads=[ktok.b], writes=[ktf.b])
        P.op("sp", lambda e: e.dma_start(out=dbg_kt.h, in_=ktf[:]), reads=[ktf.b], writes=[dbg_kt.b], dma_sem="o_dbg")
        P.op("sp", lambda e: e.dma_start(out=dbg_acc[:, :], in_=acc[:]), reads=[acc.b], writes=[dbg_acc.b], dma_sem="o_dbg")
        P.op("sp", lambda e: e.dma_start(out=dbg_cs[:, :], in_=cs[:]), reads=[cs.b], writes=[dbg_cs.b], dma_sem="o_dbg")
        for i_, t_ in enumerate((dmid, dlast, dkv)):
            P.op("sp", lambda e, i_=i_, t_=t_: e.dma_start(out=dbg_d[:, i_, :], in_=t_[:]), reads=[t_.b], writes=[dbg_d.b], dma_sem="o_dbg")
    for blk in range(L // 512):
        cols = slice(blk * 512, (blk + 1) * 512)
        g_ = gst[blk % 2]
        yo = yos[blk % 2]
        P.op("sp", lambda e, g_=g_, cols=cols: e.dma_start(out=g_[:], in_=gT_d[:, cols]), reads=[gT_d.b], writes=[g_.b],
             dma_sem="d_" + g_.b.name)
        P.op("act", lambda e, cols=cols: e.activation(tA[:], acc[:, cols], AF.Square), reads=[acc.b], writes=[tA.b])
        pm_ = pA[blk % 2]
        P.op("pe", lambda e, pm_=pm_: e.matmul(pm_[0:64, :], ones64[:], tA[:], start=True, stop=True),
             reads=[ones64.b, tA.b], writes=[pm_.b])
        P.op("act", lambda e, pm_=pm_: e.activation(rs[:], pm_[0:64, :], AF.Sqrt, bias=eps_t[0:64, :]),
             reads=[pm_.b, eps_t.b], writes=[rs.b])
        P.op("dve", lambda e: e.reciprocal(rs[:], rs[:]), reads=[rs.b], writes=[rs.b])
        P.op("act", lambda e, g_=g_: e.activation(tB[:], g_[:], AF.Silu), reads=[g_.b], writes=[tB.b])
        P.op("dve", lambda e, yo=yo, cols=cols: e.scalar_tensor_tensor(yo[:], acc[:, cols], prm[:, 2:3], rs[:], ALU.mult, ALU.mult),
             reads=[acc.b, prm.b, rs.b], writes=[yo.b])
        P.op("dve", lambda e, yo=yo: e.tensor_tensor(yo[:], yo[:], tB[:], ALU.mult), reads=[yo.b, tB.b], writes=[yo.b])
        P.op("sp", lambda e, yo=yo, cols=cols: e.dma_start(out=y_d[:, cols], in_=yo[:]), reads=[yo.b], writes=[y_d.b],
             dma_sem="o_" + yo.b.name)


_GLA_CST = None


def gla_consts():
    global _GLA_CST
    if _GLA_CST is None:
        i = np.arange(128)
        maskf = (i[:, None] <= i[None, :]).astype(np.float32)
        maskb = (i[:, None] >= i[None, :]).astype(np.float32)
        cst = np.ascontiguousarray(np.stack([maskf, maskb, np.eye(128, dtype=np.float32)], axis=1))
        rst = np.ones((64, 512), np.float32)
        rst[:, ::128] = 0.0
        _GLA_CST = (cst, rst)
    return _GLA_CST


def prep_gla(pT, b_, j, lp, L):
    n = L // 128
    cst, rst = gla_consts()
    hs = slice(64 * j, 64 * j + 64)
    v = pT[512 + 64 * j:512 + 64 * j + 64]
    vtok = np.ascontiguousarray(v.T.reshape(n, 128, 64).transpose(1, 0, 2))
    wg = np.concatenate([lp["gla_w_gate"][0][:, hs], lp["gla_w_gate"][1][:, hs]], axis=1)
    prm = np.stack([lp["gla_b_gate"][0, hs], lp["gla_b_gate"][1, hs], lp["gla_norm"][hs]], axis=1)
    return {"qT": np.ascontiguousarray(pT[64 * j:64 * j + 64]), "kT": np.ascontiguousarray(pT[256 + 64 * j:256 + 64 * j + 64]),
            "gT": np.ascontiguousarray(pT[768 + 64 * j:768 + 64 * j + 64]), "vtok": vtok,
            "z0T": np.ascontiguousarray(pT[1024:1040]), "z1T": np.ascontiguousarray(pT[1040:1056]),
            "wg": np.ascontiguousarray(wg, dtype=np.float32), "prm": np.ascontiguousarray(prm, dtype=np.float32),
            "cst": cst, "resetm": rst}


def mixer_specs(L):
    n = L // 128
    _, NT = _dil_tiles(L)
    v = [64, L]
    return {
        "gla": {"qT": v, "kT": v, "gT": v, "vtok": [128, n, 64], "z0T": [16, L], "z1T": [16, L], "wg": [16, 128],
                "prm": [64, 3], "cst": [128, 3, 128], "resetm": [64, 512]},
        "na": {"qT": v, "kT": v, "vaug": [128, n, 65], "biasg": [128, 5, 640], "maskc": [128, 5, 640]},
        "lru": {"xpad": [64, L + 3], "gate": v, "cw": [64, 4], "wax": [64, 256], "prm": [64, 7]},
        "dil": {"qT": v, "qsT": v, "kT": v, "ksT": v, "cos2": v, "sin2s": v, "vaug": [128, NT, 65], "maskab": [128, 256]},
    }


EMITTERS = {"gla": emit_gla, "na": emit_na, "lru": emit_lru, "dil": emit_dil}
PREPS = {"gla": prep_gla, "na": prep_na, "lru": prep_lru, "dil": prep_dil}


def build_mixers(L, which=("gla", "na", "lru", "dil")):
    nc = bass.Bass("TRN2", target_bir_lowering=False)
    P = Prog(nc)
    specs = mixer_specs(L)
    ds = {}
    for m in which:
        ds[m] = {k: P.dram(f"{m}_{k}", shp, F32, "ExternalInput") for k, shp in specs[m].items()}
        ds[m]["y"] = P.dram(f"{m}_y", [64, L], F32, "ExternalOutput")
    mark = P.sb_off
    for i, m in enumerate(which):
        if i:
            P.phase_reset(mark)
        EMITTERS[m](P, L, ds[m])
    P.finish()
    P.emit()
    return nc


def prep_mixers(pT, b_, j, lp, L, which=("gla", "na", "lru", "dil")):
    out = {}
    for m in which:
        for k, v in PREPS[m](pT, b_, j, lp, L).items():
            out[f"{m}_{k}"] = v
    return out


_NC_CACHE = {}


def _get_nc(key, builder):
    return builder()


def _launch(nc, in_maps):
    return run_bass_kernel_spmd(nc, in_maps, core_ids=list(range(NCORES))).results


def kernel(x, mix_norm_pre, mix_norm_post, w_in, gla_w_gate, gla_b_gate, gla_norm, na_rpb,
           lru_conv_w, lru_conv_b, lru_w_a, lru_b_a, lru_w_x, lru_b_x, lru_lambda, w_out,
           ffn_norm_pre, ffn_norm_post, ffn_w_in, ffn_w_out):
    f32 = lambda a: np.ascontiguousarray(np.asarray(a), dtype=np.float32)
    x = f32(x)
    prm = dict(mix_norm_pre=f32(mix_norm_pre), mix_norm_post=f32(mix_norm_post), w_in=f32(w_in),
               gla_w_gate=f32(gla_w_gate), gla_b_gate=f32(gla_b_gate), gla_norm=f32(gla_norm), na_rpb=f32(na_rpb),
               lru_conv_w=f32(lru_conv_w), lru_conv_b=f32(lru_conv_b), lru_w_a=f32(lru_w_a), lru_b_a=f32(lru_b_a),
               lru_w_x=f32(lru_w_x), lru_b_x=f32(lru_b_x), lru_lambda=f32(lru_lambda), w_out=f32(w_out),
               ffn_norm_pre=f32(ffn_norm_pre), ffn_norm_post=f32(ffn_norm_post), ffn_w_in=f32(ffn_w_in),
               ffn_w_out=f32(ffn_w_out))
    L = SEQ
    xf = x.reshape(BATCH * SEQ, D_MODEL)
    xT = [np.ascontiguousarray(xf[c * TPC:(c + 1) * TPC].T) for c in range(NCORES)]
    yT = None
    for l in range(DEPTH + 1):
        has_front = l > 0
        has_back = l < DEPTH
        in_maps = []
        for c in range(NCORES):
            m = {"xT": xT[c]}
            if has_front:
                lf = l - 1
                m.update({"yT": yT[c], "w_out": prm["w_out"][lf], "ffn_w_in": prm["ffn_w_in"][lf],
                          "ffn_w_out": prm["ffn_w_out"][lf],
                          "g_front": np.ascontiguousarray(np.stack([gvec(prm["mix_norm_post"][lf]), gvec(prm["ffn_norm_pre"][lf]),
                                                                    gvec(prm["ffn_norm_post"][lf])], axis=1))})
            if has_back:
                m.update({"w_in": prm["w_in"][l], "g_back": gvec(prm["mix_norm_pre"][l])})
            in_maps.append(m)
        nc = _get_nc(("dense", has_front, has_back), lambda: build_dense(has_front, has_back))
        res = _launch(nc, in_maps)
        if has_front:
            xT = [res[c]["xoT"] for c in range(NCORES)]
        if not has_back:
            break
        pT = [np.ascontiguousarray(np.concatenate([res[b_ * 4 + i]["pT"] for i in range(4)], axis=1)) for b_ in range(BATCH)]
        lp = {k: v[l] for k, v in prm.items()}
        nc = _get_nc(("mix",), lambda: build_mixers(L))
        mres = _launch(nc, [prep_mixers(pT[c // 4], c // 4, c % 4, lp, L) for c in range(NCORES)])
        yT = []
        for c in range(NCORES):
            b_, i = c // 4, c % 4
            rows = [mres[b_ * 4 + j][f"{m}_y"][:, i * TPC:(i + 1) * TPC] for m in ("gla", "na", "lru", "dil") for j in range(4)]
            yT.append(np.ascontiguousarray(np.concatenate(rows, axis=0)))
    out = np.concatenate([xT[c].T for c in range(NCORES)], axis=0).reshape(BATCH, SEQ, D_MODEL)
    return np.ascontiguousarray(out, dtype=np.float32)
```

```python
import contextlib
import numpy as np
import concourse.bass as bass
import concourse.mybir as mybir
from concourse.bass_utils import run_bass_kernel_spmd

F32 = mybir.dt.float32
BF16 = mybir.dt.bfloat16
AF = mybir.ActivationFunctionType
ALU = mybir.AluOpType

D_MODEL = 1024
SEQ = 8192
BATCH = 2
DEPTH = 4
D_IN = 3104
D_FF = 2816
EPS = 1e-6
NCORES = 8
TPC = 2048
SBUF_BASE = 16512
SBUF_LIMIT = 226000

ENGS = ("sp", "act", "pe", "dve", "pool")


class Buf:
    __slots__ = ("name", "last_w", "w_dma", "readers")

    def __init__(self, name):
        self.name = name
        self.last_w = None
        self.w_dma = False
        self.readers = {}


class Ten:
    def __init__(self, h, b):
        self.h = h
        self.b = b

    def __getitem__(self, idx):
        return self.h[idx]


class Prog:
    def __init__(self, nc):
        self.nc = nc
        self.stack = contextlib.ExitStack()
        self.ops = {e: [] for e in ENGS}
        self.waited = {e: {} for e in ENGS}
        self.sems = {}
        self.count = {}
        self.dma_sems = set()
        self.sb_off = SBUF_BASE
        self.ps_bank = 0
        self.psall = None
        self.phase = 0

    def sem(self, name):
        if name not in self.sems:
            self.sems[name] = self.stack.enter_context(self.nc.semaphore(name))
            self.count[name] = 0
        return self.sems[name]

    def sbuf(self, name, shape, dtype):
        nbytes = int(np.prod(shape[1:])) * (4 if dtype == F32 else 2)
        off = (self.sb_off + 63) // 64 * 64
        uname = f"p{self.phase}_{name}"
        h = self.nc.alloc_sbuf_tensor_at(uname, list(shape), dtype, offset=off)
        self.sb_off = off + nbytes
        assert self.sb_off <= SBUF_LIMIT, (uname, self.sb_off)
        return Ten(h, Buf(uname))

    def psum(self, name, shape, dtype=F32):
        if self.psall is None:
            self.psall = self.nc.alloc_psum_tensor("psall", [128, 4096], F32)
        e32 = int(np.prod(shape[1:])) * (4 if dtype == F32 else 2) // 4
        nb = (e32 + 511) // 512
        st = self.ps_bank * 512
        self.ps_bank += nb
        assert self.ps_bank <= 8, name
        v = self.psall[0:shape[0], st:st + e32]
        if dtype != F32:
            v = v.bitcast(dtype)
        if len(shape) == 3:
            v = v.rearrange("p (a b) -> p a b", b=shape[2])
        return Ten(v, Buf(f"p{self.phase}_{name}"))

    def barrier(self):
        for eng in ENGS:
            waits = []
            for sname, cnt in sorted(self.count.items()):
                if cnt == 0 or sname == "c_" + eng:
                    continue
                if self.waited[eng].get(sname, 0) >= cnt:
                    continue
                waits.append((sname, cnt))
                self.waited[eng][sname] = cnt
            if waits:
                self.ops[eng].append((waits, None, None, 0, True))

    def phase_reset(self, sb_mark):
        self.barrier()
        self.sb_off = sb_mark
        self.ps_bank = 0
        self.phase += 1

    def dram(self, name, shape, dtype, kind):
        h = self.nc.dram_tensor(name, list(shape), dtype, kind=kind).ap()
        return Ten(h, Buf(name))

    def op(self, eng, fn, reads=(), writes=(), dma_sem=None):
        waits = {}
        own = "c_" + eng

        def need(tok, is_dma):
            if tok is None:
                return
            s, v = tok
            if eng == "pe" and s == own:
                return
            if s in self.dma_sems:
                v = max(v, self.count[s])
            if self.waited[eng].get(s, 0) >= v:
                return
            if waits.get(s, 0) < v:
                waits[s] = v

        for b in reads:
            need(b.last_w, b.w_dma)
        for b in writes:
            grouped = (dma_sem is not None and b.w_dma and not b.readers
                       and b.last_w is not None and b.last_w[0] == dma_sem)
            if not grouped:
                need(b.last_w, b.w_dma)
            for s, v in b.readers.items():
                need((s, v), False)
        if dma_sem is not None:
            self.sem(dma_sem)
            self.dma_sems.add(dma_sem)
            sname, inc = dma_sem, 16
        else:
            self.sem(own)
            sname, inc = own, 1
        self.count[sname] += inc
        tok = (sname, self.count[sname])
        for s, v in waits.items():
            self.sem(s)
            self.waited[eng][s] = v
        for b in reads:
            if b.readers.get(tok[0], 0) < tok[1]:
                b.readers[tok[0]] = tok[1]
        for b in writes:
            b.last_w = tok
            b.w_dma = dma_sem is not None
            b.readers = {}
        self.ops[eng].append((sorted(waits.items()), fn, sname, inc, dma_sem is not None))
        return tok

    def finish(self):
        waits = [(s, self.count[s]) for s in sorted(self.dma_sems)]
        self.ops["sp"].append((waits, None, None, 0, True))

    def emit(self):
        nc = self.nc
        sems = self.sems

        def replay(e, name):
            for waits, fn, sname, inc, is_dma in self.ops[name]:
                if fn is None:
                    for s, v in waits:
                        e.wait_ge(sems[s], v)
                    continue
                if is_dma or not waits:
                    for s, v in waits:
                        e.wait_ge(sems[s], v)
                    ins = fn(e)
                else:
                    for s, v in waits[:-1]:
                        e.wait_ge(sems[s], v)
                    ins = fn(e)
                    ins._wait_ge(sems[waits[-1][0]], waits[-1][1])
                ins.then_inc(sems[sname], inc)

        with nc.Block() as block:
            @block.sync
            def _(e):
                replay(e, "sp")

            @block.scalar
            def _(e):
                replay(e, "act")

            @block.tensor
            def _(e):
                replay(e, "pe")

            @block.vector
            def _(e):
                replay(e, "dve")

            @block.gpsimd
            def _(e):
                replay(e, "pool")
        self.stack.close()


def build_dense(has_front, has_back):
    nc = bass.Bass("TRN2", target_bir_lowering=False)
    P = Prog(nc)
    T = TPC
    HT = 1024
    NBLK = HT // 512

    xT_d = P.dram("xT", [D_MODEL, T], F32, "ExternalInput")
    xv = xT_d.h.rearrange("(c p) t -> p c t", p=128)
    if has_front:
        yT_d = P.dram("yT", [D_MODEL, T], F32, "ExternalInput")
        yv = yT_d.h.rearrange("(c p) t -> p c t", p=128)
        wo_d = P.dram("w_out", [D_MODEL, D_MODEL], F32, "ExternalInput")
        wov = wo_d.h.rearrange("(c p) n -> p c n", p=128)
        wfi_d = P.dram("ffn_w_in", [D_MODEL, 2 * D_FF], F32, "ExternalInput")
        wfiv = wfi_d.h.rearrange("(c p) n -> p c n", p=128)
        wfo_d = P.dram("ffn_w_out", [D_FF, D_MODEL], F32, "ExternalInput")
        wfov = wfo_d.h.rearrange("(c p) n -> p c n", p=128)
        g_d = P.dram("g_front", [128, 3, 8], F32, "ExternalInput")
    if has_back:
        win_d = P.dram("w_in", [D_MODEL, D_IN], F32, "ExternalInput")
        winv = win_d.h.rearrange("(c p) n -> p c n", p=128)
        gb_d = P.dram("g_back", [128, 8], F32, "ExternalInput")
        pT_d = P.dram("pT", [D_IN, T], F32, "ExternalOutput")
    if has_front:
        xo_d = P.dram("xoT", [D_MODEL, T], F32, "ExternalOutput")
        xov = xo_d.h.rearrange("(c p) t -> p c t", p=128)

    x = P.sbuf("x", [128, 8, HT], F32)
    hb = P.sbuf("hb", [128, 8, HT], BF16)
    sq = P.sbuf("sq", [128, 8, 512], BF16)
    rstd = P.sbuf("rstd", [128, 512], F32)
    ones = P.sbuf("ones", [128, 128], BF16)
    epst = P.sbuf("epst", [128, 1], F32)
    slabs = [P.sbuf(f"slab{i}", [128, 8, 512], BF16) for i in range(4)]
    if has_front:
        z = P.sbuf("z", [128, 8, HT], F32)
        act = P.sbuf("act", [128, 22, HT], BF16)
        wfo_s = [P.sbuf(f"wfo{i}", [128, 22, 256], BF16) for i in range(2)]
        sil = [P.sbuf(f"sil{i}", [128, 512], F32) for i in range(2)]
        gf = P.sbuf("gf", [128, 3, 8], F32)
    if has_back:
        gbk = P.sbuf("gbk", [128, 8], F32)
        stage = [P.sbuf(f"stage{i}", [128, 512], F32) for i in range(4)]
    pss = [P.psum(f"ps{i}", [128, 512], F32) for i in range(6)]
    psn = [P.psum(f"psn{i}", [128, 512], F32) for i in range(2)]
    st = {"ps": 0, "psn": 0, "slab": 0, "wfo": 0, "sil": 0, "stage": 0, "ev": 0, "wstg": 0}

    def rr(key, lst):
        i = st[key]
        st[key] = (i + 1) % len(lst)
        return lst[i]

    P.op("dve", lambda e: e.memset(ones[:], 1.0 / D_MODEL), writes=[ones.b])
    P.op("dve", lambda e: e.memset(epst[:], EPS), writes=[epst.b])
    if has_front:
        P.op("sp", lambda e: e.dma_start(out=gf[:], in_=g_d[:, :, :]), writes=[gf.b], reads=[g_d.b], dma_sem="d_gf")
    if has_back:
        P.op("sp", lambda e: e.dma_start(out=gbk[:], in_=gb_d[:, :]), writes=[gbk.b], reads=[gb_d.b], dma_sem="d_gbk")

    def load_slab(view, c0, ncols, src_b):
        s = rr("slab", slabs)
        P.op("pool", lambda e: e.dma_start(out=s[:, :, 0:ncols], in_=view[:, :, c0:c0 + ncols]),
             reads=[src_b], writes=[s.b], dma_sem="d_" + s.b.name)
        return s

    def evac(dst_ap, dst_b, ps):
        st["ev"] ^= 1
        if st["ev"]:
            P.op("act", lambda e: e.activation(dst_ap, ps[:], AF.Copy), reads=[ps.b], writes=[dst_b])
        else:
            P.op("dve", lambda e: e.tensor_copy(dst_ap, ps[:]), reads=[ps.b], writes=[dst_b])

    def rms_rstd(src, blk):
        tsl = slice(blk * 512, (blk + 1) * 512)
        P.op("act", lambda e: e.activation(sq[:], src[:, :, tsl], AF.Square), reads=[src.b], writes=[sq.b])
        ps = rr("psn", psn)
        for ci in range(8):
            P.op("pe", lambda e, ci=ci: e.matmul(ps[:], ones[:], sq[:, ci, :], start=(ci == 0), stop=(ci == 7)),
                 reads=[ones.b, sq.b], writes=[ps.b])
        P.op("act", lambda e: e.activation(rstd[:], ps[:], AF.Sqrt, bias=epst[:], scale=1.0),
             reads=[ps.b, epst.b], writes=[rstd.b])
        P.op("dve", lambda e: e.reciprocal(rstd[:], rstd[:]), reads=[rstd.b], writes=[rstd.b])

    def pre_norm(g_ap_fn, g_b, blk):
        tsl = slice(blk * 512, (blk + 1) * 512)
        rms_rstd(x, blk)
        for ci in range(8):
            P.op("dve", lambda e, ci=ci: e.scalar_tensor_tensor(hb[:, ci, tsl], x[:, ci, tsl], g_ap_fn(ci), rstd[:],
                                                                ALU.mult, ALU.mult),
                 reads=[x.b, rstd.b, g_b], writes=[hb.b])

    def post_norm_add(g_ap_fn, g_b, blk):
        tsl = slice(blk * 512, (blk + 1) * 512)
        rms_rstd(z, blk)
        for ci in range(8):
            P.op("dve", lambda e, ci=ci: e.scalar_tensor_tensor(z[:, ci, tsl], z[:, ci, tsl], g_ap_fn(ci), rstd[:],
                                                                ALU.mult, ALU.mult),
                 reads=[z.b, rstd.b, g_b], writes=[z.b])
        P.op("dve", lambda e: e.tensor_tensor(x[:, :, tsl], x[:, :, tsl], z[:, :, tsl], ALU.add),
             reads=[x.b, z.b], writes=[x.b])

    for half in range(2):
        t0 = half * HT
        P.op("sp", lambda e, t0=t0: e.dma_start(out=x[:], in_=xv[:, :, t0:t0 + HT]),
             reads=[xT_d.b], writes=[x.b], dma_sem="d_x")
        if has_front:
            P.op("pool", lambda e, t0=t0: e.dma_start(out=act[:, 0:8, :], in_=yv[:, :, t0:t0 + HT]),
                 reads=[yT_d.b], writes=[act.b], dma_sem="d_act")
            for sl in range(2):
                s = load_slab(wov, sl * 512, 512, wo_d.b)
                for ccl in range(4):
                    cc = sl * 4 + ccl
                    for blk in range(NBLK):
                        tsl = slice(blk * 512, (blk + 1) * 512)
                        ps = rr("ps", pss)
                        for ci in range(8):
                            P.op("pe", lambda e, ci=ci, s=s, ccl=ccl, tsl=tsl, ps=ps: e.matmul(
                                ps[:], s[:, ci, ccl * 128:(ccl + 1) * 128], act[:, ci, tsl],
                                start=(ci == 0), stop=(ci == 7)), reads=[s.b, act.b], writes=[ps.b])
                        evac(z[:, cc, tsl], z.b, ps)
            for blk in range(NBLK):
                post_norm_add(lambda ci: gf[:, 0, ci:ci + 1], gf.b, blk)
            for blk in range(NBLK):
                pre_norm(lambda ci: gf[:, 1, ci:ci + 1], gf.b, blk)
            for sl in range(6):
                ncf = min(512, D_FF - sl * 512)
                sg = load_slab(wfiv, sl * 512, ncf, wfi_d.b)
                su = load_slab(wfiv, D_FF + sl * 512, ncf, wfi_d.b)
                for fcl in range(ncf // 128):
                    fc = sl * 4 + fcl
                    for blk in range(NBLK):
                        tsl = slice(blk * 512, (blk + 1) * 512)
                        pg = rr("ps", pss)
                        for ci in range(8):
                            P.op("pe", lambda e, ci=ci, sg=sg, fcl=fcl, tsl=tsl, pg=pg: e.matmul(
                                pg[:], sg[:, ci, fcl * 128:(fcl + 1) * 128], hb[:, ci, tsl],
                                start=(ci == 0), stop=(ci == 7)), reads=[sg.b, hb.b], writes=[pg.b])
                        pu = rr("ps", pss)
                        for ci in range(8):
                            P.op("pe", lambda e, ci=ci, su=su, fcl=fcl, tsl=tsl, pu=pu: e.matmul(
                                pu[:], su[:, ci, fcl * 128:(fcl + 1) * 128], hb[:, ci, tsl],
                                start=(ci == 0), stop=(ci == 7)), reads=[su.b, hb.b], writes=[pu.b])
                        sb = rr("sil", sil)
                        P.op("act", lambda e, sb=sb, pg=pg: e.activation(sb[:], pg[:], AF.Silu),
                             reads=[pg.b], writes=[sb.b])
                        P.op("dve", lambda e, sb=sb, pu=pu, fc=fc, tsl=tsl: e.tensor_tensor(
                            act[:, fc, tsl], sb[:], pu[:], ALU.mult), reads=[sb.b, pu.b], writes=[act.b])
            for cc2 in range(4):
                w = rr("wfo", wfo_s)
                P.op("pool", lambda e, w=w, cc2=cc2: e.dma_start(out=w[:], in_=wfov[:, :, cc2 * 256:(cc2 + 1) * 256]),
                     reads=[wfo_d.b], writes=[w.b], dma_sem="d_" + w.b.name)
                for ccl in range(2):
                    cc = cc2 * 2 + ccl
                    for blk in range(NBLK):
                        tsl = slice(blk * 512, (blk + 1) * 512)
                        ps = rr("ps", pss)
                        for fc in range(22):
                            P.op("pe", lambda e, fc=fc, w=w, tsl=tsl, ps=ps, ccl=ccl: e.matmul(
                                ps[:], w[:, fc, ccl * 128:(ccl + 1) * 128], act[:, fc, tsl], start=(fc == 0), stop=(fc == 21)),
                                reads=[w.b, act.b], writes=[ps.b])
                        evac(z[:, cc, tsl], z.b, ps)
            for blk in range(NBLK):
                post_norm_add(lambda ci: gf[:, 2, ci:ci + 1], gf.b, blk)
        if has_back:
            for blk in range(NBLK):
                pre_norm(lambda ci: gbk[:, ci:ci + 1], gbk.b, blk)
            for sl in range(7):
                ncols = 512 if sl < 6 else 32
                s = load_slab(winv, sl * 512, ncols, win_d.b)
                for ccl in range((ncols + 127) // 128):
                    m = min(128, ncols - ccl * 128)
                    c0 = sl * 512 + ccl * 128
                    for blk in range(NBLK):
                        tsl = slice(blk * 512, (blk + 1) * 512)
                        ps = rr("ps", pss)
                        for ci in range(8):
                            P.op("pe", lambda e, ci=ci, s=s, ccl=ccl, m=m, tsl=tsl, ps=ps: e.matmul(
                                ps[0:m, :], s[:, ci, ccl * 128:ccl * 128 + m], hb[:, ci, tsl],
                                start=(ci == 0), stop=(ci == 7)), reads=[s.b, hb.b], writes=[ps.b])
                        sg_ = rr("stage", stage)
                        st["ev"] ^= 1
                        if st["ev"]:
                            P.op("act", lambda e, sg_=sg_, ps=ps, m=m: e.activation(sg_[0:m, :], ps[0:m, :], AF.Copy),
                                 reads=[ps.b], writes=[sg_.b])
                        else:
                            P.op("dve", lambda e, sg_=sg_, ps=ps, m=m: e.tensor_copy(sg_[0:m, :], ps[0:m, :]),
                                 reads=[ps.b], writes=[sg_.b])
                        P.op("sp", lambda e, sg_=sg_, m=m, c0=c0, t0=t0, blk=blk: e.dma_start(
                            out=pT_d[c0:c0 + m, t0 + blk * 512:t0 + (blk + 1) * 512], in_=sg_[0:m, :]),
                            reads=[sg_.b], writes=[pT_d.b], dma_sem="o_" + sg_.b.name)
        if has_front:
            P.op("sp", lambda e, t0=t0: e.dma_start(out=xov[:, :, t0:t0 + HT], in_=x[:]),
                 reads=[x.b], writes=[xo_d.b], dma_sem="o_x")
    P.finish()
    P.emit()
    return nc


_DENSE_CACHE = {}


def run_dense(has_front, has_back, in_maps):
    key = (has_front, has_back)
    nc = build_dense(has_front, has_back)
    res = run_bass_kernel_spmd(nc, in_maps, core_ids=list(range(NCORES)))
    return res.results


def gvec(g):
    return np.ascontiguousarray(g.reshape(8, 128).T)


def _consts(P, npart=64):
    one_t = P.sbuf("one_t", [128, 1], F32)
    eps_t = P.sbuf("eps_t", [128, 1], F32)
    P.op("dve", lambda e: e.memset(one_t[:], 1.0), writes=[one_t.b])
    P.op("dve", lambda e: e.memset(eps_t[:], EPS), writes=[eps_t.b])
    return one_t, eps_t


def emit_lru(P, L, d):
    TB = 1024
    NB = L // TB
    xp_d, gate_d, cw_d, wax_d, prm_d, y_d = d["xpad"], d["gate"], d["cw"], d["wax"], d["prm"], d["y"]

    one_t, eps_t = _consts(P)
    xp = P.sbuf("xp", [64, L + 3], F32)
    xc = P.sbuf("xc", [64, L], F32)
    xcb = P.sbuf("xcb", [64, L], BF16)
    hf = P.sbuf("hf", [64, L], F32)
    cw = P.sbuf("cw_s", [64, 4], F32)
    wax = P.sbuf("wax_s", [64, 256], F32)
    waxb = P.sbuf("waxb", [64, 256], BF16)
    prm = P.sbuf("prm_s", [64, 7], F32)
    sp_ = P.sbuf("sp_s", [64, 2], F32)
    s8 = P.sbuf("s8", [64, 2], F32)
    s16 = P.sbuf("s16", [64, 2], F32)
    carry = P.sbuf("carry", [64, 1], F32)
    r_s = [P.sbuf(f"r{i}", [64, TB], F32) for i in range(2)]
    i_s = [P.sbuf(f"i{i}", [64, TB], F32) for i in range(2)]
    a_s = [P.sbuf(f"a{i}", [64, TB], F32) for i in range(2)]
    a2s = [P.sbuf(f"a2{i}", [64, TB], F32) for i in range(2)]
    u_s = [P.sbuf(f"u{i}", [64, TB], F32) for i in range(2)]
    hbk = P.sbuf("hbk", [64, TB], F32)
    gts = [P.sbuf(f"gt{i}", [64, TB], F32) for i in range(2)]
    gls = [P.sbuf(f"gl{i}", [64, TB], F32) for i in range(2)]
    yb = [P.sbuf(f"yb{i}", [64, TB], F32) for i in range(2)]
    pss = [P.psum(f"ps{i}", [64, 512], F32) for i in range(4)]
    st = {"ps": 0}

    def nps():
        i = st["ps"]
        st["ps"] = (i + 1) % 4
        return pss[i]

    for t, d in ((xp, xp_d), (cw, cw_d), (wax, wax_d), (prm, prm_d)):
        P.op("sp", lambda e, t=t, d=d: e.dma_start(out=t[:], in_=d[:, :]), reads=[d.b], writes=[t.b],
             dma_sem="d_" + t.b.name)
    P.op("dve", lambda e: e.tensor_copy(waxb[:], wax[:]), reads=[wax.b], writes=[waxb.b])
    P.op("act", lambda e: e.activation(sp_[:], prm[:, 5:7], AF.Exp, scale=-1.0), reads=[prm.b], writes=[sp_.b])
    P.op("act", lambda e: e.activation(sp_[:], sp_[:], AF.Ln, bias=one_t[0:64, :]), reads=[sp_.b, one_t.b], writes=[sp_.b])
    P.op("dve", lambda e: e.tensor_scalar(s8[:], sp_[:], -8.0, None, ALU.mult), reads=[sp_.b], writes=[s8.b])
    P.op("dve", lambda e: e.tensor_scalar(s16[:], sp_[:], -16.0, None, ALU.mult), reads=[sp_.b], writes=[s16.b])
    for b in range(NB):
        sl = slice(b * TB, (b + 1) * TB)
        P.op("act", lambda e, sl=sl: e.activation(xc[:, sl], xp[:, sl], AF.Identity, bias=prm[:, 0:1], scale=cw[:, 0:1]),
             reads=[xp.b, prm.b, cw.b], writes=[xc.b])
        for j in range(1, 4):
            P.op("dve", lambda e, sl=sl, j=j, b=b: e.scalar_tensor_tensor(
                xc[:, sl], xp[:, b * TB + j:(b + 1) * TB + j], cw[:, j:j + 1], xc[:, sl], ALU.mult, ALU.add),
                reads=[xp.b, cw.b, xc.b], writes=[xc.b])
        P.op("pool", lambda e, sl=sl: e.tensor_copy(xcb[:, sl], xc[:, sl]), reads=[xc.b], writes=[xcb.b])

    for e_ in range(2):
        order = range(NB) if e_ == 0 else range(NB - 1, -1, -1)
        first = True
        for b in order:
            sl = slice(b * TB, (b + 1) * TB)
            r_, i_, a_, a2, u_, gl = r_s[b % 2], i_s[b % 2], a_s[b % 2], a2s[b % 2], u_s[b % 2], gls[b % 2]
            if e_ == 1:
                gt = gts[b % 2]
                P.op("sp", lambda e, gt=gt, sl=sl, r_=r_, i_=i_, a_=a_, a2=a2, u_=u_, gl=gl: e.dma_start(out=gt[:], in_=gate_d[:, sl]),
                     reads=[gate_d.b], writes=[gt.b], dma_sem="d_" + gt.b.name)
            for sb in range(TB // 512):
                c0 = b * TB + sb * 512
                pr = nps()
                P.op("pe", lambda e, pr=pr, c0=c0, e_=e_, r_=r_, i_=i_, a_=a_, a2=a2, u_=u_, gl=gl: e.matmul(pr[:], waxb[:, e_ * 64:(e_ + 1) * 64], xcb[:, c0:c0 + 512],
                                                            start=True, stop=True),
                     reads=[waxb.b, xcb.b], writes=[pr.b])
                pi = nps()
                P.op("pe", lambda e, pi=pi, c0=c0, e_=e_, r_=r_, i_=i_, a_=a_, a2=a2, u_=u_, gl=gl: e.matmul(pi[:], waxb[:, 128 + e_ * 64:128 + (e_ + 1) * 64],
                                                            xcb[:, c0:c0 + 512], start=True, stop=True),
                     reads=[waxb.b, xcb.b], writes=[pi.b])
                P.op("act", lambda e, pr=pr, sb=sb, e_=e_, r_=r_, i_=i_, a_=a_, a2=a2, u_=u_, gl=gl: e.activation(r_[:, sb * 512:(sb + 1) * 512], pr[:], AF.Sigmoid,
                                                                 bias=prm[:, 1 + e_:2 + e_]),
                     reads=[pr.b, prm.b], writes=[r_.b])
                P.op("act", lambda e, pi=pi, sb=sb, e_=e_, r_=r_, i_=i_, a_=a_, a2=a2, u_=u_, gl=gl: e.activation(i_[:, sb * 512:(sb + 1) * 512], pi[:], AF.Sigmoid,
                                                                 bias=prm[:, 3 + e_:4 + e_]),
                     reads=[pi.b, prm.b], writes=[i_.b])
            P.op("act", lambda e, e_=e_, r_=r_, i_=i_, a_=a_, a2=a2, u_=u_, gl=gl: e.activation(a_[:], r_[:], AF.Exp, scale=s8[:, e_:e_ + 1]),
                 reads=[r_.b, s8.b], writes=[a_.b])
            P.op("act", lambda e, e_=e_, r_=r_, i_=i_, a_=a_, a2=a2, u_=u_, gl=gl: e.activation(a2[:], r_[:], AF.Exp, scale=s16[:, e_:e_ + 1]),
                 reads=[r_.b, s16.b], writes=[a2.b])
            P.op("act", lambda e, r_=r_, i_=i_, a_=a_, a2=a2, u_=u_, gl=gl: e.activation(a2[:], a2[:], AF.Sqrt, bias=one_t[0:64, :], scale=-1.0),
                 reads=[a2.b, one_t.b], writes=[a2.b])
            P.op("dve", lambda e, sl=sl, r_=r_, i_=i_, a_=a_, a2=a2, u_=u_, gl=gl: e.tensor_tensor(u_[:], i_[:], xc[:, sl], ALU.mult),
                 reads=[i_.b, xc.b], writes=[u_.b])
            P.op("dve", lambda e, r_=r_, i_=i_, a_=a_, a2=a2, u_=u_, gl=gl: e.tensor_tensor(u_[:], u_[:], a2[:], ALU.mult), reads=[u_.b, a2.b], writes=[u_.b])
            if e_ == 0:
                init = 0.0 if first else hf[:, b * TB - 1:b * TB]
                P.op("dve", lambda e, sl=sl, init=init, r_=r_, i_=i_, a_=a_, a2=a2, u_=u_, gl=gl: e.tensor_tensor_scan(hf[:, sl], a_[:], u_[:], init, ALU.mult, ALU.add),
                     reads=[a_.b, u_.b, hf.b], writes=[hf.b])
            else:
                init = 0.0 if first else carry[:]
                P.op("dve", lambda e, init=init, r_=r_, i_=i_, a_=a_, a2=a2, u_=u_, gl=gl: e.tensor_tensor_scan(hbk[:, ::-1], a_[:, ::-1], u_[:, ::-1], init,
                                                                      ALU.mult, ALU.add),
                     reads=[a_.b, u_.b, carry.b], writes=[hbk.b])
                P.op("dve", lambda e, r_=r_, i_=i_, a_=a_, a2=a2, u_=u_, gl=gl: e.tensor_copy(carry[:], hbk[:, 0:1]), reads=[hbk.b], writes=[carry.b])
                P.op("act", lambda e, gt=gt, r_=r_, i_=i_, a_=a_, a2=a2, u_=u_, gl=gl: e.activation(gl[:], gt[:], AF.Gelu_apprx_tanh), reads=[gt.b], writes=[gl.b])
                yo = yb[b % 2]
                P.op("dve", lambda e, sl=sl, yo=yo, r_=r_, i_=i_, a_=a_, a2=a2, u_=u_, gl=gl: e.tensor_tensor(yo[:], hf[:, sl], hbk[:], ALU.add),
                     reads=[hf.b, hbk.b], writes=[yo.b])
                P.op("dve", lambda e, yo=yo, r_=r_, i_=i_, a_=a_, a2=a2, u_=u_, gl=gl: e.tensor_tensor(yo[:], yo[:], gl[:], ALU.mult), reads=[yo.b, gl.b], writes=[yo.b])
                P.op("sp", lambda e, sl=sl, yo=yo, r_=r_, i_=i_, a_=a_, a2=a2, u_=u_, gl=gl: e.dma_start(out=y_d[:, sl], in_=yo[:]), reads=[yo.b], writes=[y_d.b],
                     dma_sem="o_" + yo.b.name)
            first = False


def prep_lru(pT, b_, j, lp, L):
    c0 = 1824 + 64 * j
    xpad = np.zeros((64, L + 3), np.float32)
    xpad[:, 2:2 + L] = pT[c0:c0 + 64]
    g0 = 2080 + 64 * j
    hs = slice(64 * j, 64 * j + 64)
    wax = np.concatenate([lp["lru_w_a"][0, j], lp["lru_w_a"][1, j], lp["lru_w_x"][0, j], lp["lru_w_x"][1, j]], axis=1)
    prm = np.stack([lp["lru_conv_b"][hs], lp["lru_b_a"][0, hs], lp["lru_b_a"][1, hs], lp["lru_b_x"][0, hs],
                    lp["lru_b_x"][1, hs], lp["lru_lambda"][0, hs], lp["lru_lambda"][1, hs]], axis=1)
    return {"xpad": xpad, "gate": np.ascontiguousarray(pT[g0:g0 + 64]),
            "cw": np.ascontiguousarray(lp["lru_conv_w"][:, hs].T), "wax": np.ascontiguousarray(wax, dtype=np.float32),
            "prm": np.ascontiguousarray(prm, dtype=np.float32)}


def _attn_finalize(P, acc, sel, y_d, L, pds, tag):
    rds = [P.sbuf(f"rd{tag}{i}", [64, 512], F32) for i in range(2)]
    yos = [P.sbuf(f"yo{tag}{i}", [64, 512], F32) for i in range(2)]
    for blk in range(L // 512):
        cols = slice(blk * 512, (blk + 1) * 512)
        pd = pds[blk % 2]
        rd = rds[blk % 2]
        yo = yos[blk % 2]
        P.op("pe", lambda e, pd=pd, cols=cols: e.matmul(pd[:], sel[:], acc[:, cols], start=True, stop=True),
             reads=[sel.b, acc.b], writes=[pd.b])
        P.op("dve", lambda e, pd=pd, rd=rd: e.reciprocal(rd[:], pd[:]), reads=[pd.b], writes=[rd.b])
        P.op("dve", lambda e, rd=rd, yo=yo, cols=cols: e.tensor_tensor(yo[:], acc[0:64, cols], rd[:], ALU.mult),
             reads=[acc.b, rd.b], writes=[yo.b])
        P.op("sp", lambda e, yo=yo, cols=cols: e.dma_start(out=y_d[:, cols], in_=yo[:]), reads=[yo.b], writes=[y_d.b],
             dma_sem="o_" + yo.b.name)


def _make_sel(P):
    sel = P.sbuf("sel", [65, 64], F32)
    P.op("dve", lambda e: e.memset(sel[:], 0.0), writes=[sel.b])
    P.op("dve", lambda e: e.memset(sel[64:65, :], 1.0), writes=[sel.b])
    return sel


def _load_cast(P, dst, src_d, L, stg, nrows=64, blkw=2048, eng="pool"):
    for b in range(L // blkw):
        s = stg[b % len(stg)]
        sl = slice(b * blkw, (b + 1) * blkw)
        P.op("sp", lambda e, s=s, sl=sl: e.dma_start(out=s[0:nrows, 0:blkw], in_=src_d[:, sl]), reads=[src_d.b], writes=[s.b],
             dma_sem="d_" + s.b.name)
        P.op(eng, lambda e, s=s, sl=sl: e.tensor_copy(dst[:, sl], s[0:nrows, 0:blkw]), reads=[s.b], writes=[dst.b])


def emit_na(P, L, d):
    ntile = L // 128
    qT_d, kT_d, va_d, bias_d, mask_d, y_d = d["qT"], d["kT"], d["vaug"], d["biasg"], d["maskc"], d["y"]

    qb = P.sbuf("qb", [64, L], BF16)
    kb = P.sbuf("kb", [64, L], BF16)
    stg = [P.sbuf(f"stg{i}", [64, 2048], F32) for i in range(2)]
    vb = P.sbuf("vb", [128, ntile, 65], BF16)
    vst = [P.sbuf(f"vst{i}", [128, 16, 65], F32) for i in range(2)]
    Fm = P.sbuf("Fm", [128, 5, 640], BF16)
    bst = P.sbuf("bst", [128, 640], F32)
    mst = P.sbuf("mst", [128, 640], F32)
    acc = P.sbuf("acc", [65, L], F32)
    sel = _make_sel(P)
    pexp = [P.sbuf(f"pexp{i}", [128, 640], BF16) for i in range(2)]
    pms = [P.sbuf(f"pm{i}", [128, 640], BF16) for i in range(2)]
    pSs = [P.psum(f"pS{i}", [128, 1024], F32) for i in range(2)]
    pos = [P.psum(f"po{i}", [65, 512], F32) for i in range(2)]
    pds = [P.psum(f"pd{i}", [64, 512], F32) for i in range(2)]

    _load_cast(P, qb, qT_d, L, stg)
    _load_cast(P, kb, kT_d, L, stg)
    for g in range((ntile + 15) // 16):
        s = vst[g % 2]
        n = min(16, ntile - g * 16)
        P.op("sp", lambda e, s=s, g=g, n=n: e.dma_start(out=s[:, 0:n, :], in_=va_d[:, g * 16:g * 16 + n, :]),
             reads=[va_d.b], writes=[s.b], dma_sem="d_" + s.b.name)
        P.op("pool", lambda e, s=s, g=g, n=n: e.tensor_copy(vb[:, g * 16:g * 16 + n, :], s[:, 0:n, :]),
             reads=[s.b], writes=[vb.b])
    for fi in range(5):
        P.op("sp", lambda e, fi=fi: e.dma_start(out=bst[:], in_=bias_d[:, fi, :]), reads=[bias_d.b], writes=[bst.b], dma_sem="d_bst")
        P.op("sp", lambda e, fi=fi: e.dma_start(out=mst[:], in_=mask_d[:, fi, :]), reads=[mask_d.b], writes=[mst.b], dma_sem="d_mst")
        P.op("act", lambda e: e.activation(bst[:], bst[:], AF.Exp), reads=[bst.b], writes=[bst.b])
        P.op("dve", lambda e, fi=fi: e.tensor_tensor(Fm[:, fi, :], bst[:], mst[:], ALU.mult), reads=[bst.b, mst.b], writes=[Fm.b])

    def na_stage1(m):
        tb = min(max(m - 2, 0), ntile - 5)
        fi = 0 if m == 0 else 1 if m == 1 else 3 if m == ntile - 2 else 4 if m == ntile - 1 else 2
        pS = pSs[m % 2]
        for jt in range(5):
            P.op("pe", lambda e, pS=pS, jt=jt, tb=tb, m=m: e.matmul(
                pS[:, jt * 128:(jt + 1) * 128], kb[:, (tb + jt) * 128:(tb + jt + 1) * 128], qb[:, m * 128:(m + 1) * 128],
                start=True, stop=True), reads=[kb.b, qb.b], writes=[pS.b])
        pe_ = pexp[m % 2]
        P.op("act", lambda e, pe_=pe_, pS=pS: e.activation(pe_[:], pS[:, 0:640], AF.Exp, scale=0.125),
             reads=[pS.b], writes=[pe_.b])
        pm_ = pms[m % 2]
        P.op("dve", lambda e, pe_=pe_, pm_=pm_, fi=fi: e.tensor_tensor(pm_[:], pe_[:], Fm[:, fi, :], ALU.mult),
             reads=[pe_.b, Fm.b], writes=[pm_.b])
        return pm_, tb

    def na_stage2(m, pm_, tb):
        po = pos[(m // 4) % 2]
        for jt in range(5):
            P.op("pe", lambda e, po=po, jt=jt, tb=tb, m=m, pm_=pm_: e.matmul(
                po[:, (m % 4) * 128:(m % 4 + 1) * 128], vb[:, tb + jt, :], pm_[:, jt * 128:(jt + 1) * 128],
                start=(jt == 0), stop=(jt == 4)), reads=[vb.b, pm_.b], writes=[po.b])
        if m % 4 == 3:
            P.op("act", lambda e, po=po, m=m: e.activation(acc[:, (m - 3) * 128:(m + 1) * 128], po[:], AF.Copy),
                 reads=[po.b], writes=[acc.b])

    cur = na_stage1(0)
    for m in range(ntile):
        nxt = na_stage1(m + 1) if m + 1 < ntile else None
        na_stage2(m, *cur)
        cur = nxt
    _attn_finalize(P, acc, sel, y_d, L, pds, "n")


def na_tables(L):
    rows = L // 64
    ntile = L // 128
    ms = [0, 1, 2, ntile - 2, ntile - 1]
    dr_i = np.zeros((5, 128, 5, 128), np.int64)
    dc_i = np.zeros((5, 128, 5, 128), np.int64)
    mask = np.zeros((5, 128, 5, 128), np.float32)
    pk = np.arange(128)
    fq = np.arange(128)
    for vi, m in enumerate(ms):
        tb = min(max(m - 2, 0), ntile - 5)
        for jt in range(5):
            krow = 2 * (tb + jt) + pk // 64
            kc = pk % 64
            qrow = 2 * m + fq // 64
            qc = fq % 64
            rs = np.clip(qrow - 4, 0, rows - 8)
            row_ok = (krow[:, None] >= rs[None, :]) & (krow[:, None] < rs[None, :] + 8)
            cs = np.clip(qc - 8, 0, 48)
            col_ok = (kc[:, None] >= cs[None, :]) & (kc[:, None] < cs[None, :] + 16)
            dr = np.clip(krow[:, None] - qrow[None, :], -7, 7)
            dc = np.clip(kc[:, None] - qc[None, :], -15, 15)
            dr_i[vi, :, jt, :] = dr + 7
            dc_i[vi, :, jt, :] = dc + 15
            mask[vi, :, jt, :] = (row_ok & col_ok).astype(np.float32)
    return dr_i, dc_i, mask


_NA_TAB = {}


def prep_na(pT, b_, j, lp, L):
    if L not in _NA_TAB:
        _NA_TAB[L] = na_tables(L)
    dr_i, dc_i, mask = _NA_TAB[L]
    ntile = L // 128
    q0, k0, v0 = 1056 + 64 * j, 1312 + 64 * j, 1568 + 64 * j
    v = pT[v0:v0 + 64]
    vaug = np.ones((128, ntile, 65), np.float32)
    vaug[:, :, 0:64] = v.T.reshape(ntile, 128, 64).transpose(1, 0, 2)
    rpb = lp["na_rpb"][j]
    biasg = rpb[dr_i, dc_i]
    biasg = np.ascontiguousarray(biasg.transpose(1, 0, 2, 3).reshape(128, 5, 640), dtype=np.float32)
    maskc = np.ascontiguousarray(mask.transpose(1, 0, 2, 3).reshape(128, 5, 640))
    return {"qT": np.ascontiguousarray(pT[q0:q0 + 64]), "kT": np.ascontiguousarray(pT[k0:k0 + 64]),
            "vaug": vaug, "biasg": biasg, "maskc": maskc}


DIL_D = (1, 4, 16)
DIL_PAD = 1024


def _dil_tiles(L):
    idx = {}
    t = 0
    for d in DIL_D:
        n = L // d
        for r in range(d):
            for m in range(n // 128 + 1):
                idx[(d, r, m)] = t
                t += 1
    return idx, t


def emit_dil(P, L, d):
    tidx, NT = _dil_tiles(L)
    names = ("qT", "qsT", "kT", "ksT", "cos2", "sin2s")
    dd = {nm: d[nm] for nm in names}
    va_d, mk_d, y_d = d["vaug"], d["maskab"], d["y"]

    qh = P.sbuf("qh", [64, L], BF16)
    kh = P.sbuf("kh", [64, L + 2 * DIL_PAD], BF16)
    BW = 1024
    stg = {nm: P.sbuf("s_" + nm, [64, BW], F32) for nm in names}
    t1 = P.sbuf("t1", [64, BW], F32)
    t2 = P.sbuf("t2", [64, BW], F32)
    vb = P.sbuf("vb", [128, NT, 65], BF16)
    vst = [P.sbuf(f"vst{i}", [128, 16, 65], F32) for i in range(2)]
    mk = P.sbuf("mk", [128, 256], F32)
    acc = P.sbuf("acc", [65, L], F32)
    sel = _make_sel(P)
    pexp = [P.sbuf(f"pexp{i}", [128, 256], BF16) for i in range(2)]
    pms = [P.sbuf(f"pm{i}", [128, 256], BF16) for i in range(3)]
    pSs = [P.psum(f"pS{i}", [128, 512], F32) for i in range(3)]
    pos = [P.psum(f"po{i}", [65, 512], F32) for i in range(2)]
    pds = [P.psum(f"pd{i}", [64, 512], F32) for i in range(2)]

    P.op("sp", lambda e: e.dma_start(out=mk[:], in_=mk_d[:, :]), reads=[mk_d.b], writes=[mk.b], dma_sem="d_mk")
    P.op("dve", lambda e: e.memset(kh[:, 0:DIL_PAD], 0.0), writes=[kh.b])
    P.op("dve", lambda e: e.memset(kh[:, DIL_PAD + L:DIL_PAD + L + DIL_PAD], 0.0), writes=[kh.b])
    for g in range((NT + 15) // 16):
        s = vst[g % 2]
        n = min(16, NT - g * 16)
        P.op("sp", lambda e, s=s, g=g, n=n: e.dma_start(out=s[:, 0:n, :], in_=va_d[:, g * 16:g * 16 + n, :]),
             reads=[va_d.b], writes=[s.b], dma_sem="d_" + s.b.name)
        P.op("pool", lambda e, s=s, g=g, n=n: e.tensor_copy(vb[:, g * 16:g * 16 + n, :], s[:, 0:n, :]),
             reads=[s.b], writes=[vb.b])
    for b in range(L // BW):
        sl = slice(b * BW, (b + 1) * BW)
        for nm in names:
            P.op("sp", lambda e, nm=nm, sl=sl: e.dma_start(out=stg[nm][:], in_=dd[nm][:, sl]), reads=[dd[nm].b],
                 writes=[stg[nm].b], dma_sem="d_s_" + nm)
        for (a, s_, dst, off) in (("qT", "qsT", qh, 0), ("kT", "ksT", kh, DIL_PAD)):
            P.op("dve", lambda e, a=a: e.tensor_tensor(t1[:], stg[a][:], stg["cos2"][:], ALU.mult),
                 reads=[stg[a].b, stg["cos2"].b], writes=[t1.b])
            P.op("pool", lambda e, s_=s_: e.tensor_tensor(t2[:], stg[s_][:], stg["sin2s"][:], ALU.mult),
                 reads=[stg[s_].b, stg["sin2s"].b], writes=[t2.b])
            P.op("dve", lambda e, dst=dst, off=off, b=b: e.tensor_tensor(
                dst[:, off + b * BW:off + (b + 1) * BW], t1[:], t2[:], ALU.add), reads=[t1.b, t2.b], writes=[dst.b])

    tiles = []
    for d in DIL_D:
        nq = (L // d) // 128
        for r in range(d):
            for m in range(nq + 1):
                tiles.append((d, r, m, nq))
    cnt = {"po": 0}
    state = {"po": None}

    def dil_stage1(i):
        d, r, m, nq = tiles[i]
        c_lo = 128 if m == 0 else 0
        c_hi = 128 if m == nq else 256
        i0_ = 128 * (m - 1) + c_lo
        cntq = c_hi - c_lo
        ks = DIL_PAD + r + d * (128 * m - 64)
        qs = r + d * i0_
        pS = pSs[i % 3]
        pe_ = pexp[i % 2]
        pm_ = pms[i % 3]
        P.op("pe", lambda e, pS=pS, ks=ks, qs=qs, d=d, cntq=cntq, c_lo=c_lo, c_hi=c_hi: e.matmul(
            pS[:, c_lo:c_hi], kh[:, ks:ks + 127 * d + 1:d], qh[:, qs:qs + (cntq - 1) * d + 1:d], start=True, stop=True),
            reads=[kh.b, qh.b], writes=[pS.b])
        P.op("act", lambda e, pe_=pe_, pS=pS, c_lo=c_lo, c_hi=c_hi: e.activation(
            pe_[:, c_lo:c_hi], pS[:, c_lo:c_hi], AF.Exp, scale=0.125), reads=[pS.b], writes=[pe_.b])
        P.op("dve", lambda e, pe_=pe_, pm_=pm_, c_lo=c_lo, c_hi=c_hi: e.tensor_tensor(
            pm_[:, c_lo:c_hi], pe_[:, c_lo:c_hi], mk[:, c_lo:c_hi], ALU.mult), reads=[pe_.b, mk.b], writes=[pm_.b])

    def dil_stage2(i):
        d, r, m, nq = tiles[i]
        if m == 0:
            return
        prev = pms[(i - 1) % 3]
        pm_ = pms[i % 3]
        mq = m - 1
        if mq % 4 == 0:
            state["po"] = pos[cnt["po"] % 2]
            cnt["po"] += 1
        po = state["po"]
        osl = slice((mq % 4) * 128, (mq % 4 + 1) * 128)
        P.op("pe", lambda e, po=po, osl=osl, prev=prev, tA=tidx[(d, r, mq)]: e.matmul(
            po[:, osl], vb[:, tA, :], prev[:, 128:256], start=True, stop=False),
            reads=[vb.b, prev.b], writes=[po.b])
        P.op("pe", lambda e, po=po, osl=osl, pm_=pm_, tB=tidx[(d, r, m)]: e.matmul(
            po[:, osl], vb[:, tB, :], pm_[:, 0:128], start=False, stop=True),
            reads=[vb.b, pm_.b], writes=[po.b])
        if mq % 4 == 3 or mq == nq - 1:
            mq0 = mq - (mq % 4)
            w = (mq % 4 + 1) * 128
            a0 = r + d * 128 * mq0
            if d == 1:
                P.op("act", lambda e, po=po, a0=a0, w=w: e.activation(acc[:, a0:a0 + w], po[:, 0:w], AF.Copy),
                     reads=[po.b], writes=[acc.b])
            else:
                P.op("dve", lambda e, po=po, a0=a0, w=w, d=d: e.tensor_tensor(
                    acc[:, a0:a0 + (w - 1) * d + 1:d], acc[:, a0:a0 + (w - 1) * d + 1:d], po[:, 0:w], ALU.add),
                    reads=[po.b, acc.b], writes=[acc.b])

    dil_stage1(0)
    for i in range(len(tiles)):
        if i + 1 < len(tiles):
            dil_stage1(i + 1)
        dil_stage2(i)
    _attn_finalize(P, acc, sel, y_d, L, pds, "d")


_ROPE = {}


def rope_tables(L):
    if L not in _ROPE:
        pos = np.arange(L, dtype=np.float32)
        inv_freq = (np.float32(10000.0) ** (-np.arange(0, 64, 2, dtype=np.float32) / np.float32(64))).astype(np.float32)
        ang = (pos[:, None] * inv_freq[None, :]).astype(np.float32)
        c = np.cos(ang).astype(np.float32).T
        s = np.sin(ang).astype(np.float32).T
        _ROPE[L] = (np.ascontiguousarray(np.concatenate([c, c], 0)), np.ascontiguousarray(np.concatenate([-s, s], 0)))
    return _ROPE[L]


_DIL_MASK = None


def prep_dil(pT, b_, j, lp, L):
    tidx, NT = _dil_tiles(L)
    q0, k0, v0 = 2336 + 64 * j, 2592 + 64 * j, 2848 + 64 * j
    q = pT[q0:q0 + 64]
    k = pT[k0:k0 + 64]
    v = pT[v0:v0 + 64]
    cos2, sin2s = rope_tables(L)
    vT = np.ascontiguousarray(v.T)
    vaug = np.zeros((128, NT, 65), np.float32)
    pk = np.arange(128)
    for d in DIL_D:
        n = L // d
        for r in range(d):
            for m in range(n // 128 + 1):
                i = 128 * m - 64 + pk
                ok = (i >= 0) & (i < n)
                tok = r + d * i[ok]
                t = tidx[(d, r, m)]
                vaug[ok, t, 0:64] = vT[tok]
                vaug[ok, t, 64] = 1.0
    fq = np.arange(128)
    mb = (pk[:, None] <= fq[None, :]).astype(np.float32)
    ma = (pk[:, None] >= fq[None, :]).astype(np.float32)
    return {"qT": np.ascontiguousarray(q), "qsT": np.ascontiguousarray(np.concatenate([q[32:], q[:32]], 0)),
            "kT": np.ascontiguousarray(k), "ksT": np.ascontiguousarray(np.concatenate([k[32:], k[:32]], 0)),
            "cos2": cos2, "sin2s": sin2s, "vaug": vaug, "maskab": np.ascontiguousarray(np.concatenate([mb, ma], 1))}


def emit_gla(P, L, d, debug=False):
    n = L // 128
    qT_d, kT_d, gT_d, vt_d = d["qT"], d["kT"], d["gT"], d["vtok"]
    z_d = [d["z0T"], d["z1T"]]
    wg_d, prm_d, cst_d, rst_d, y_d = d["wg"], d["prm"], d["cst"], d["resetm"], d["y"]

    one_t, eps_t = _consts(P)
    cs = P.sbuf("cs", [64, L], F32)
    csm = P.sbuf("csm", [64, n, 1], F32)
    qt = P.sbuf("qt", [64, L], BF16)
    kt = P.sbuf("kt", [64, L], BF16)
    acc = P.sbuf("acc", [64, L], F32)
    vb = P.sbuf("vb", [128, n, 64], BF16)
    ktok = P.sbuf("ktok", [128, n, 64], BF16)
    vst = [P.sbuf(f"vst{i}", [128, 16, 64], F32) for i in range(2)]
    BW = 1024
    stgq = [P.sbuf(f"stgq{i}", [64, BW], F32) for i in range(2)]
    stgk = [P.sbuf(f"stgk{i}", [64, BW], F32) for i in range(2)]
    eps_ = [P.sbuf(f"ep{i}", [64, BW], F32) for i in range(2)]
    ems_ = [P.sbuf(f"em{i}", [64, BW], F32) for i in range(2)]
    zs = [P.sbuf(f"zs{i}", [16, 512], F32) for i in range(2)]
    wg = P.sbuf("wg_s", [16, 128], F32)
    prm = P.sbuf("prm_s", [64, 3], F32)
    nbg = P.sbuf("nbg", [64, 2], F32)
    cst = P.sbuf("cst_s", [128, 3, 128], F32)
    ident = P.sbuf("ident", [64, 64], BF16)
    rst = P.sbuf("rst", [64, 512], F32)
    tAs = [P.sbuf(f"tA{i}", [64, 512], F32) for i in range(2)]
    tBs = [P.sbuf(f"tB{i}", [64, 512], F32) for i in range(2)]
    dmid = P.sbuf("dmid", [64, n], F32)
    dlast = P.sbuf("dlast", [64, n], F32)
    dkv = P.sbuf("dkv", [64, n], F32)
    S = P.sbuf("S", [64, 64], F32)
    Sts = [P.sbuf(f"St{i}", [64, 64], BF16) for i in range(2)]
    tkv = P.sbuf("tkv", [64, 64], F32)
    Asb = [P.sbuf(f"Asb{i}", [128, 128], BF16) for i in range(2)]
    ones64 = P.sbuf("ones64", [64, 64], F32)
    gst = [P.sbuf(f"gst{i}", [64, 512], F32) for i in range(2)]
    yos = [P.sbuf(f"yo{i}", [64, 512], F32) for i in range(2)]
    rs = P.sbuf("rs", [64, 512], F32)
    pA = [P.psum(f"pA{i}", [128, 512], F32) for i in range(2)]
    pT_ = [P.psum(f"pT{i}", [128, 4, 64], BF16) for i in range(2)]
    pKV = [P.psum(f"pKV{i}", [64, 64], F32) for i in range(2)]
    pO = [P.psum(f"pO{i}", [64, 512], F32) for i in range(2)]

    for t, d in ((wg, wg_d), (prm, prm_d), (cst, cst_d), (rst, rst_d)):
        P.op("sp", lambda e, t=t, d=d: e.dma_start(out=t[:], in_=d.h), reads=[d.b], writes=[t.b], dma_sem="d_" + t.b.name)
    P.op("dve", lambda e: e.tensor_copy(ident[:], cst[0:64, 2, 0:64]), reads=[cst.b], writes=[ident.b])
    P.op("dve", lambda e: e.tensor_scalar(nbg[:], prm[:, 0:2], -1.0, None, ALU.mult), reads=[prm.b], writes=[nbg.b])
    P.op("dve", lambda e: e.memset(ones64[:], 1.0 / 64.0), writes=[ones64.b])
    for g in range((n + 15) // 16):
        s = vst[g % 2]
        m_ = min(16, n - g * 16)
        P.op("sp", lambda e, s=s, g=g, m_=m_: e.dma_start(out=s[:, 0:m_, :], in_=vt_d[:, g * 16:g * 16 + m_, :]),
             reads=[vt_d.b], writes=[s.b], dma_sem="d_" + s.b.name)
        P.op("pool", lambda e, s=s, g=g, m_=m_: e.tensor_copy(vb[:, g * 16:g * 16 + m_, :], s[:, 0:m_, :]),
             reads=[s.b], writes=[vb.b])

    csv = cs[:].rearrange("p (n c) -> p n c", c=128)
    for e_ in range(2):
        mid = 63 if e_ == 0 else 64
        last = 127 if e_ == 0 else 0
        for blk in range(L // 512):
            cols = slice(blk * 512, (blk + 1) * 512)
            z_ = zs[blk % 2]
            tA, tB = tAs[blk % 2], tBs[blk % 2]
            P.op("sp", lambda e, z_=z_, cols=cols, e_=e_: e.dma_start(out=z_[:], in_=z_d[e_][:, cols]),
                 reads=[z_d[e_].b], writes=[z_.b], dma_sem="d_" + z_.b.name)
            pl = pA[blk % 2]
            P.op("pe", lambda e, pl=pl, z_=z_, e_=e_: e.matmul(pl[0:64, :], wg[:, e_ * 64:(e_ + 1) * 64], z_[:],
                                                             start=True, stop=True),
                 reads=[wg.b, z_.b], writes=[pl.b])
            P.op("act", lambda e, pl=pl, e_=e_, tA=tA: e.activation(tA[:], pl[0:64, :], AF.Exp, bias=nbg[:, e_:e_ + 1], scale=-1.0),
                 reads=[pl.b, nbg.b], writes=[tA.b])
            P.op("act", lambda e, tA=tA, tB=tB: e.activation(tB[:], tA[:], AF.Ln, bias=one_t[0:64, :]),
                 reads=[tA.b, one_t.b], writes=[tB.b])
            if e_ == 0:
                P.op("dve", lambda e, cols=cols, tB=tB: e.tensor_tensor_scan(cs[:, cols], rst[:], tB[:], 0.0, ALU.mult, ALU.add),
                     reads=[rst.b, tB.b], writes=[cs.b])
            else:
                P.op("dve", lambda e, cols=cols, tB=tB: e.tensor_tensor_scan(cs[:, cols][:, ::-1], rst[:], tB[:, ::-1], 0.0,
                                                                      ALU.mult, ALU.add),
                     reads=[rst.b, tB.b], writes=[cs.b])
        P.op("act", lambda e, mid=mid: e.activation(dmid[:], cs[:, mid::128], AF.Exp, scale=-1.0 / 16),
             reads=[cs.b], writes=[dmid.b])
        P.op("act", lambda e, last=last: e.activation(dlast[:], cs[:, last::128], AF.Exp, scale=-1.0 / 16),
             reads=[cs.b], writes=[dlast.b])
        P.op("dve", lambda e, mid=mid: e.tensor_copy(csm[:], csv[:, :, mid:mid + 1]), reads=[cs.b], writes=[csm.b])
        P.op("dve", lambda e: e.tensor_tensor(csv, csv, csm[:].to_broadcast([64, n, 128]), ALU.subtract),
             reads=[cs.b, csm.b], writes=[cs.b])
        P.op("act", lambda e, last=last: e.activation(dkv[:], cs[:, last::128], AF.Exp, scale=-1.0 / 16),
             reads=[cs.b], writes=[dkv.b])
        for b in range(L // BW):
            sl = slice(b * BW, (b + 1) * BW)
            sq_, sk_ = stgq[b % 2], stgk[b % 2]
            ep, em = eps_[b % 2], ems_[b % 2]
            P.op("sp", lambda e, sq_=sq_, sl=sl: e.dma_start(out=sq_[:], in_=qT_d[:, sl]), reads=[qT_d.b], writes=[sq_.b],
                 dma_sem="d_" + sq_.b.name)
            P.op("sp", lambda e, sk_=sk_, sl=sl: e.dma_start(out=sk_[:], in_=kT_d[:, sl]), reads=[kT_d.b], writes=[sk_.b],
                 dma_sem="d_" + sk_.b.name)
            P.op("act", lambda e, sl=sl, ep=ep: e.activation(ep[:], cs[:, sl], AF.Exp, scale=-1.0 / 16), reads=[cs.b], writes=[ep.b])
            P.op("act", lambda e, sl=sl, em=em: e.activation(em[:], cs[:, sl], AF.Exp, scale=1.0 / 16), reads=[cs.b], writes=[em.b])
            P.op("dve", lambda e, sl=sl, sq_=sq_, ep=ep: e.scalar_tensor_tensor(qt[:, sl], sq_[:], 0.125, ep[:], ALU.mult, ALU.mult),
                 reads=[sq_.b, ep.b], writes=[qt.b])
            P.op("pool", lambda e, sl=sl, sk_=sk_, em=em: e.tensor_tensor(kt[:, sl], sk_[:], em[:], ALU.mult),
                 reads=[sk_.b, em.b], writes=[kt.b])
        for g in range(n // 4):
            pt = pT_[g % 2]
            for i in range(4):
                c = g * 4 + i
                P.op("pe", lambda e, pt=pt, i=i, c=c: e.transpose(pt[:, i, :], kt[:, c * 128:(c + 1) * 128], ident[:]),
                     reads=[kt.b, ident.b], writes=[pt.b])
            P.op("act", lambda e, pt=pt, g=g: e.activation(ktok[:, g * 4:(g + 1) * 4, :], pt[:], AF.Copy),
                 reads=[pt.b], writes=[ktok.b])
        P.op("dve", lambda e: e.memset(S[:], 0.0), writes=[S.b])
        P.op("dve", lambda e: e.memset(Sts[0][:], 0.0), writes=[Sts[0].b])
        order = list(range(n)) if e_ == 0 else list(range(n - 1, -1, -1))
        mslot = 0 if e_ == 0 else 1

        def emit_A(k):
            c = order[k]
            pa = pA[k % 2]
            P.op("pe", lambda e, pa=pa, c=c: e.matmul(pa[:, 0:128], kt[:, c * 128:(c + 1) * 128], qt[:, c * 128:(c + 1) * 128],
                                                      start=True, stop=True), reads=[kt.b, qt.b], writes=[pa.b])
            P.op("dve", lambda e, pa=pa, k=k, mslot=mslot: e.tensor_tensor(Asb[k % 2][:], pa[:, 0:128], cst[:, mslot, :], ALU.mult),
                 reads=[pa.b, cst.b], writes=[Asb[k % 2].b])

        emit_A(0)
        for k, c in enumerate(order):
            if k + 1 < n:
                emit_A(k + 1)
            pk = pKV[k % 2]
            P.op("pe", lambda e, pk=pk, c=c: e.matmul(pk[:], ktok[:, c, :], vb[:, c, :], start=True, stop=True),
                 reads=[ktok.b, vb.b], writes=[pk.b])
            po = pO[(c // 4) % 2]
            osl = slice((c % 4) * 128, (c % 4 + 1) * 128)
            St = Sts[k % 2]
            P.op("pe", lambda e, po=po, osl=osl, c=c, k=k: e.matmul(po[:, osl], vb[:, c, :], Asb[k % 2][:], start=True, stop=False),
                 reads=[vb.b, Asb[k % 2].b], writes=[po.b])
            P.op("pe", lambda e, po=po, osl=osl, c=c, St=St: e.matmul(po[:, osl], St[:], qt[:, c * 128:(c + 1) * 128],
                                                                      start=False, stop=True),
                 reads=[St.b, qt.b], writes=[po.b])
            if k + 1 < n:
                cn = order[k + 1]
                Sn = Sts[(k + 1) % 2]
                P.op("dve", lambda e, pk=pk, c=c: e.tensor_scalar(tkv[:], pk[:], dkv[:, c:c + 1], None, ALU.mult),
                     reads=[pk.b, dkv.b], writes=[tkv.b])
                P.op("dve", lambda e, c=c: e.scalar_tensor_tensor(S[:], S[:], dlast[:, c:c + 1], tkv[:], ALU.mult, ALU.add),
                     reads=[S.b, dlast.b, tkv.b], writes=[S.b])
                P.op("dve", lambda e, Sn=Sn, cn=cn: e.tensor_scalar(Sn[:], S[:], dmid[:, cn:cn + 1], None, ALU.mult),
                     reads=[S.b, dmid.b], writes=[Sn.b])
            done = (c % 4 == 3) if e_ == 0 else (c % 4 == 0)
            if done:
                g0 = (c // 4) * 512
                if e_ == 0:
                    P.op("act", lambda e, po=po, g0=g0: e.activation(acc[:, g0:g0 + 512], po[:], AF.Copy),
                         reads=[po.b], writes=[acc.b])
                else:
                    P.op("dve", lambda e, po=po, g0=g0: e.tensor_tensor(acc[:, g0:g0 + 512], acc[:, g0:g0 + 512], po[:], ALU.add),
                         reads=[po.b, acc.b], writes=[acc.b])
    if debug:
        dbg_acc = P.dram("dbg_acc", [64, L], F32, "ExternalOutput")
        dbg_cs = P.dram("dbg_cs", [64, L], F32, "ExternalOutput")
        dbg_d = P.dram("dbg_d", [64, 3, n], F32, "ExternalOutput")
        dbg_kt = P.dram("dbg_kt", [128, n, 64], F32, "ExternalOutput")
        ktf = P.sbuf("ktf", [128, n, 64], F32)
        P.op("dve", lambda e: e.tensor_copy(ktf[:], ktok[:]), reads=[ktok.b], writes=[ktf.b])
        P.op("sp", lambda e: e.dma_start(out=dbg_kt.h, in_=ktf[:]), reads=[ktf.b], writes=[dbg_kt.b], dma_sem="o_dbg")
        P.op("sp", lambda e: e.dma_start(out=dbg_acc[:, :], in_=acc[:]), reads=[acc.b], writes=[dbg_acc.b], dma_sem="o_dbg")
        P.op("sp", lambda e: e.dma_start(out=dbg_cs[:, :], in_=cs[:]), reads=[cs.b], writes=[dbg_cs.b], dma_sem="o_dbg")
        for i_, t_ in enumerate((dmid, dlast, dkv)):
            P.op("sp", lambda e, i_=i_, t_=t_: e.dma_start(out=dbg_d[:, i_, :], in_=t_[:]), reads=[t_.b], writes=[dbg_d.b], dma_sem="o_dbg")
    for blk in range(L // 512):
        cols = slice(blk * 512, (blk + 1) * 512)
        g_ = gst[blk % 2]
        yo = yos[blk % 2]
        tA, tB = tAs[blk % 2], tBs[blk % 2]
        P.op("sp", lambda e, g_=g_, cols=cols: e.dma_start(out=g_[:], in_=gT_d[:, cols]), reads=[gT_d.b], writes=[g_.b],
             dma_sem="d_" + g_.b.name)
        P.op("act", lambda e, cols=cols, tA=tA: e.activation(tA[:], acc[:, cols], AF.Square), reads=[acc.b], writes=[tA.b])
        pm_ = pA[blk % 2]
        P.op("pe", lambda e, pm_=pm_, tA=tA: e.matmul(pm_[0:64, :], ones64[:], tA[:], start=True, stop=True),
             reads=[ones64.b, tA.b], writes=[pm_.b])
        P.op("act", lambda e, pm_=pm_: e.activation(rs[:], pm_[0:64, :], AF.Sqrt, bias=eps_t[0:64, :]),
             reads=[pm_.b, eps_t.b], writes=[rs.b])
        P.op("dve", lambda e: e.reciprocal(rs[:], rs[:]), reads=[rs.b], writes=[rs.b])
        P.op("act", lambda e, g_=g_, tB=tB: e.activation(tB[:], g_[:], AF.Silu), reads=[g_.b], writes=[tB.b])
        P.op("dve", lambda e, yo=yo, cols=cols: e.scalar_tensor_tensor(yo[:], acc[:, cols], prm[:, 2:3], rs[:], ALU.mult, ALU.mult),
             reads=[acc.b, prm.b, rs.b], writes=[yo.b])
        P.op("dve", lambda e, yo=yo, tB=tB: e.tensor_tensor(yo[:], yo[:], tB[:], ALU.mult), reads=[yo.b, tB.b], writes=[yo.b])
        P.op("sp", lambda e, yo=yo, cols=cols: e.dma_start(out=y_d[:, cols], in_=yo[:]), reads=[yo.b], writes=[y_d.b],
             dma_sem="o_" + yo.b.name)


_GLA_CST = None


def gla_consts():
    global _GLA_CST
    if _GLA_CST is None:
        i = np.arange(128)
        maskf = (i[:, None] <= i[None, :]).astype(np.float32)
        maskb = (i[:, None] >= i[None, :]).astype(np.float32)
        cst = np.ascontiguousarray(np.stack([maskf, maskb, np.eye(128, dtype=np.float32)], axis=1))
        rst = np.ones((64, 512), np.float32)
        rst[:, ::128] = 0.0
        _GLA_CST = (cst, rst)
    return _GLA_CST


def prep_gla(pT, b_, j, lp, L):
    n = L // 128
    cst, rst = gla_consts()
    hs = slice(64 * j, 64 * j + 64)
    v = pT[512 + 64 * j:512 + 64 * j + 64]
    vtok = np.ascontiguousarray(v.T.reshape(n, 128, 64).transpose(1, 0, 2))
    wg = np.concatenate([lp["gla_w_gate"][0][:, hs], lp["gla_w_gate"][1][:, hs]], axis=1)
    prm = np.stack([lp["gla_b_gate"][0, hs], lp["gla_b_gate"][1, hs], lp["gla_norm"][hs]], axis=1)
    return {"qT": np.ascontiguousarray(pT[64 * j:64 * j + 64]), "kT": np.ascontiguousarray(pT[256 + 64 * j:256 + 64 * j + 64]),
            "gT": np.ascontiguousarray(pT[768 + 64 * j:768 + 64 * j + 64]), "vtok": vtok,
            "z0T": np.ascontiguousarray(pT[1024:1040]), "z1T": np.ascontiguousarray(pT[1040:1056]),
            "wg": np.ascontiguousarray(wg, dtype=np.float32), "prm": np.ascontiguousarray(prm, dtype=np.float32),
            "cst": cst, "resetm": rst}


def mixer_specs(L):
    n = L // 128
    _, NT = _dil_tiles(L)
    v = [64, L]
    return {
        "gla": {"qT": v, "kT": v, "gT": v, "vtok": [128, n, 64], "z0T": [16, L], "z1T": [16, L], "wg": [16, 128],
                "prm": [64, 3], "cst": [128, 3, 128], "resetm": [64, 512]},
        "na": {"qT": v, "kT": v, "vaug": [128, n, 65], "biasg": [128, 5, 640], "maskc": [128, 5, 640]},
        "lru": {"xpad": [64, L + 3], "gate": v, "cw": [64, 4], "wax": [64, 256], "prm": [64, 7]},
        "dil": {"qT": v, "qsT": v, "kT": v, "ksT": v, "cos2": v, "sin2s": v, "vaug": [128, NT, 65], "maskab": [128, 256]},
    }


EMITTERS = {"gla": emit_gla, "na": emit_na, "lru": emit_lru, "dil": emit_dil}
PREPS = {"gla": prep_gla, "na": prep_na, "lru": prep_lru, "dil": prep_dil}


def build_mixers(L, which=("gla", "na", "lru", "dil")):
    nc = bass.Bass("TRN2", target_bir_lowering=False)
    P = Prog(nc)
    specs = mixer_specs(L)
    ds = {}
    for m in which:
        ds[m] = {k: P.dram(f"{m}_{k}", shp, F32, "ExternalInput") for k, shp in specs[m].items()}
        ds[m]["y"] = P.dram(f"{m}_y", [64, L], F32, "ExternalOutput")
    mark = P.sb_off
    for i, m in enumerate(which):
        if i:
            P.phase_reset(mark)
        EMITTERS[m](P, L, ds[m])
    P.finish()
    P.emit()
    return nc


def prep_mixers(pT, b_, j, lp, L, which=("gla", "na", "lru", "dil")):
    out = {}
    for m in which:
        for k, v in PREPS[m](pT, b_, j, lp, L).items():
            out[f"{m}_{k}"] = v
    return out


_NC_CACHE = {}


def _get_nc(key, builder):
    return builder()


def _launch(nc, in_maps):
    return run_bass_kernel_spmd(nc, in_maps, core_ids=list(range(NCORES))).results


def kernel(x, mix_norm_pre, mix_norm_post, w_in, gla_w_gate, gla_b_gate, gla_norm, na_rpb,
           lru_conv_w, lru_conv_b, lru_w_a, lru_b_a, lru_w_x, lru_b_x, lru_lambda, w_out,
           ffn_norm_pre, ffn_norm_post, ffn_w_in, ffn_w_out):
    f32 = lambda a: np.ascontiguousarray(np.asarray(a), dtype=np.float32)
    x = f32(x)
    prm = dict(mix_norm_pre=f32(mix_norm_pre), mix_norm_post=f32(mix_norm_post), w_in=f32(w_in),
               gla_w_gate=f32(gla_w_gate), gla_b_gate=f32(gla_b_gate), gla_norm=f32(gla_norm), na_rpb=f32(na_rpb),
               lru_conv_w=f32(lru_conv_w), lru_conv_b=f32(lru_conv_b), lru_w_a=f32(lru_w_a), lru_b_a=f32(lru_b_a),
               lru_w_x=f32(lru_w_x), lru_b_x=f32(lru_b_x), lru_lambda=f32(lru_lambda), w_out=f32(w_out),
               ffn_norm_pre=f32(ffn_norm_pre), ffn_norm_post=f32(ffn_norm_post), ffn_w_in=f32(ffn_w_in),
               ffn_w_out=f32(ffn_w_out))
    L = SEQ
    xf = x.reshape(BATCH * SEQ, D_MODEL)
    xT = [np.ascontiguousarray(xf[c * TPC:(c + 1) * TPC].T) for c in range(NCORES)]
    yT = None
    for l in range(DEPTH + 1):
        has_front = l > 0
        has_back = l < DEPTH
        in_maps = []
        for c in range(NCORES):
            m = {"xT": xT[c]}
            if has_front:
                lf = l - 1
                m.update({"yT": yT[c], "w_out": prm["w_out"][lf], "ffn_w_in": prm["ffn_w_in"][lf],
                          "ffn_w_out": prm["ffn_w_out"][lf],
                          "g_front": np.ascontiguousarray(np.stack([gvec(prm["mix_norm_post"][lf]), gvec(prm["ffn_norm_pre"][lf]),
                                                                    gvec(prm["ffn_norm_post"][lf])], axis=1))})
            if has_back:
                m.update({"w_in": prm["w_in"][l], "g_back": gvec(prm["mix_norm_pre"][l])})
            in_maps.append(m)
        nc = _get_nc(("dense", has_front, has_back), lambda: build_dense(has_front, has_back))
        res = _launch(nc, in_maps)
        if has_front:
            xT = [res[c]["xoT"] for c in range(NCORES)]
        if not has_back:
            break
        pT = [np.ascontiguousarray(np.concatenate([res[b_ * 4 + i]["pT"] for i in range(4)], axis=1)) for b_ in range(BATCH)]
        lp = {k: v[l] for k, v in prm.items()}
        nc = _get_nc(("mix",), lambda: build_mixers(L))
        mres = _launch(nc, [prep_mixers(pT[c // 4], c // 4, c % 4, lp, L) for c in range(NCORES)])
        yT = []
        for c in range(NCORES):
            b_, i = c // 4, c % 4
            rows = [mres[b_ * 4 + j][f"{m}_y"][:, i * TPC:(i + 1) * TPC] for m in ("gla", "na", "lru", "dil") for j in range(4)]
            yT.append(np.ascontiguousarray(np.concatenate(rows, axis=0)))
    out = np.concatenate([xT[c].T for c in range(NCORES)], axis=0).reshape(BATCH, SEQ, D_MODEL)
    return np.ascontiguousarray(out, dtype=np.float32)
```

```python
import contextlib
import numpy as np
import concourse.bass as bass
import concourse.mybir as mybir
from concourse.bass_utils import run_bass_kernel_spmd

F32 = mybir.dt.float32
BF16 = mybir.dt.bfloat16
AF = mybir.ActivationFunctionType
ALU = mybir.AluOpType

D_MODEL = 1024
SEQ = 8192
BATCH = 2
DEPTH = 4
D_IN = 3104
D_FF = 2816
EPS = 1e-6
NCORES = 8
TPC = 2048
SBUF_BASE = 16512
SBUF_LIMIT = 226000

ENGS = ("sp", "act", "pe", "dve", "pool")


class Buf:
    __slots__ = ("name", "last_w", "w_dma", "readers")

    def __init__(self, name):
        self.name = name
        self.last_w = None
        self.w_dma = False
        self.readers = {}


class Ten:
    def __init__(self, h, b):
        self.h = h
        self.b = b

    def __getitem__(self, idx):
        return self.h[idx]


class Prog:
    def __init__(self, nc):
        self.nc = nc
        self.stack = contextlib.ExitStack()
        self.ops = {e: [] for e in ENGS}
        self.waited = {e: {} for e in ENGS}
        self.sems = {}
        self.count = {}
        self.dma_sems = set()
        self.sb_off = SBUF_BASE
        self.ps_bank = 0
        self.psall = None
        self.phase = 0

    def sem(self, name):
        if name not in self.sems:
            self.sems[name] = self.stack.enter_context(self.nc.semaphore(name))
            self.count[name] = 0
        return self.sems[name]

    def sbuf(self, name, shape, dtype):
        nbytes = int(np.prod(shape[1:])) * (4 if dtype == F32 else 2)
        off = (self.sb_off + 63) // 64 * 64
        uname = f"p{self.phase}_{name}"
        h = self.nc.alloc_sbuf_tensor_at(uname, list(shape), dtype, offset=off)
        self.sb_off = off + nbytes
        assert self.sb_off <= SBUF_LIMIT, (uname, self.sb_off)
        return Ten(h, Buf(uname))

    def psum(self, name, shape, dtype=F32):
        if self.psall is None:
            self.psall = self.nc.alloc_psum_tensor("psall", [128, 4096], F32)
        e32 = int(np.prod(shape[1:])) * (4 if dtype == F32 else 2) // 4
        nb = (e32 + 511) // 512
        st = self.ps_bank * 512
        self.ps_bank += nb
        assert self.ps_bank <= 8, name
        v = self.psall[0:shape[0], st:st + e32]
        if dtype != F32:
            v = v.bitcast(dtype)
        if len(shape) == 3:
            v = v.rearrange("p (a b) -> p a b", b=shape[2])
        return Ten(v, Buf(f"p{self.phase}_{name}"))

    def barrier(self):
        for eng in ENGS:
            waits = []
            for sname, cnt in sorted(self.count.items()):
                if cnt == 0 or sname == "c_" + eng:
                    continue
                if self.waited[eng].get(sname, 0) >= cnt:
                    continue
                waits.append((sname, cnt))
                self.waited[eng][sname] = cnt
            if waits:
                self.ops[eng].append((waits, None, None, 0, True))

    def phase_reset(self, sb_mark):
        self.barrier()
        self.sb_off = sb_mark
        self.ps_bank = 0
        self.phase += 1

    def dram(self, name, shape, dtype, kind):
        h = self.nc.dram_tensor(name, list(shape), dtype, kind=kind).ap()
        return Ten(h, Buf(name))

    def op(self, eng, fn, reads=(), writes=(), dma_sem=None):
        waits = {}
        own = "c_" + eng

        def need(tok, is_dma):
            if tok is None:
                return
            s, v = tok
            if eng == "pe" and s == own:
                return
            if s in self.dma_sems:
                v = max(v, self.count[s])
            if self.waited[eng].get(s, 0) >= v:
                return
            if waits.get(s, 0) < v:
                waits[s] = v

        for b in reads:
            need(b.last_w, b.w_dma)
        for b in writes:
            grouped = (dma_sem is not None and b.w_dma and not b.readers
                       and b.last_w is not None and b.last_w[0] == dma_sem)
            if not grouped:
                need(b.last_w, b.w_dma)
            for s, v in b.readers.items():
                need((s, v), False)
        if dma_sem is not None:
            self.sem(dma_sem)
            self.dma_sems.add(dma_sem)
            sname, inc = dma_sem, 16
        else:
            self.sem(own)
            sname, inc = own, 1
        self.count[sname] += inc
        tok = (sname, self.count[sname])
        for s, v in waits.items():
            self.sem(s)
            self.waited[eng][s] = v
        for b in reads:
            if b.readers.get(tok[0], 0) < tok[1]:
                b.readers[tok[0]] = tok[1]
        for b in writes:
            b.last_w = tok
            b.w_dma = dma_sem is not None
            b.readers = {}
        self.ops[eng].append((sorted(waits.items()), fn, sname, inc, dma_sem is not None))
        return tok

    def finish(self):
        waits = [(s, self.count[s]) for s in sorted(self.dma_sems)]
        self.ops["sp"].append((waits, None, None, 0, True))

    def emit(self):
        nc = self.nc
        sems = self.sems

        def replay(e, name):
            for waits, fn, sname, inc, is_dma in self.ops[name]:
                if fn is None:
                    for s, v in waits:
                        e.wait_ge(sems[s], v)
                    continue
                if is_dma or not waits:
                    for s, v in waits:
                        e.wait_ge(sems[s], v)
                    ins = fn(e)
                else:
                    for s, v in waits[:-1]:
                        e.wait_ge(sems[s], v)
                    ins = fn(e)
                    ins._wait_ge(sems[waits[-1][0]], waits[-1][1])
                ins.then_inc(sems[sname], inc)

        with nc.Block() as block:
            @block.sync
            def _(e):
                replay(e, "sp")

            @block.scalar
            def _(e):
                replay(e, "act")

            @block.tensor
            def _(e):
                replay(e, "pe")

            @block.vector
            def _(e):
                replay(e, "dve")

            @block.gpsimd
            def _(e):
                replay(e, "pool")
        self.stack.close()


def build_dense(has_front, has_back):
    nc = bass.Bass("TRN2", target_bir_lowering=False)
    P = Prog(nc)
    T = TPC
    HT = 1024
    NBLK = HT // 512

    xT_d = P.dram("xT", [D_MODEL, T], F32, "ExternalInput")
    xv = xT_d.h.rearrange("(c p) t -> p c t", p=128)
    if has_front:
        yT_d = P.dram("yT", [D_MODEL, T], F32, "ExternalInput")
        yv = yT_d.h.rearrange("(c p) t -> p c t", p=128)
        wo_d = P.dram("w_out", [D_MODEL, D_MODEL], F32, "ExternalInput")
        wov = wo_d.h.rearrange("(c p) n -> p c n", p=128)
        wfi_d = P.dram("ffn_w_in", [D_MODEL, 2 * D_FF], F32, "ExternalInput")
        wfiv = wfi_d.h.rearrange("(c p) n -> p c n", p=128)
        wfo_d = P.dram("ffn_w_out", [D_FF, D_MODEL], F32, "ExternalInput")
        wfov = wfo_d.h.rearrange("(c p) n -> p c n", p=128)
        g_d = P.dram("g_front", [128, 3, 8], F32, "ExternalInput")
    if has_back:
        win_d = P.dram("w_in", [D_MODEL, D_IN], F32, "ExternalInput")
        winv = win_d.h.rearrange("(c p) n -> p c n", p=128)
        gb_d = P.dram("g_back", [128, 8], F32, "ExternalInput")
        pT_d = P.dram("pT", [D_IN, T], F32, "ExternalOutput")
    if has_front:
        xo_d = P.dram("xoT", [D_MODEL, T], F32, "ExternalOutput")
        xov = xo_d.h.rearrange("(c p) t -> p c t", p=128)

    x = P.sbuf("x", [128, 8, HT], F32)
    hb = P.sbuf("hb", [128, 8, HT], BF16)
    sq = P.sbuf("sq", [128, 8, 512], BF16)
    rstd = P.sbuf("rstd", [128, 512], F32)
    ones = P.sbuf("ones", [128, 128], BF16)
    epst = P.sbuf("epst", [128, 1], F32)
    slabs = [P.sbuf(f"slab{i}", [128, 8, 512], BF16) for i in range(4)]
    if has_front:
        z = P.sbuf("z", [128, 8, HT], F32)
        act = P.sbuf("act", [128, 22, HT], BF16)
        wfo_s = [P.sbuf(f"wfo{i}", [128, 22, 256], BF16) for i in range(2)]
        sil = [P.sbuf(f"sil{i}", [128, 512], F32) for i in range(2)]
        gf = P.sbuf("gf", [128, 3, 8], F32)
    if has_back:
        gbk = P.sbuf("gbk", [128, 8], F32)
        stage = [P.sbuf(f"stage{i}", [128, 512], F32) for i in range(4)]
    pss = [P.psum(f"ps{i}", [128, 512], F32) for i in range(6)]
    psn = [P.psum(f"psn{i}", [128, 512], F32) for i in range(2)]
    st = {"ps": 0, "psn": 0, "slab": 0, "wfo": 0, "sil": 0, "stage": 0, "ev": 0, "wstg": 0}

    def rr(key, lst):
        i = st[key]
        st[key] = (i + 1) % len(lst)
        return lst[i]

    P.op("dve", lambda e: e.memset(ones[:], 1.0 / D_MODEL), writes=[ones.b])
    P.op("dve", lambda e: e.memset(epst[:], EPS), writes=[epst.b])
    if has_front:
        P.op("sp", lambda e: e.dma_start(out=gf[:], in_=g_d[:, :, :]), writes=[gf.b], reads=[g_d.b], dma_sem="d_gf")
    if has_back:
        P.op("sp", lambda e: e.dma_start(out=gbk[:], in_=gb_d[:, :]), writes=[gbk.b], reads=[gb_d.b], dma_sem="d_gbk")

    def load_slab(view, c0, ncols, src_b):
        s = rr("slab", slabs)
        P.op("pool", lambda e: e.dma_start(out=s[:, :, 0:ncols], in_=view[:, :, c0:c0 + ncols]),
             reads=[src_b], writes=[s.b], dma_sem="d_" + s.b.name)
        return s

    def evac(dst_ap, dst_b, ps):
        st["ev"] ^= 1
        if st["ev"]:
            P.op("act", lambda e: e.activation(dst_ap, ps[:], AF.Copy), reads=[ps.b], writes=[dst_b])
        else:
            P.op("dve", lambda e: e.tensor_copy(dst_ap, ps[:]), reads=[ps.b], writes=[dst_b])

    def rms_rstd(src, blk):
        tsl = slice(blk * 512, (blk + 1) * 512)
        P.op("act", lambda e: e.activation(sq[:], src[:, :, tsl], AF.Square), reads=[src.b], writes=[sq.b])
        ps = rr("psn", psn)
        for ci in range(8):
            P.op("pe", lambda e, ci=ci: e.matmul(ps[:], ones[:], sq[:, ci, :], start=(ci == 0), stop=(ci == 7)),
                 reads=[ones.b, sq.b], writes=[ps.b])
        P.op("act", lambda e: e.activation(rstd[:], ps[:], AF.Sqrt, bias=epst[:], scale=1.0),
             reads=[ps.b, epst.b], writes=[rstd.b])
        P.op("dve", lambda e: e.reciprocal(rstd[:], rstd[:]), reads=[rstd.b], writes=[rstd.b])

    def pre_norm(g_ap_fn, g_b, blk):
        tsl = slice(blk * 512, (blk + 1) * 512)
        rms_rstd(x, blk)
        for ci in range(8):
            P.op("dve", lambda e, ci=ci: e.scalar_tensor_tensor(hb[:, ci, tsl], x[:, ci, tsl], g_ap_fn(ci), rstd[:],
                                                                ALU.mult, ALU.mult),
                 reads=[x.b, rstd.b, g_b], writes=[hb.b])

    def post_norm_add(g_ap_fn, g_b, blk):
        tsl = slice(blk * 512, (blk + 1) * 512)
        rms_rstd(z, blk)
        for ci in range(8):
            P.op("dve", lambda e, ci=ci: e.scalar_tensor_tensor(z[:, ci, tsl], z[:, ci, tsl], g_ap_fn(ci), rstd[:],
                                                                ALU.mult, ALU.mult),
                 reads=[z.b, rstd.b, g_b], writes=[z.b])
        P.op("dve", lambda e: e.tensor_tensor(x[:, :, tsl], x[:, :, tsl], z[:, :, tsl], ALU.add),
             reads=[x.b, z.b], writes=[x.b])

    for half in range(2):
        t0 = half * HT
        P.op("sp", lambda e, t0=t0: e.dma_start(out=x[:], in_=xv[:, :, t0:t0 + HT]),
             reads=[xT_d.b], writes=[x.b], dma_sem="d_x")
        if has_front:
            P.op("pool", lambda e, t0=t0: e.dma_start(out=act[:, 0:8, :], in_=yv[:, :, t0:t0 + HT]),
                 reads=[yT_d.b], writes=[act.b], dma_sem="d_act")
            for sl in range(2):
                s = load_slab(wov, sl * 512, 512, wo_d.b)
                for ccl in range(4):
                    cc = sl * 4 + ccl
                    for blk in range(NBLK):
                        tsl = slice(blk * 512, (blk + 1) * 512)
                        ps = rr("ps", pss)
                        for ci in range(8):
                            P.op("pe", lambda e, ci=ci, s=s, ccl=ccl, tsl=tsl, ps=ps: e.matmul(
                                ps[:], s[:, ci, ccl * 128:(ccl + 1) * 128], act[:, ci, tsl],
                                start=(ci == 0), stop=(ci == 7)), reads=[s.b, act.b], writes=[ps.b])
                        evac(z[:, cc, tsl], z.b, ps)
            for blk in range(NBLK):
                post_norm_add(lambda ci: gf[:, 0, ci:ci + 1], gf.b, blk)
            for blk in range(NBLK):
                pre_norm(lambda ci: gf[:, 1, ci:ci + 1], gf.b, blk)
            for sl in range(6):
                ncf = min(512, D_FF - sl * 512)
                sg = load_slab(wfiv, sl * 512, ncf, wfi_d.b)
                su = load_slab(wfiv, D_FF + sl * 512, ncf, wfi_d.b)
                for fcl in range(ncf // 128):
                    fc = sl * 4 + fcl
                    for blk in range(NBLK):
                        tsl = slice(blk * 512, (blk + 1) * 512)
                        pg = rr("ps", pss)
                        for ci in range(8):
                            P.op("pe", lambda e, ci=ci, sg=sg, fcl=fcl, tsl=tsl, pg=pg: e.matmul(
                                pg[:], sg[:, ci, fcl * 128:(fcl + 1) * 128], hb[:, ci, tsl],
                                start=(ci == 0), stop=(ci == 7)), reads=[sg.b, hb.b], writes=[pg.b])
                        pu = rr("ps", pss)
                        for ci in range(8):
                            P.op("pe", lambda e, ci=ci, su=su, fcl=fcl, tsl=tsl, pu=pu: e.matmul(
                                pu[:], su[:, ci, fcl * 128:(fcl + 1) * 128], hb[:, ci, tsl],
                                start=(ci == 0), stop=(ci == 7)), reads=[su.b, hb.b], writes=[pu.b])
                        sb = rr("sil", sil)
                        P.op("act", lambda e, sb=sb, pg=pg: e.activation(sb[:], pg[:], AF.Silu),
                             reads=[pg.b], writes=[sb.b])
                        P.op("dve", lambda e, sb=sb, pu=pu, fc=fc, tsl=tsl: e.tensor_tensor(
                            act[:, fc, tsl], sb[:], pu[:], ALU.mult), reads=[sb.b, pu.b], writes=[act.b])
            for cc2 in range(4):
                w = rr("wfo", wfo_s)
                P.op("pool", lambda e, w=w, cc2=cc2: e.dma_start(out=w[:], in_=wfov[:, :, cc2 * 256:(cc2 + 1) * 256]),
                     reads=[wfo_d.b], writes=[w.b], dma_sem="d_" + w.b.name)
                for ccl in range(2):
                    cc = cc2 * 2 + ccl
                    for blk in range(NBLK):
                        tsl = slice(blk * 512, (blk + 1) * 512)
                        ps = rr("ps", pss)
                        for fc in range(22):
                            P.op("pe", lambda e, fc=fc, w=w, tsl=tsl, ps=ps, ccl=ccl: e.matmul(
                                ps[:], w[:, fc, ccl * 128:(ccl + 1) * 128], act[:, fc, tsl], start=(fc == 0), stop=(fc == 21)),
                                reads=[w.b, act.b], writes=[ps.b])
                        evac(z[:, cc, tsl], z.b, ps)
            for blk in range(NBLK):
                post_norm_add(lambda ci: gf[:, 2, ci:ci + 1], gf.b, blk)
        if has_back:
            for blk in range(NBLK):
                pre_norm(lambda ci: gbk[:, ci:ci + 1], gbk.b, blk)
            for sl in range(7):
                ncols = 512 if sl < 6 else 32
                s = load_slab(winv, sl * 512, ncols, win_d.b)
                for ccl in range((ncols + 127) // 128):
                    m = min(128, ncols - ccl * 128)
                    c0 = sl * 512 + ccl * 128
                    for blk in range(NBLK):
                        tsl = slice(blk * 512, (blk + 1) * 512)
                        ps = rr("ps", pss)
                        for ci in range(8):
                            P.op("pe", lambda e, ci=ci, s=s, ccl=ccl, m=m, tsl=tsl, ps=ps: e.matmul(
                                ps[0:m, :], s[:, ci, ccl * 128:ccl * 128 + m], hb[:, ci, tsl],
                                start=(ci == 0), stop=(ci == 7)), reads=[s.b, hb.b], writes=[ps.b])
                        sg_ = rr("stage", stage)
                        st["ev"] ^= 1
                        if st["ev"]:
                            P.op("act", lambda e, sg_=sg_, ps=ps, m=m: e.activation(sg_[0:m, :], ps[0:m, :], AF.Copy),
                                 reads=[ps.b], writes=[sg_.b])
                        else:
                            P.op("dve", lambda e, sg_=sg_, ps=ps, m=m: e.tensor_copy(sg_[0:m, :], ps[0:m, :]),
                                 reads=[ps.b], writes=[sg_.b])
                        P.op("sp", lambda e, sg_=sg_, m=m, c0=c0, t0=t0, blk=blk: e.dma_start(
                            out=pT_d[c0:c0 + m, t0 + blk * 512:t0 + (blk + 1) * 512], in_=sg_[0:m, :]),
                            reads=[sg_.b], writes=[pT_d.b], dma_sem="o_" + sg_.b.name)
        if has_front:
            P.op("sp", lambda e, t0=t0: e.dma_start(out=xov[:, :, t0:t0 + HT], in_=x[:]),
                 reads=[x.b], writes=[xo_d.b], dma_sem="o_x")
    P.finish()
    P.emit()
    return nc


_DENSE_CACHE = {}


def run_dense(has_front, has_back, in_maps):
    key = (has_front, has_back)
    nc = build_dense(has_front, has_back)
    res = run_bass_kernel_spmd(nc, in_maps, core_ids=list(range(NCORES)))
    return res.results


def gvec(g):
    return np.ascontiguousarray(g.reshape(8, 128).T)


def _consts(P, npart=64):
    one_t = P.sbuf("one_t", [128, 1], F32)
    eps_t = P.sbuf("eps_t", [128, 1], F32)
    P.op("dve", lambda e: e.memset(one_t[:], 1.0), writes=[one_t.b])
    P.op("dve", lambda e: e.memset(eps_t[:], EPS), writes=[eps_t.b])
    return one_t, eps_t


def emit_lru(P, L, d):
    TB = 1024
    NB = L // TB
    xp_d, gate_d, cw_d, wax_d, prm_d, y_d = d["xpad"], d["gate"], d["cw"], d["wax"], d["prm"], d["y"]

    one_t, eps_t = _consts(P)
    xp = P.sbuf("xp", [64, L + 3], F32)
    xc = P.sbuf("xc", [64, L], F32)
    xcb = P.sbuf("xcb", [64, L], BF16)
    hf = P.sbuf("hf", [64, L], F32)
    cw = P.sbuf("cw_s", [64, 4], F32)
    wax = P.sbuf("wax_s", [64, 256], F32)
    waxb = P.sbuf("waxb", [64, 256], BF16)
    prm = P.sbuf("prm_s", [64, 7], F32)
    sp_ = P.sbuf("sp_s", [64, 2], F32)
    s8 = P.sbuf("s8", [64, 2], F32)
    s16 = P.sbuf("s16", [64, 2], F32)
    carry = P.sbuf("carry", [64, 1], F32)
    r_s = [P.sbuf(f"r{i}", [64, TB], F32) for i in range(2)]
    i_s = [P.sbuf(f"i{i}", [64, TB], F32) for i in range(2)]
    a_s = [P.sbuf(f"a{i}", [64, TB], F32) for i in range(2)]
    a2s = [P.sbuf(f"a2{i}", [64, TB], F32) for i in range(2)]
    u_s = [P.sbuf(f"u{i}", [64, TB], F32) for i in range(2)]
    hbk = P.sbuf("hbk", [64, TB], F32)
    gts = [P.sbuf(f"gt{i}", [64, TB], F32) for i in range(2)]
    gls = [P.sbuf(f"gl{i}", [64, TB], F32) for i in range(2)]
    yb = [P.sbuf(f"yb{i}", [64, TB], F32) for i in range(2)]
    pss = [P.psum(f"ps{i}", [64, 512], F32) for i in range(4)]
    st = {"ps": 0}

    def nps():
        i = st["ps"]
        st["ps"] = (i + 1) % 4
        return pss[i]

    for t, d in ((xp, xp_d), (cw, cw_d), (wax, wax_d), (prm, prm_d)):
        P.op("sp", lambda e, t=t, d=d: e.dma_start(out=t[:], in_=d[:, :]), reads=[d.b], writes=[t.b],
             dma_sem="d_" + t.b.name)
    P.op("dve", lambda e: e.tensor_copy(waxb[:], wax[:]), reads=[wax.b], writes=[waxb.b])
    P.op("act", lambda e: e.activation(sp_[:], prm[:, 5:7], AF.Exp, scale=-1.0), reads=[prm.b], writes=[sp_.b])
    P.op("act", lambda e: e.activation(sp_[:], sp_[:], AF.Ln, bias=one_t[0:64, :]), reads=[sp_.b, one_t.b], writes=[sp_.b])
    P.op("dve", lambda e: e.tensor_scalar(s8[:], sp_[:], -8.0, None, ALU.mult), reads=[sp_.b], writes=[s8.b])
    P.op("dve", lambda e: e.tensor_scalar(s16[:], sp_[:], -16.0, None, ALU.mult), reads=[sp_.b], writes=[s16.b])
    for b in range(NB):
        sl = slice(b * TB, (b + 1) * TB)
        P.op("act", lambda e, sl=sl: e.activation(xc[:, sl], xp[:, sl], AF.Identity, bias=prm[:, 0:1], scale=cw[:, 0:1]),
             reads=[xp.b, prm.b, cw.b], writes=[xc.b])
        for j in range(1, 4):
            P.op("dve", lambda e, sl=sl, j=j, b=b: e.scalar_tensor_tensor(
                xc[:, sl], xp[:, b * TB + j:(b + 1) * TB + j], cw[:, j:j + 1], xc[:, sl], ALU.mult, ALU.add),
                reads=[xp.b, cw.b, xc.b], writes=[xc.b])
        P.op("pool", lambda e, sl=sl: e.tensor_copy(xcb[:, sl], xc[:, sl]), reads=[xc.b], writes=[xcb.b])

    for e_ in range(2):
        order = range(NB) if e_ == 0 else range(NB - 1, -1, -1)
        first = True
        for b in order:
            sl = slice(b * TB, (b + 1) * TB)
            r_, i_, a_, a2, u_, gl = r_s[b % 2], i_s[b % 2], a_s[b % 2], a2s[b % 2], u_s[b % 2], gls[b % 2]
            if e_ == 1:
                gt = gts[b % 2]
                P.op("sp", lambda e, gt=gt, sl=sl, r_=r_, i_=i_, a_=a_, a2=a2, u_=u_, gl=gl: e.dma_start(out=gt[:], in_=gate_d[:, sl]),
                     reads=[gate_d.b], writes=[gt.b], dma_sem="d_" + gt.b.name)
            for sb in range(TB // 512):
                c0 = b * TB + sb * 512
                pr = nps()
                P.op("pe", lambda e, pr=pr, c0=c0, e_=e_, r_=r_, i_=i_, a_=a_, a2=a2, u_=u_, gl=gl: e.matmul(pr[:], waxb[:, e_ * 64:(e_ + 1) * 64], xcb[:, c0:c0 + 512],
                                                            start=True, stop=True),
                     reads=[waxb.b, xcb.b], writes=[pr.b])
                pi = nps()
                P.op("pe", lambda e, pi=pi, c0=c0, e_=e_, r_=r_, i_=i_, a_=a_, a2=a2, u_=u_, gl=gl: e.matmul(pi[:], waxb[:, 128 + e_ * 64:128 + (e_ + 1) * 64],
                                                            xcb[:, c0:c0 + 512], start=True, stop=True),
                     reads=[waxb.b, xcb.b], writes=[pi.b])
                P.op("act", lambda e, pr=pr, sb=sb, e_=e_, r_=r_, i_=i_, a_=a_, a2=a2, u_=u_, gl=gl: e.activation(r_[:, sb * 512:(sb + 1) * 512], pr[:], AF.Sigmoid,
                                                                 bias=prm[:, 1 + e_:2 + e_]),
                     reads=[pr.b, prm.b], writes=[r_.b])
                P.op("act", lambda e, pi=pi, sb=sb, e_=e_, r_=r_, i_=i_, a_=a_, a2=a2, u_=u_, gl=gl: e.activation(i_[:, sb * 512:(sb + 1) * 512], pi[:], AF.Sigmoid,
                                                                 bias=prm[:, 3 + e_:4 + e_]),
                     reads=[pi.b, prm.b], writes=[i_.b])
            P.op("act", lambda e, e_=e_, r_=r_, i_=i_, a_=a_, a2=a2, u_=u_, gl=gl: e.activation(a_[:], r_[:], AF.Exp, scale=s8[:, e_:e_ + 1]),
                 reads=[r_.b, s8.b], writes=[a_.b])
            P.op("act", lambda e, e_=e_, r_=r_, i_=i_, a_=a_, a2=a2, u_=u_, gl=gl: e.activation(a2[:], r_[:], AF.Exp, scale=s16[:, e_:e_ + 1]),
                 reads=[r_.b, s16.b], writes=[a2.b])
            P.op("act", lambda e, r_=r_, i_=i_, a_=a_, a2=a2, u_=u_, gl=gl: e.activation(a2[:], a2[:], AF.Sqrt, bias=one_t[0:64, :], scale=-1.0),
                 reads=[a2.b, one_t.b], writes=[a2.b])
            P.op("dve", lambda e, sl=sl, r_=r_, i_=i_, a_=a_, a2=a2, u_=u_, gl=gl: e.tensor_tensor(u_[:], i_[:], xc[:, sl], ALU.mult),
                 reads=[i_.b, xc.b], writes=[u_.b])
            P.op("dve", lambda e, r_=r_, i_=i_, a_=a_, a2=a2, u_=u_, gl=gl: e.tensor_tensor(u_[:], u_[:], a2[:], ALU.mult), reads=[u_.b, a2.b], writes=[u_.b])
            if e_ == 0:
                init = 0.0 if first else hf[:, b * TB - 1:b * TB]
                P.op("dve", lambda e, sl=sl, init=init, r_=r_, i_=i_, a_=a_, a2=a2, u_=u_, gl=gl: e.tensor_tensor_scan(hf[:, sl], a_[:], u_[:], init, ALU.mult, ALU.add),
                     reads=[a_.b, u_.b, hf.b], writes=[hf.b])
            else:
                init = 0.0 if first else carry[:]
                P.op("dve", lambda e, init=init, r_=r_, i_=i_, a_=a_, a2=a2, u_=u_, gl=gl: e.tensor_tensor_scan(hbk[:, ::-1], a_[:, ::-1], u_[:, ::-1], init,
                                                                      ALU.mult, ALU.add),
                     reads=[a_.b, u_.b, carry.b], writes=[hbk.b])
                P.op("dve", lambda e, r_=r_, i_=i_, a_=a_, a2=a2, u_=u_, gl=gl: e.tensor_copy(carry[:], hbk[:, 0:1]), reads=[hbk.b], writes=[carry.b])
                P.op("act", lambda e, gt=gt, r_=r_, i_=i_, a_=a_, a2=a2, u_=u_, gl=gl: e.activation(gl[:], gt[:], AF.Gelu_apprx_tanh), reads=[gt.b], writes=[gl.b])
                yo = yb[b % 2]
                P.op("dve", lambda e, sl=sl, yo=yo, r_=r_, i_=i_, a_=a_, a2=a2, u_=u_, gl=gl: e.tensor_tensor(yo[:], hf[:, sl], hbk[:], ALU.add),
                     reads=[hf.b, hbk.b], writes=[yo.b])
                P.op("dve", lambda e, yo=yo, r_=r_, i_=i_, a_=a_, a2=a2, u_=u_, gl=gl: e.tensor_tensor(yo[:], yo[:], gl[:], ALU.mult), reads=[yo.b, gl.b], writes=[yo.b])
                P.op("sp", lambda e, sl=sl, yo=yo, r_=r_, i_=i_, a_=a_, a2=a2, u_=u_, gl=gl: e.dma_start(out=y_d[:, sl], in_=yo[:]), reads=[yo.b], writes=[y_d.b],
                     dma_sem="o_" + yo.b.name)
            first = False


def prep_lru(pT, b_, j, lp, L):
    c0 = 1824 + 64 * j
    xpad = np.zeros((64, L + 3), np.float32)
    xpad[:, 2:2 + L] = pT[c0:c0 + 64]
    g0 = 2080 + 64 * j
    hs = slice(64 * j, 64 * j + 64)
    wax = np.concatenate([lp["lru_w_a"][0, j], lp["lru_w_a"][1, j], lp["lru_w_x"][0, j], lp["lru_w_x"][1, j]], axis=1)
    prm = np.stack([lp["lru_conv_b"][hs], lp["lru_b_a"][0, hs], lp["lru_b_a"][1, hs], lp["lru_b_x"][0, hs],
                    lp["lru_b_x"][1, hs], lp["lru_lambda"][0, hs], lp["lru_lambda"][1, hs]], axis=1)
    return {"xpad": xpad, "gate": np.ascontiguousarray(pT[g0:g0 + 64]),
            "cw": np.ascontiguousarray(lp["lru_conv_w"][:, hs].T), "wax": np.ascontiguousarray(wax, dtype=np.float32),
            "prm": np.ascontiguousarray(prm, dtype=np.float32)}


def _attn_finalize(P, acc, sel, y_d, L, pds, tag):
    rds = [P.sbuf(f"rd{tag}{i}", [64, 512], F32) for i in range(2)]
    yos = [P.sbuf(f"yo{tag}{i}", [64, 512], F32) for i in range(2)]
    for blk in range(L // 512):
        cols = slice(blk * 512, (blk + 1) * 512)
        pd = pds[blk % 2]
        rd = rds[blk % 2]
        yo = yos[blk % 2]
        P.op("pe", lambda e, pd=pd, cols=cols: e.matmul(pd[:], sel[:], acc[:, cols], start=True, stop=True),
             reads=[sel.b, acc.b], writes=[pd.b])
        P.op("dve", lambda e, pd=pd, rd=rd: e.reciprocal(rd[:], pd[:]), reads=[pd.b], writes=[rd.b])
        P.op("dve", lambda e, rd=rd, yo=yo, cols=cols: e.tensor_tensor(yo[:], acc[0:64, cols], rd[:], ALU.mult),
             reads=[acc.b, rd.b], writes=[yo.b])
        P.op("sp", lambda e, yo=yo, cols=cols: e.dma_start(out=y_d[:, cols], in_=yo[:]), reads=[yo.b], writes=[y_d.b],
             dma_sem="o_" + yo.b.name)


def _make_sel(P):
    sel = P.sbuf("sel", [65, 64], F32)
    P.op("dve", lambda e: e.memset(sel[:], 0.0), writes=[sel.b])
    P.op("dve", lambda e: e.memset(sel[64:65, :], 1.0), writes=[sel.b])
    return sel


def _load_cast(P, dst, src_d, L, stg, nrows=64, blkw=2048, eng="pool"):
    for b in range(L // blkw):
        s = stg[b % len(stg)]
        sl = slice(b * blkw, (b + 1) * blkw)
        P.op("sp", lambda e, s=s, sl=sl: e.dma_start(out=s[0:nrows, 0:blkw], in_=src_d[:, sl]), reads=[src_d.b], writes=[s.b],
             dma_sem="d_" + s.b.name)
        if b % 2 == 0:
            P.op("act", lambda e, s=s, sl=sl: e.activation(dst[:, sl], s[0:nrows, 0:blkw], AF.Copy), reads=[s.b], writes=[dst.b])
        else:
            P.op("dve", lambda e, s=s, sl=sl: e.tensor_copy(dst[:, sl], s[0:nrows, 0:blkw]), reads=[s.b], writes=[dst.b])


def emit_na(P, L, d):
    ntile = L // 128
    qT_d, kT_d, va_d, bias_d, mask_d, y_d = d["qT"], d["kT"], d["vaug"], d["biasg"], d["maskc"], d["y"]

    qb = P.sbuf("qb", [64, L], BF16)
    kb = P.sbuf("kb", [64, L], BF16)
    stg = [P.sbuf(f"stg{i}", [64, 2048], F32) for i in range(2)]
    vb = P.sbuf("vb", [128, ntile, 65], BF16)
    vst = [P.sbuf(f"vst{i}", [128, 16, 65], F32) for i in range(2)]
    Fm = P.sbuf("Fm", [128, 5, 640], BF16)
    bst = P.sbuf("bst", [128, 640], F32)
    mst = P.sbuf("mst", [128, 640], F32)
    acc = P.sbuf("acc", [65, L], F32)
    sel = _make_sel(P)
    pexp = [P.sbuf(f"pexp{i}", [128, 640], BF16) for i in range(2)]
    pms = [P.sbuf(f"pm{i}", [128, 640], BF16) for i in range(2)]
    pSs = [P.psum(f"pS{i}", [128, 1024], F32) for i in range(2)]
    pos = [P.psum(f"po{i}", [65, 512], F32) for i in range(2)]
    pds = [P.psum(f"pd{i}", [64, 512], F32) for i in range(2)]

    _load_cast(P, qb, qT_d, L, stg)
    _load_cast(P, kb, kT_d, L, stg)
    for g in range((ntile + 15) // 16):
        s = vst[g % 2]
        n = min(16, ntile - g * 16)
        P.op("sp", lambda e, s=s, g=g, n=n: e.dma_start(out=s[:, 0:n, :], in_=va_d[:, g * 16:g * 16 + n, :]),
             reads=[va_d.b], writes=[s.b], dma_sem="d_" + s.b.name)
        P.op("act", lambda e, s=s, g=g, n=n: e.activation(vb[:, g * 16:g * 16 + n, :], s[:, 0:n, :], AF.Copy),
             reads=[s.b], writes=[vb.b])
    for fi in range(5):
        P.op("sp", lambda e, fi=fi: e.dma_start(out=bst[:], in_=bias_d[:, fi, :]), reads=[bias_d.b], writes=[bst.b], dma_sem="d_bst")
        P.op("sp", lambda e, fi=fi: e.dma_start(out=mst[:], in_=mask_d[:, fi, :]), reads=[mask_d.b], writes=[mst.b], dma_sem="d_mst")
        P.op("act", lambda e: e.activation(bst[:], bst[:], AF.Exp), reads=[bst.b], writes=[bst.b])
        P.op("dve", lambda e, fi=fi: e.tensor_tensor(Fm[:, fi, :], bst[:], mst[:], ALU.mult), reads=[bst.b, mst.b], writes=[Fm.b])

    def na_stage1(m):
        tb = min(max(m - 2, 0), ntile - 5)
        fi = 0 if m == 0 else 1 if m == 1 else 3 if m == ntile - 2 else 4 if m == ntile - 1 else 2
        pS = pSs[m % 2]
        for jt in range(5):
            P.op("pe", lambda e, pS=pS, jt=jt, tb=tb, m=m: e.matmul(
                pS[:, jt * 128:(jt + 1) * 128], kb[:, (tb + jt) * 128:(tb + jt + 1) * 128], qb[:, m * 128:(m + 1) * 128],
                start=True, stop=True), reads=[kb.b, qb.b], writes=[pS.b])
        pe_ = pexp[m % 2]
        P.op("act", lambda e, pe_=pe_, pS=pS: e.activation(pe_[:], pS[:, 0:640], AF.Exp, scale=0.125),
             reads=[pS.b], writes=[pe_.b])
        pm_ = pms[m % 2]
        P.op("dve", lambda e, pe_=pe_, pm_=pm_, fi=fi: e.tensor_tensor(pm_[:], pe_[:], Fm[:, fi, :], ALU.mult),
             reads=[pe_.b, Fm.b], writes=[pm_.b])
        return pm_, tb

    def na_stage2(m, pm_, tb):
        po = pos[(m // 4) % 2]
        for jt in range(5):
            P.op("pe", lambda e, po=po, jt=jt, tb=tb, m=m, pm_=pm_: e.matmul(
                po[:, (m % 4) * 128:(m % 4 + 1) * 128], vb[:, tb + jt, :], pm_[:, jt * 128:(jt + 1) * 128],
                start=(jt == 0), stop=(jt == 4)), reads=[vb.b, pm_.b], writes=[po.b])
        if m % 4 == 3:
            P.op("act", lambda e, po=po, m=m: e.activation(acc[:, (m - 3) * 128:(m + 1) * 128], po[:], AF.Copy),
                 reads=[po.b], writes=[acc.b])

    cur = na_stage1(0)
    for m in range(ntile):
        nxt = na_stage1(m + 1) if m + 1 < ntile else None
        na_stage2(m, *cur)
        cur = nxt
    _attn_finalize(P, acc, sel, y_d, L, pds, "n")


def na_tables(L):
    rows = L // 64
    ntile = L // 128
    ms = [0, 1, 2, ntile - 2, ntile - 1]
    dr_i = np.zeros((5, 128, 5, 128), np.int64)
    dc_i = np.zeros((5, 128, 5, 128), np.int64)
    mask = np.zeros((5, 128, 5, 128), np.float32)
    pk = np.arange(128)
    fq = np.arange(128)
    for vi, m in enumerate(ms):
        tb = min(max(m - 2, 0), ntile - 5)
        for jt in range(5):
            krow = 2 * (tb + jt) + pk // 64
            kc = pk % 64
            qrow = 2 * m + fq // 64
            qc = fq % 64
            rs = np.clip(qrow - 4, 0, rows - 8)
            row_ok = (krow[:, None] >= rs[None, :]) & (krow[:, None] < rs[None, :] + 8)
            cs = np.clip(qc - 8, 0, 48)
            col_ok = (kc[:, None] >= cs[None, :]) & (kc[:, None] < cs[None, :] + 16)
            dr = np.clip(krow[:, None] - qrow[None, :], -7, 7)
            dc = np.clip(kc[:, None] - qc[None, :], -15, 15)
            dr_i[vi, :, jt, :] = dr + 7
            dc_i[vi, :, jt, :] = dc + 15
            mask[vi, :, jt, :] = (row_ok & col_ok).astype(np.float32)
    return dr_i, dc_i, mask


_NA_TAB = {}


def prep_na(pT, b_, j, lp, L):
    if L not in _NA_TAB:
        _NA_TAB[L] = na_tables(L)
    dr_i, dc_i, mask = _NA_TAB[L]
    ntile = L // 128
    q0, k0, v0 = 1056 + 64 * j, 1312 + 64 * j, 1568 + 64 * j
    v = pT[v0:v0 + 64]
    vaug = np.ones((128, ntile, 65), np.float32)
    vaug[:, :, 0:64] = v.T.reshape(ntile, 128, 64).transpose(1, 0, 2)
    rpb = lp["na_rpb"][j]
    biasg = rpb[dr_i, dc_i]
    biasg = np.ascontiguousarray(biasg.transpose(1, 0, 2, 3).reshape(128, 5, 640), dtype=np.float32)
    maskc = np.ascontiguousarray(mask.transpose(1, 0, 2, 3).reshape(128, 5, 640))
    return {"qT": np.ascontiguousarray(pT[q0:q0 + 64]), "kT": np.ascontiguousarray(pT[k0:k0 + 64]),
            "vaug": vaug, "biasg": biasg, "maskc": maskc}


DIL_D = (1, 4, 16)
DIL_PAD = 1024


def _dil_tiles(L):
    idx = {}
    t = 0
    for d in DIL_D:
        n = L // d
        for r in range(d):
            for m in range(n // 128 + 1):
                idx[(d, r, m)] = t
                t += 1
    return idx, t


def emit_dil(P, L, d):
    tidx, NT = _dil_tiles(L)
    names = ("qT", "qsT", "kT", "ksT", "cos2", "sin2s")
    dd = {nm: d[nm] for nm in names}
    va_d, mk_d, y_d = d["vaug"], d["maskab"], d["y"]

    qh = P.sbuf("qh", [64, L], BF16)
    kh = P.sbuf("kh", [64, L + 2 * DIL_PAD], BF16)
    BW = 1024
    stg = {nm: P.sbuf("s_" + nm, [64, BW], F32) for nm in names}
    t1 = P.sbuf("t1", [64, BW], F32)
    t2 = P.sbuf("t2", [64, BW], F32)
    vb = P.sbuf("vb", [128, NT, 65], BF16)
    vst = [P.sbuf(f"vst{i}", [128, 16, 65], F32) for i in range(2)]
    mk = P.sbuf("mk", [128, 256], F32)
    acc = P.sbuf("acc", [65, L], F32)
    sel = _make_sel(P)
    pexp = [P.sbuf(f"pexp{i}", [128, 256], BF16) for i in range(2)]
    pms = [P.sbuf(f"pm{i}", [128, 256], BF16) for i in range(3)]
    pSs = [P.psum(f"pS{i}", [128, 512], F32) for i in range(3)]
    pos = [P.psum(f"po{i}", [65, 512], F32) for i in range(2)]
    pds = [P.psum(f"pd{i}", [64, 512], F32) for i in range(2)]

    P.op("sp", lambda e: e.dma_start(out=mk[:], in_=mk_d[:, :]), reads=[mk_d.b], writes=[mk.b], dma_sem="d_mk")
    P.op("dve", lambda e: e.memset(kh[:, 0:DIL_PAD], 0.0), writes=[kh.b])
    P.op("dve", lambda e: e.memset(kh[:, DIL_PAD + L:DIL_PAD + L + DIL_PAD], 0.0), writes=[kh.b])
    for g in range((NT + 15) // 16):
        s = vst[g % 2]
        n = min(16, NT - g * 16)
        P.op("sp", lambda e, s=s, g=g, n=n: e.dma_start(out=s[:, 0:n, :], in_=va_d[:, g * 16:g * 16 + n, :]),
             reads=[va_d.b], writes=[s.b], dma_sem="d_" + s.b.name)
        P.op("act", lambda e, s=s, g=g, n=n: e.activation(vb[:, g * 16:g * 16 + n, :], s[:, 0:n, :], AF.Copy),
             reads=[s.b], writes=[vb.b])
    for b in range(L // BW):
        sl = slice(b * BW, (b + 1) * BW)
        for nm in names:
            P.op("sp", lambda e, nm=nm, sl=sl: e.dma_start(out=stg[nm][:], in_=dd[nm][:, sl]), reads=[dd[nm].b],
                 writes=[stg[nm].b], dma_sem="d_s_" + nm)
        for (a, s_, dst, off) in (("qT", "qsT", qh, 0), ("kT", "ksT", kh, DIL_PAD)):
            P.op("dve", lambda e, a=a: e.tensor_tensor(t1[:], stg[a][:], stg["cos2"][:], ALU.mult),
                 reads=[stg[a].b, stg["cos2"].b], writes=[t1.b])
            P.op("pool", lambda e, s_=s_: e.tensor_tensor(t2[:], stg[s_][:], stg["sin2s"][:], ALU.mult),
                 reads=[stg[s_].b, stg["sin2s"].b], writes=[t2.b])
            P.op("dve", lambda e, dst=dst, off=off, b=b: e.tensor_tensor(
                dst[:, off + b * BW:off + (b + 1) * BW], t1[:], t2[:], ALU.add), reads=[t1.b, t2.b], writes=[dst.b])

    tiles = []
    for d in DIL_D:
        nq = (L // d) // 128
        for r in range(d):
            for m in range(nq + 1):
                tiles.append((d, r, m, nq))
    cnt = {"po": 0}
    state = {"po": None}

    def dil_stage1(i):
        d, r, m, nq = tiles[i]
        c_lo = 128 if m == 0 else 0
        c_hi = 128 if m == nq else 256
        i0_ = 128 * (m - 1) + c_lo
        cntq = c_hi - c_lo
        ks = DIL_PAD + r + d * (128 * m - 64)
        qs = r + d * i0_
        pS = pSs[i % 3]
        pe_ = pexp[i % 2]
        pm_ = pms[i % 3]
        P.op("pe", lambda e, pS=pS, ks=ks, qs=qs, d=d, cntq=cntq, c_lo=c_lo, c_hi=c_hi: e.matmul(
            pS[:, c_lo:c_hi], kh[:, ks:ks + 127 * d + 1:d], qh[:, qs:qs + (cntq - 1) * d + 1:d], start=True, stop=True),
            reads=[kh.b, qh.b], writes=[pS.b])
        P.op("act", lambda e, pe_=pe_, pS=pS, c_lo=c_lo, c_hi=c_hi: e.activation(
            pe_[:, c_lo:c_hi], pS[:, c_lo:c_hi], AF.Exp, scale=0.125), reads=[pS.b], writes=[pe_.b])
        P.op("dve", lambda e, pe_=pe_, pm_=pm_, c_lo=c_lo, c_hi=c_hi: e.tensor_tensor(
            pm_[:, c_lo:c_hi], pe_[:, c_lo:c_hi], mk[:, c_lo:c_hi], ALU.mult), reads=[pe_.b, mk.b], writes=[pm_.b])

    def dil_stage2(i):
        d, r, m, nq = tiles[i]
        if m == 0:
            return
        prev = pms[(i - 1) % 3]
        pm_ = pms[i % 3]
        mq = m - 1
        if mq % 4 == 0:
            state["po"] = pos[cnt["po"] % 2]
            cnt["po"] += 1
        po = state["po"]
        osl = slice((mq % 4) * 128, (mq % 4 + 1) * 128)
        P.op("pe", lambda e, po=po, osl=osl, prev=prev, tA=tidx[(d, r, mq)]: e.matmul(
            po[:, osl], vb[:, tA, :], prev[:, 128:256], start=True, stop=False),
            reads=[vb.b, prev.b], writes=[po.b])
        P.op("pe", lambda e, po=po, osl=osl, pm_=pm_, tB=tidx[(d, r, m)]: e.matmul(
            po[:, osl], vb[:, tB, :], pm_[:, 0:128], start=False, stop=True),
            reads=[vb.b, pm_.b], writes=[po.b])
        if mq % 4 == 3 or mq == nq - 1:
            mq0 = mq - (mq % 4)
            w = (mq % 4 + 1) * 128
            a0 = r + d * 128 * mq0
            if d == 1:
                P.op("act", lambda e, po=po, a0=a0, w=w: e.activation(acc[:, a0:a0 + w], po[:, 0:w], AF.Copy),
                     reads=[po.b], writes=[acc.b])
            else:
                P.op("dve", lambda e, po=po, a0=a0, w=w, d=d: e.tensor_tensor(
                    acc[:, a0:a0 + (w - 1) * d + 1:d], acc[:, a0:a0 + (w - 1) * d + 1:d], po[:, 0:w], ALU.add),
                    reads=[po.b, acc.b], writes=[acc.b])

    dil_stage1(0)
    for i in range(len(tiles)):
        if i + 1 < len(tiles):
            dil_stage1(i + 1)
        dil_stage2(i)
    _attn_finalize(P, acc, sel, y_d, L, pds, "d")


_ROPE = {}


def rope_tables(L):
    if L not in _ROPE:
        pos = np.arange(L, dtype=np.float32)
        inv_freq = (np.float32(10000.0) ** (-np.arange(0, 64, 2, dtype=np.float32) / np.float32(64))).astype(np.float32)
        ang = (pos[:, None] * inv_freq[None, :]).astype(np.float32)
        c = np.cos(ang).astype(np.float32).T
        s = np.sin(ang).astype(np.float32).T
        _ROPE[L] = (np.ascontiguousarray(np.concatenate([c, c], 0)), np.ascontiguousarray(np.concatenate([-s, s], 0)))
    return _ROPE[L]


_DIL_MASK = None


def prep_dil(pT, b_, j, lp, L):
    tidx, NT = _dil_tiles(L)
    q0, k0, v0 = 2336 + 64 * j, 2592 + 64 * j, 2848 + 64 * j
    q = pT[q0:q0 + 64]
    k = pT[k0:k0 + 64]
    v = pT[v0:v0 + 64]
    cos2, sin2s = rope_tables(L)
    vT = np.ascontiguousarray(v.T)
    vaug = np.zeros((128, NT, 65), np.float32)
    pk = np.arange(128)
    for d in DIL_D:
        n = L // d
        for r in range(d):
            for m in range(n // 128 + 1):
                i = 128 * m - 64 + pk
                ok = (i >= 0) & (i < n)
                tok = r + d * i[ok]
                t = tidx[(d, r, m)]
                vaug[ok, t, 0:64] = vT[tok]
                vaug[ok, t, 64] = 1.0
    fq = np.arange(128)
    mb = (pk[:, None] <= fq[None, :]).astype(np.float32)
    ma = (pk[:, None] >= fq[None, :]).astype(np.float32)
    return {"qT": np.ascontiguousarray(q), "qsT": np.ascontiguousarray(np.concatenate([q[32:], q[:32]], 0)),
            "kT": np.ascontiguousarray(k), "ksT": np.ascontiguousarray(np.concatenate([k[32:], k[:32]], 0)),
            "cos2": cos2, "sin2s": sin2s, "vaug": vaug, "maskab": np.ascontiguousarray(np.concatenate([mb, ma], 1))}


def emit_gla(P, L, d, debug=False):
    n = L // 128
    qT_d, kT_d, gT_d, vt_d = d["qT"], d["kT"], d["gT"], d["vtok"]
    z_d = [d["z0T"], d["z1T"]]
    wg_d, prm_d, cst_d, rst_d, y_d = d["wg"], d["prm"], d["cst"], d["resetm"], d["y"]

    one_t, eps_t = _consts(P)
    cs = P.sbuf("cs", [64, L], F32)
    csm = P.sbuf("csm", [64, n, 1], F32)
    qt = P.sbuf("qt", [64, L], BF16)
    kt = P.sbuf("kt", [64, L], BF16)
    acc = P.sbuf("acc", [64, L], F32)
    vb = P.sbuf("vb", [128, n, 64], BF16)
    ktok = P.sbuf("ktok", [128, n, 64], BF16)
    vst = [P.sbuf(f"vst{i}", [128, 16, 64], F32) for i in range(2)]
    BW = 1024
    stgq = [P.sbuf(f"stgq{i}", [64, BW], F32) for i in range(2)]
    stgk = [P.sbuf(f"stgk{i}", [64, BW], F32) for i in range(2)]
    eps_ = [P.sbuf(f"ep{i}", [64, BW], F32) for i in range(2)]
    ems_ = [P.sbuf(f"em{i}", [64, BW], F32) for i in range(2)]
    zs = [P.sbuf(f"zs{i}", [16, 512], F32) for i in range(2)]
    wg = P.sbuf("wg_s", [16, 128], F32)
    prm = P.sbuf("prm_s", [64, 3], F32)
    nbg = P.sbuf("nbg", [64, 2], F32)
    cst = P.sbuf("cst_s", [128, 3, 128], F32)
    ident = P.sbuf("ident", [64, 64], BF16)
    rst = P.sbuf("rst", [64, 512], F32)
    tAs = [P.sbuf(f"tA{i}", [64, 512], F32) for i in range(2)]
    tBs = [P.sbuf(f"tB{i}", [64, 512], F32) for i in range(2)]
    dmid = P.sbuf("dmid", [64, n], F32)
    dlast = P.sbuf("dlast", [64, n], F32)
    dkv = P.sbuf("dkv", [64, n], F32)
    S = P.sbuf("S", [64, 64], F32)
    Sts = [P.sbuf(f"St{i}", [64, 64], BF16) for i in range(2)]
    tkv = P.sbuf("tkv", [64, 64], F32)
    Asb = [P.sbuf(f"Asb{i}", [128, 128], BF16) for i in range(2)]
    ones64 = P.sbuf("ones64", [64, 64], F32)
    gst = [P.sbuf(f"gst{i}", [64, 512], F32) for i in range(2)]
    yos = [P.sbuf(f"yo{i}", [64, 512], F32) for i in range(2)]
    rs = P.sbuf("rs", [64, 512], F32)
    pA = [P.psum(f"pA{i}", [128, 512], F32) for i in range(2)]
    pT_ = [P.psum(f"pT{i}", [128, 4, 64], BF16) for i in range(2)]
    pKV = [P.psum(f"pKV{i}", [64, 64], F32) for i in range(2)]
    pO = [P.psum(f"pO{i}", [64, 512], F32) for i in range(2)]

    for t, d in ((wg, wg_d), (prm, prm_d), (cst, cst_d), (rst, rst_d)):
        P.op("sp", lambda e, t=t, d=d: e.dma_start(out=t[:], in_=d.h), reads=[d.b], writes=[t.b], dma_sem="d_" + t.b.name)
    P.op("dve", lambda e: e.tensor_copy(ident[:], cst[0:64, 2, 0:64]), reads=[cst.b], writes=[ident.b])
    P.op("dve", lambda e: e.tensor_scalar(nbg[:], prm[:, 0:2], -1.0, None, ALU.mult), reads=[prm.b], writes=[nbg.b])
    P.op("dve", lambda e: e.memset(ones64[:], 1.0 / 64.0), writes=[ones64.b])
    for g in range((n + 15) // 16):
        s = vst[g % 2]
        m_ = min(16, n - g * 16)
        P.op("sp", lambda e, s=s, g=g, m_=m_: e.dma_start(out=s[:, 0:m_, :], in_=vt_d[:, g * 16:g * 16 + m_, :]),
             reads=[vt_d.b], writes=[s.b], dma_sem="d_" + s.b.name)
        P.op("act", lambda e, s=s, g=g, m_=m_: e.activation(vb[:, g * 16:g * 16 + m_, :], s[:, 0:m_, :], AF.Copy),
             reads=[s.b], writes=[vb.b])

    csv = cs[:].rearrange("p (n c) -> p n c", c=128)
    for e_ in range(2):
        mid = 63 if e_ == 0 else 64
        last = 127 if e_ == 0 else 0
        for blk in range(L // 512):
            cols = slice(blk * 512, (blk + 1) * 512)
            z_ = zs[blk % 2]
            tA, tB = tAs[blk % 2], tBs[blk % 2]
            P.op("sp", lambda e, z_=z_, cols=cols, e_=e_: e.dma_start(out=z_[:], in_=z_d[e_][:, cols]),
                 reads=[z_d[e_].b], writes=[z_.b], dma_sem="d_" + z_.b.name)
            pl = pA[blk % 2]
            P.op("pe", lambda e, pl=pl, z_=z_, e_=e_: e.matmul(pl[0:64, :], wg[:, e_ * 64:(e_ + 1) * 64], z_[:],
                                                             start=True, stop=True),
                 reads=[wg.b, z_.b], writes=[pl.b])
            P.op("act", lambda e, pl=pl, e_=e_, tA=tA: e.activation(tA[:], pl[0:64, :], AF.Exp, bias=nbg[:, e_:e_ + 1], scale=-1.0),
                 reads=[pl.b, nbg.b], writes=[tA.b])
            P.op("act", lambda e, tA=tA, tB=tB: e.activation(tB[:], tA[:], AF.Ln, bias=one_t[0:64, :]),
                 reads=[tA.b, one_t.b], writes=[tB.b])
            if e_ == 0:
                P.op("dve", lambda e, cols=cols, tB=tB: e.tensor_tensor_scan(cs[:, cols], rst[:], tB[:], 0.0, ALU.mult, ALU.add),
                     reads=[rst.b, tB.b], writes=[cs.b])
            else:
                P.op("dve", lambda e, cols=cols, tB=tB: e.tensor_tensor_scan(cs[:, cols][:, ::-1], rst[:], tB[:, ::-1], 0.0,
                                                                      ALU.mult, ALU.add),
                     reads=[rst.b, tB.b], writes=[cs.b])
        P.op("act", lambda e, mid=mid: e.activation(dmid[:], cs[:, mid::128], AF.Exp, scale=-1.0 / 16),
             reads=[cs.b], writes=[dmid.b])
        P.op("act", lambda e, last=last: e.activation(dlast[:], cs[:, last::128], AF.Exp, scale=-1.0 / 16),
             reads=[cs.b], writes=[dlast.b])
        P.op("dve", lambda e, mid=mid: e.tensor_copy(csm[:], csv[:, :, mid:mid + 1]), reads=[cs.b], writes=[csm.b])
        P.op("dve", lambda e: e.tensor_tensor(csv, csv, csm[:].to_broadcast([64, n, 128]), ALU.subtract),
             reads=[cs.b, csm.b], writes=[cs.b])
        P.op("act", lambda e, last=last: e.activation(dkv[:], cs[:, last::128], AF.Exp, scale=-1.0 / 16),
             reads=[cs.b], writes=[dkv.b])
        for b in range(L // BW):
            sl = slice(b * BW, (b + 1) * BW)
            sq_, sk_ = stgq[b % 2], stgk[b % 2]
            ep, em = eps_[b % 2], ems_[b % 2]
            P.op("sp", lambda e, sq_=sq_, sl=sl: e.dma_start(out=sq_[:], in_=qT_d[:, sl]), reads=[qT_d.b], writes=[sq_.b],
                 dma_sem="d_" + sq_.b.name)
            P.op("sp", lambda e, sk_=sk_, sl=sl: e.dma_start(out=sk_[:], in_=kT_d[:, sl]), reads=[kT_d.b], writes=[sk_.b],
                 dma_sem="d_" + sk_.b.name)
            P.op("act", lambda e, sl=sl, ep=ep: e.activation(ep[:], cs[:, sl], AF.Exp, scale=-1.0 / 16), reads=[cs.b], writes=[ep.b])
            P.op("act", lambda e, sl=sl, em=em: e.activation(em[:], cs[:, sl], AF.Exp, scale=1.0 / 16), reads=[cs.b], writes=[em.b])
            P.op("dve", lambda e, sl=sl, sq_=sq_, ep=ep: e.scalar_tensor_tensor(qt[:, sl], sq_[:], 0.125, ep[:], ALU.mult, ALU.mult),
                 reads=[sq_.b, ep.b], writes=[qt.b])
            P.op("pool", lambda e, sl=sl, sk_=sk_, em=em: e.tensor_tensor(kt[:, sl], sk_[:], em[:], ALU.mult),
                 reads=[sk_.b, em.b], writes=[kt.b])
        for g in range(n // 4):
            pt = pT_[g % 2]
            for i in range(4):
                c = g * 4 + i
                P.op("pe", lambda e, pt=pt, i=i, c=c: e.transpose(pt[:, i, :], kt[:, c * 128:(c + 1) * 128], ident[:]),
                     reads=[kt.b, ident.b], writes=[pt.b])
            P.op("act", lambda e, pt=pt, g=g: e.activation(ktok[:, g * 4:(g + 1) * 4, :], pt[:], AF.Copy),
                 reads=[pt.b], writes=[ktok.b])
        P.op("dve", lambda e: e.memset(S[:], 0.0), writes=[S.b])
        P.op("dve", lambda e: e.memset(Sts[0][:], 0.0), writes=[Sts[0].b])
        order = list(range(n)) if e_ == 0 else list(range(n - 1, -1, -1))
        mslot = 0 if e_ == 0 else 1

        def emit_A(k):
            c = order[k]
            pa = pA[k % 2]
            P.op("pe", lambda e, pa=pa, c=c: e.matmul(pa[:, 0:128], kt[:, c * 128:(c + 1) * 128], qt[:, c * 128:(c + 1) * 128],
                                                      start=True, stop=True), reads=[kt.b, qt.b], writes=[pa.b])
            P.op("dve", lambda e, pa=pa, k=k, mslot=mslot: e.tensor_tensor(Asb[k % 2][:], pa[:, 0:128], cst[:, mslot, :], ALU.mult),
                 reads=[pa.b, cst.b], writes=[Asb[k % 2].b])

        emit_A(0)
        for k, c in enumerate(order):
            if k + 1 < n:
                emit_A(k + 1)
            pk = pKV[k % 2]
            P.op("pe", lambda e, pk=pk, c=c: e.matmul(pk[:], ktok[:, c, :], vb[:, c, :], start=True, stop=True),
                 reads=[ktok.b, vb.b], writes=[pk.b])
            po = pO[(c // 4) % 2]
            osl = slice((c % 4) * 128, (c % 4 + 1) * 128)
            St = Sts[k % 2]
            P.op("pe", lambda e, po=po, osl=osl, c=c, k=k: e.matmul(po[:, osl], vb[:, c, :], Asb[k % 2][:], start=True, stop=False),
                 reads=[vb.b, Asb[k % 2].b], writes=[po.b])
            P.op("pe", lambda e, po=po, osl=osl, c=c, St=St: e.matmul(po[:, osl], St[:], qt[:, c * 128:(c + 1) * 128],
                                                                      start=False, stop=True),
                 reads=[St.b, qt.b], writes=[po.b])
            if k + 1 < n:
                cn = order[k + 1]
                Sn = Sts[(k + 1) % 2]
                P.op("dve", lambda e, pk=pk, c=c: e.tensor_scalar(tkv[:], pk[:], dkv[:, c:c + 1], None, ALU.mult),
                     reads=[pk.b, dkv.b], writes=[tkv.b])
                P.op("dve", lambda e, c=c: e.scalar_tensor_tensor(S[:], S[:], dlast[:, c:c + 1], tkv[:], ALU.mult, ALU.add),
                     reads=[S.b, dlast.b, tkv.b], writes=[S.b])
                P.op("dve", lambda e, Sn=Sn, cn=cn: e.tensor_scalar(Sn[:], S[:], dmid[:, cn:cn + 1], None, ALU.mult),
                     reads=[S.b, dmid.b], writes=[Sn.b])
            done = (c % 4 == 3) if e_ == 0 else (c % 4 == 0)
            if done:
                g0 = (c // 4) * 512
                if e_ == 0:
                    P.op("act", lambda e, po=po, g0=g0: e.activation(acc[:, g0:g0 + 512], po[:], AF.Copy),
                         reads=[po.b], writes=[acc.b])
                else:
                    P.op("dve", lambda e, po=po, g0=g0: e.tensor_tensor(acc[:, g0:g0 + 512], acc[:, g0:g0 + 512], po[:], ALU.add),
                         reads=[po.b, acc.b], writes=[acc.b])
    if debug:
        dbg_acc = P.dram("dbg_acc", [64, L], F32, "ExternalOutput")
        dbg_cs = P.dram("dbg_cs", [64, L], F32, "ExternalOutput")
        dbg_d = P.dram("dbg_d", [64, 3, n], F32, "ExternalOutput")
        dbg_kt = P.dram("dbg_kt", [128, n, 64], F32, "ExternalOutput")
        ktf = P.sbuf("ktf", [128, n, 64], F32)
        P.op("dve", lambda e: e.tensor_copy(ktf[:], ktok[:]), reads=[ktok.b], writes=[ktf.b])
        P.op("sp", lambda e: e.dma_start(out=dbg_kt.h, in_=ktf[:]), reads=[ktf.b], writes=[dbg_kt.b], dma_sem="o_dbg")
        P.op("sp", lambda e: e.dma_start(out=dbg_acc[:, :], in_=acc[:]), reads=[acc.b], writes=[dbg_acc.b], dma_sem="o_dbg")
        P.op("sp", lambda e: e.dma_start(out=dbg_cs[:, :], in_=cs[:]), reads=[cs.b], writes=[dbg_cs.b], dma_sem="o_dbg")
        for i_, t_ in enumerate((dmid, dlast, dkv)):
            P.op("sp", lambda e, i_=i_, t_=t_: e.dma_start(out=dbg_d[:, i_, :], in_=t_[:]), reads=[t_.b], writes=[dbg_d.b], dma_sem="o_dbg")
    for blk in range(L // 512):
        cols = slice(blk * 512, (blk + 1) * 512)
        g_ = gst[blk % 2]
        yo = yos[blk % 2]
        tA, tB = tAs[blk % 2], tBs[blk % 2]
        P.op("sp", lambda e, g_=g_, cols=cols: e.dma_start(out=g_[:], in_=gT_d[:, cols]), reads=[gT_d.b], writes=[g_.b],
             dma_sem="d_" + g_.b.name)
        P.op("act", lambda e, cols=cols, tA=tA: e.activation(tA[:], acc[:, cols], AF.Square), reads=[acc.b], writes=[tA.b])
        pm_ = pA[blk % 2]
        P.op("pe", lambda e, pm_=pm_, tA=tA: e.matmul(pm_[0:64, :], ones64[:], tA[:], start=True, stop=True),
             reads=[ones64.b, tA.b], writes=[pm_.b])
        P.op("act", lambda e, pm_=pm_: e.activation(rs[:], pm_[0:64, :], AF.Sqrt, bias=eps_t[0:64, :]),
             reads=[pm_.b, eps_t.b], writes=[rs.b])
        P.op("dve", lambda e: e.reciprocal(rs[:], rs[:]), reads=[rs.b], writes=[rs.b])
        P.op("act", lambda e, g_=g_, tB=tB: e.activation(tB[:], g_[:], AF.Silu), reads=[g_.b], writes=[tB.b])
        P.op("dve", lambda e, yo=yo, cols=cols: e.scalar_tensor_tensor(yo[:], acc[:, cols], prm[:, 2:3], rs[:], ALU.mult, ALU.mult),
             reads=[acc.b, prm.b, rs.b], writes=[yo.b])
        P.op("dve", lambda e, yo=yo, tB=tB: e.tensor_tensor(yo[:], yo[:], tB[:], ALU.mult), reads=[yo.b, tB.b], writes=[yo.b])
        P.op("sp", lambda e, yo=yo, cols=cols: e.dma_start(out=y_d[:, cols], in_=yo[:]), reads=[yo.b], writes=[y_d.b],
             dma_sem="o_" + yo.b.name)


_GLA_CST = None


def gla_consts():
    global _GLA_CST
    if _GLA_CST is None:
        i = np.arange(128)
        maskf = (i[:, None] <= i[None, :]).astype(np.float32)
        maskb = (i[:, None] >= i[None, :]).astype(np.float32)
        cst = np.ascontiguousarray(np.stack([maskf, maskb, np.eye(128, dtype=np.float32)], axis=1))
        rst = np.ones((64, 512), np.float32)
        rst[:, ::128] = 0.0
        _GLA_CST = (cst, rst)
    return _GLA_CST


def prep_gla(pT, b_, j, lp, L):
    n = L // 128
    cst, rst = gla_consts()
    hs = slice(64 * j, 64 * j + 64)
    v = pT[512 + 64 * j:512 + 64 * j + 64]
    vtok = np.ascontiguousarray(v.T.reshape(n, 128, 64).transpose(1, 0, 2))
    wg = np.concatenate([lp["gla_w_gate"][0][:, hs], lp["gla_w_gate"][1][:, hs]], axis=1)
    prm = np.stack([lp["gla_b_gate"][0, hs], lp["gla_b_gate"][1, hs], lp["gla_norm"][hs]], axis=1)
    return {"qT": np.ascontiguousarray(pT[64 * j:64 * j + 64]), "kT": np.ascontiguousarray(pT[256 + 64 * j:256 + 64 * j + 64]),
            "gT": np.ascontiguousarray(pT[768 + 64 * j:768 + 64 * j + 64]), "vtok": vtok,
            "z0T": np.ascontiguousarray(pT[1024:1040]), "z1T": np.ascontiguousarray(pT[1040:1056]),
            "wg": np.ascontiguousarray(wg, dtype=np.float32), "prm": np.ascontiguousarray(prm, dtype=np.float32),
            "cst": cst, "resetm": rst}


def mixer_specs(L):
    n = L // 128
    _, NT = _dil_tiles(L)
    v = [64, L]
    return {
        "gla": {"qT": v, "kT": v, "gT": v, "vtok": [128, n, 64], "z0T": [16, L], "z1T": [16, L], "wg": [16, 128],
                "prm": [64, 3], "cst": [128, 3, 128], "resetm": [64, 512]},
        "na": {"qT": v, "kT": v, "vaug": [128, n, 65], "biasg": [128, 5, 640], "maskc": [128, 5, 640]},
        "lru": {"xpad": [64, L + 3], "gate": v, "cw": [64, 4], "wax": [64, 256], "prm": [64, 7]},
        "dil": {"qT": v, "qsT": v, "kT": v, "ksT": v, "cos2": v, "sin2s": v, "vaug": [128, NT, 65], "maskab": [128, 256]},
    }


EMITTERS = {"gla": emit_gla, "na": emit_na, "lru": emit_lru, "dil": emit_dil}
PREPS = {"gla": prep_gla, "na": prep_na, "lru": prep_lru, "dil": prep_dil}


def build_mixers(L, which=("gla", "na", "lru", "dil")):
    nc = bass.Bass("TRN2", target_bir_lowering=False)
    P = Prog(nc)
    specs = mixer_specs(L)
    ds = {}
    for m in which:
        ds[m] = {k: P.dram(f"{m}_{k}", shp, F32, "ExternalInput") for k, shp in specs[m].items()}
        ds[m]["y"] = P.dram(f"{m}_y", [64, L], F32, "ExternalOutput")
    mark = P.sb_off
    for i, m in enumerate(which):
        if i:
            P.phase_reset(mark)
        EMITTERS[m](P, L, ds[m])
    P.finish()
    P.emit()
    return nc


def prep_mixers(pT, b_, j, lp, L, which=("gla", "na", "lru", "dil")):
    out = {}
    for m in which:
        for k, v in PREPS[m](pT, b_, j, lp, L).items():
            out[f"{m}_{k}"] = v
    return out


_NC_CACHE = {}


def _get_nc(key, builder):
    return builder()


def _launch(nc, in_maps):
    return run_bass_kernel_spmd(nc, in_maps, core_ids=list(range(NCORES))).results


def kernel(x, mix_norm_pre, mix_norm_post, w_in, gla_w_gate, gla_b_gate, gla_norm, na_rpb,
           lru_conv_w, lru_conv_b, lru_w_a, lru_b_a, lru_w_x, lru_b_x, lru_lambda, w_out,
           ffn_norm_pre, ffn_norm_post, ffn_w_in, ffn_w_out):
    f32 = lambda a: np.ascontiguousarray(np.asarray(a), dtype=np.float32)
    x = f32(x)
    prm = dict(mix_norm_pre=f32(mix_norm_pre), mix_norm_post=f32(mix_norm_post), w_in=f32(w_in),
               gla_w_gate=f32(gla_w_gate), gla_b_gate=f32(gla_b_gate), gla_norm=f32(gla_norm), na_rpb=f32(na_rpb),
               lru_conv_w=f32(lru_conv_w), lru_conv_b=f32(lru_conv_b), lru_w_a=f32(lru_w_a), lru_b_a=f32(lru_b_a),
               lru_w_x=f32(lru_w_x), lru_b_x=f32(lru_b_x), lru_lambda=f32(lru_lambda), w_out=f32(w_out),
               ffn_norm_pre=f32(ffn_norm_pre), ffn_norm_post=f32(ffn_norm_post), ffn_w_in=f32(ffn_w_in),
               ffn_w_out=f32(ffn_w_out))
    L = SEQ
    xf = x.reshape(BATCH * SEQ, D_MODEL)
    xT = [np.ascontiguousarray(xf[c * TPC:(c + 1) * TPC].T) for c in range(NCORES)]
    yT = None
    for l in range(DEPTH + 1):
        has_front = l > 0
        has_back = l < DEPTH
        in_maps = []
        for c in range(NCORES):
            m = {"xT": xT[c]}
            if has_front:
                lf = l - 1
                m.update({"yT": yT[c], "w_out": prm["w_out"][lf], "ffn_w_in": prm["ffn_w_in"][lf],
                          "ffn_w_out": prm["ffn_w_out"][lf],
                          "g_front": np.ascontiguousarray(np.stack([gvec(prm["mix_norm_post"][lf]), gvec(prm["ffn_norm_pre"][lf]),
                                                                    gvec(prm["ffn_norm_post"][lf])], axis=1))})
            if has_back:
                m.update({"w_in": prm["w_in"][l], "g_back": gvec(prm["mix_norm_pre"][l])})
            in_maps.append(m)
        nc = _get_nc(("dense", has_front, has_back), lambda: build_dense(has_front, has_back))
        res = _launch(nc, in_maps)
        if has_front:
            xT = [res[c]["xoT"] for c in range(NCORES)]
        if not has_back:
            break
        pT = [np.ascontiguousarray(np.concatenate([res[b_ * 4 + i]["pT"] for i in range(4)], axis=1)) for b_ in range(BATCH)]
        lp = {k: v[l] for k, v in prm.items()}
        nc = _get_nc(("mix",), lambda: build_mixers(L))
        mres = _launch(nc, [prep_mixers(pT[c // 4], c // 4, c % 4, lp, L) for c in range(NCORES)])
        yT = []
        for c in range(NCORES):
            b_, i = c // 4, c % 4
            rows = [mres[b_ * 4 + j][f"{m}_y"][:, i * TPC:(i + 1) * TPC] for m in ("gla", "na", "lru", "dil") for j in range(4)]
            yT.append(np.ascontiguousarray(np.concatenate(rows, axis=0)))
    out = np.concatenate([xT[c].T for c in range(NCORES)], axis=0).reshape(BATCH, SEQ, D_MODEL)
    return np.ascontiguousarray(out, dtype=np.float32)
```
